# Optimizing a Trainium2 kernel written in Bass

```python
import math
import jax
import jax.numpy as jnp
from jax import lax
import numpy as np

D_MODEL = 1024
BATCH = 16
SEQ = 2048
DEPTH = 4

GRID_W = 64
CTX_LEN = 256
N_MIXERS = 3
N_LAYERS_A = (DEPTH + 2) // N_MIXERS
N_LAYERS_B = (DEPTH + 1) // N_MIXERS
N_LAYERS_C = DEPTH // N_MIXERS
N_MOD = 6
HEAD_DIM = 64
N_HEADS = D_MODEL // (2 * HEAD_DIM)
V_HEAD_DIM = 2 * HEAD_DIM
Q_BLOCK = 128
ROPE_THETA = 10000.0
D_FF = 4 * D_MODEL
HYENA_EMB_DIM = 33
HYENA_FILTER_ORDER = 64
HYENA_SHORT_WIDTH = 3
HYENA_DECAY_TARGET = 1e-2
HYENA_FAST_DECAY_PCT = 0.3
HYENA_SLOW_DECAY_PCT = 1.5
CONV_WIDTH = 31
NORM_EPS = 1e-6
LN_EPS = 1e-5

kernel_name = "hybrid_diffattn_hyena_conformer_dit"


def rms_norm(x, g, eps=NORM_EPS):
    xf = x.astype(jnp.float32)
    y = xf * lax.rsqrt(jnp.mean(xf * xf, axis=-1, keepdims=True) + eps)
    return (y * g.astype(jnp.float32)).astype(x.dtype)


def layer_norm(x, g, b, eps=LN_EPS):
    xf = x.astype(jnp.float32)
    mu = jnp.mean(xf, axis=-1, keepdims=True)
    var = jnp.mean(jnp.square(xf - mu), axis=-1, keepdims=True)
    y = (xf - mu) * lax.rsqrt(var + eps)
    return (y * g.astype(jnp.float32) + b.astype(jnp.float32)).astype(x.dtype)


def modulate(x, shift, scale):
    return x * (1.0 + scale) + shift


def depthwise_conv(x, w, b):
    y = lax.conv_general_dilated(
        x, w[:, None, :].astype(x.dtype), window_strides=(1,), padding="SAME",
        dimension_numbers=("NWC", "WIO", "NWC"), feature_group_count=x.shape[-1])
    return y + b


def grid_positions(n_tokens):
    rows = n_tokens // GRID_W
    row_ids = jnp.repeat(jnp.arange(rows, dtype=jnp.float32), GRID_W)
    col_ids = jnp.tile(jnp.arange(GRID_W, dtype=jnp.float32), rows)
    return row_ids, col_ids


def rope_1d(x, pos):
    half = x.shape[-1] // 2
    inv_freq = ROPE_THETA ** (-jnp.arange(half, dtype=jnp.float32) / half)
    ang = pos[:, None] * inv_freq[None, :]
    cos = jnp.cos(ang)[None, :, None, :]
    sin = jnp.sin(ang)[None, :, None, :]
    x1 = x[..., :half].astype(jnp.float32)
    x2 = x[..., half:].astype(jnp.float32)
    return jnp.concatenate([x1 * cos - x2 * sin, x2 * cos + x1 * sin], axis=-1).astype(x.dtype)


def rope_2d(x, rows, cols):
    rd = x.shape[-1] // 2
    return jnp.concatenate([rope_1d(x[..., :rd], rows), rope_1d(x[..., rd:], cols)], axis=-1)


def diff_attend(q, k, v, lam):
    s = jnp.einsum("bqhpd,bkhpd->bhpqk", q, k).astype(jnp.float32) * (HEAD_DIM ** -0.5)
    p = jax.nn.softmax(s, axis=-1)
    a = p[:, :, 0] - lam * p[:, :, 1]
    return jnp.einsum("bhqk,bkhe->bqhe", a.astype(v.dtype), v)


def differential_attention(u_lat, u_ctx, w_qkv, w_out, lam_vec, subln_g, lam_init, with_ctx):
    bsz, n_lat, _ = u_lat.shape

    def project(u):
        n = u.shape[1]
        q, k, v = jnp.split(u @ w_qkv, 3, axis=-1)
        return (q.reshape(bsz, n, N_HEADS, 2, HEAD_DIM),
                k.reshape(bsz, n, N_HEADS, 2, HEAD_DIM),
                v.reshape(bsz, n, N_HEADS, V_HEAD_DIM))

    q_l, k_l, v_l = project(u_lat)
    q_c, k_c, v_c = project(u_ctx)
    rows, cols = grid_positions(n_lat)
    q_l = rope_2d(q_l.reshape(bsz, n_lat, 2 * N_HEADS, HEAD_DIM), rows, cols).reshape(q_l.shape)
    k_l = rope_2d(k_l.reshape(bsz, n_lat, 2 * N_HEADS, HEAD_DIM), rows, cols).reshape(k_l.shape)
    lf = lam_vec.astype(jnp.float32)
    lam = jnp.exp(jnp.sum(lf[0] * lf[1])) - jnp.exp(jnp.sum(lf[2] * lf[3])) + lam_init
    k_all = jnp.concatenate([k_c, k_l], axis=1)
    v_all = jnp.concatenate([v_c, v_l], axis=1)
    n_blk = n_lat // Q_BLOCK
    q_blocks = jnp.moveaxis(q_l.reshape(bsz, n_blk, Q_BLOCK, N_HEADS, 2, HEAD_DIM), 1, 0)
    o_l = lax.map(lambda qb: diff_attend(qb, k_all, v_all, lam), q_blocks)
    o_l = jnp.moveaxis(o_l, 0, 1).reshape(bsz, n_lat, N_HEADS, V_HEAD_DIM)

    def merge_heads(o):
        o = rms_norm(o, subln_g) * (1.0 - lam_init)
        return o.reshape(bsz, o.shape[1], D_MODEL) @ w_out

    y_l = merge_heads(o_l)
    y_c = merge_heads(diff_attend(q_c, k_c, v_c, lam)) if with_ctx else None
    return y_l, y_c


def hyena_filters(n, w1, b1, w2, b2, freq, w_out):
    bands = (HYENA_EMB_DIM - 1) // 2
    t = jnp.linspace(0.0, 1.0, n, dtype=jnp.float32)[:, None]
    w = 2.0 * math.pi * jnp.arange(n, dtype=jnp.float32)[:, None] / n
    f = jnp.linspace(1e-4, bands - 1, bands, dtype=jnp.float32)[None, :]
    z = jnp.concatenate([t, jnp.cos(f * w), -jnp.sin(f * w)], axis=-1)
    hdn = jnp.sin(freq * (z @ w1 + b1))
    for j in range(w2.shape[0]):
        hdn = jnp.sin(freq * (hdn @ w2[j] + b2[j]))
    h = (hdn @ w_out).reshape(n, 2, D_MODEL)
    deltas = jnp.abs(jnp.linspace(math.log(HYENA_DECAY_TARGET) / HYENA_SLOW_DECAY_PCT,
                                  math.log(HYENA_DECAY_TARGET) / HYENA_FAST_DECAY_PCT,
                                  D_MODEL, dtype=jnp.float32))
    h = h * jnp.exp(-t * deltas[None, :])[:, None, :]
    return h[:, 0], h[:, 1]


def two_sided_fft_conv(v, h_fwd, h_bwd):
    n = v.shape[1]
    kern = jnp.concatenate([h_fwd, jnp.zeros((1, h_fwd.shape[1]), h_fwd.dtype), h_bwd[:0:-1]], axis=0)
    vf = jnp.fft.rfft(v.astype(jnp.float32), n=2 * n, axis=1)
    kf = jnp.fft.rfft(kern.astype(jnp.float32), n=2 * n, axis=0)
    y = jnp.fft.irfft(vf * kf[None], n=2 * n, axis=1)[:, :n]
    return y.astype(v.dtype)


def hyena_mixer(u, w_in, b_in, w_short, b_short, fw1, fb1, fw2, fb2, ffreq, fw_out, d_bias, w_out, b_out):
    z = depthwise_conv(u @ w_in + b_in, w_short, b_short)
    x0, x1, v = jnp.split(z, 3, axis=-1)
    h_fwd, h_bwd = hyena_filters(u.shape[1], fw1, fb1, fw2, fb2, ffreq, fw_out)
    v = v * x1
    v = two_sided_fft_conv(v, h_fwd, h_bwd) + v * d_bias
    return (v * x0) @ w_out + b_out


def conformer_conv(u, w_pw1, b_pw1, w_dw, b_dw, ln_g, ln_b, w_pw2, b_pw2):
    a, g = jnp.split(u @ w_pw1 + b_pw1, 2, axis=-1)
    z = depthwise_conv(a * jax.nn.sigmoid(g), w_dw, b_dw)
    z = jax.nn.silu(layer_norm(z, ln_g, ln_b))
    return z @ w_pw2 + b_pw2


def squared_relu_mlp(u, w_in, w_out):
    return jnp.square(jax.nn.relu(u @ w_in)) @ w_out


def setup_inputs(seed: int = 0) -> dict:
    key = jax.random.key(seed)
    ks = iter(jax.random.split(key, 48))
    D = D_MODEL

    def nrm(shape, scale):
        return scale * jax.random.normal(next(ks), shape, jnp.float32)

    return {
        "x": nrm((BATCH, SEQ, D), 1.0),
        "c": nrm((BATCH, D), 1.0),
        "ctx": nrm((BATCH, CTX_LEN, D), 1.0),
        "c_ctx": nrm((D,), 1.0),
        "w_mod": nrm((DEPTH, D, N_MOD * D), 0.5 * D ** -0.5),
        "b_mod": nrm((DEPTH, N_MOD * D), 0.02),
        "norm_mix_pre": 1.0 + nrm((DEPTH, D), 0.05),
        "norm_mix_post": 1.0 + nrm((DEPTH, D), 0.05),
        "norm_mlp_pre": 1.0 + nrm((DEPTH, D), 0.05),
        "norm_mlp_post": 1.0 + nrm((DEPTH, D), 0.05),
        "w_mlp_in": nrm((DEPTH, D, D_FF), D ** -0.5),
        "w_mlp_out": nrm((DEPTH, D_FF, D), D_FF ** -0.5),
        "attn_w_qkv": nrm((N_LAYERS_A, D, 3 * D), D ** -0.5),
        "attn_w_out": nrm((N_LAYERS_A, D, D), D ** -0.5),
        "attn_lambda": nrm((N_LAYERS_A, 4, HEAD_DIM), 0.1),
        "attn_subln": 1.0 + nrm((N_LAYERS_A, V_HEAD_DIM), 0.05),
        "hy_w_in": nrm((N_LAYERS_B, D, 3 * D), D ** -0.5),
        "hy_b_in": nrm((N_LAYERS_B, 3 * D), 0.02),
        "hy_w_short": nrm((N_LAYERS_B, HYENA_SHORT_WIDTH, 3 * D), HYENA_SHORT_WIDTH ** -0.5),
        "hy_b_short": nrm((N_LAYERS_B, 3 * D), 0.02),
        "hy_filt_w1": nrm((N_LAYERS_B, HYENA_EMB_DIM, HYENA_FILTER_ORDER), HYENA_EMB_DIM ** -0.5),
        "hy_filt_b1": nrm((N_LAYERS_B, HYENA_FILTER_ORDER), 0.1),
        "hy_filt_w2": nrm((N_LAYERS_B, 2, HYENA_FILTER_ORDER, HYENA_FILTER_ORDER), HYENA_FILTER_ORDER ** -0.5),
        "hy_filt_b2": nrm((N_LAYERS_B, 2, HYENA_FILTER_ORDER), 0.1),
        "hy_filt_freq": 1.0 + nrm((N_LAYERS_B, HYENA_FILTER_ORDER), 0.05),
        "hy_filt_w_out": nrm((N_LAYERS_B, HYENA_FILTER_ORDER, 2 * D), HYENA_FILTER_ORDER ** -0.5),
        "hy_bias": nrm((N_LAYERS_B, D), 1.0),
        "hy_w_out": nrm((N_LAYERS_B, D, D), D ** -0.5),
        "hy_b_out": nrm((N_LAYERS_B, D), 0.02),
        "cv_w_pw1": nrm((N_LAYERS_C, D, 2 * D), D ** -0.5),
        "cv_b_pw1": nrm((N_LAYERS_C, 2 * D), 0.02),
        "cv_w_dw": nrm((N_LAYERS_C, CONV_WIDTH, D), CONV_WIDTH ** -0.5),
        "cv_b_dw": nrm((N_LAYERS_C, D), 0.02),
        "cv_ln_g": 1.0 + nrm((N_LAYERS_C, D), 0.05),
        "cv_ln_b": nrm((N_LAYERS_C, D), 0.02),
        "cv_w_pw2": nrm((N_LAYERS_C, D, D), D ** -0.5),
        "cv_b_pw2": nrm((N_LAYERS_C, D), 0.02),
    }


def reference(x, c, ctx, c_ctx, w_mod, b_mod, norm_mix_pre, norm_mix_post, norm_mlp_pre, norm_mlp_post,
              w_mlp_in, w_mlp_out, attn_w_qkv, attn_w_out, attn_lambda, attn_subln,
              hy_w_in, hy_b_in, hy_w_short, hy_b_short, hy_filt_w1, hy_filt_b1, hy_filt_w2, hy_filt_b2,
              hy_filt_freq, hy_filt_w_out, hy_bias, hy_w_out, hy_b_out,
              cv_w_pw1, cv_b_pw1, cv_w_dw, cv_b_dw, cv_ln_g, cv_ln_b, cv_w_pw2, cv_b_pw2):
    h_lat, h_ctx = x, ctx
    silu_c = jax.nn.silu(c)
    silu_cc = jax.nn.silu(c_ctx)
    for i in range(DEPTH):
        last = i == DEPTH - 1
        kind, j = i % N_MIXERS, i // N_MIXERS
        need_ctx_out = not last
        need_ctx_in = need_ctx_out or kind == 0
        mod_l = jnp.split((silu_c @ w_mod[i] + b_mod[i])[:, None, :], N_MOD, axis=-1)
        mod_c = jnp.split(silu_cc @ w_mod[i] + b_mod[i], N_MOD, axis=-1)
        u_l = modulate(rms_norm(h_lat, norm_mix_pre[i]), mod_l[0], mod_l[1])
        u_c = modulate(rms_norm(h_ctx, norm_mix_pre[i]), mod_c[0], mod_c[1]) if need_ctx_in else None
        if kind == 0:
            y_l, y_c = differential_attention(u_l, u_c, attn_w_qkv[j], attn_w_out[j], attn_lambda[j],
                                              attn_subln[j], 0.8 - 0.6 * math.exp(-0.3 * i), need_ctx_out)
        elif kind == 1:
            hp = (hy_w_in[j], hy_b_in[j], hy_w_short[j], hy_b_short[j], hy_filt_w1[j], hy_filt_b1[j],
                  hy_filt_w2[j], hy_filt_b2[j], hy_filt_freq[j], hy_filt_w_out[j], hy_bias[j],
                  hy_w_out[j], hy_b_out[j])
            y_l = hyena_mixer(u_l, *hp)
            y_c = hyena_mixer(u_c, *hp) if need_ctx_out else None
        else:
            cp = (cv_w_pw1[j], cv_b_pw1[j], cv_w_dw[j], cv_b_dw[j], cv_ln_g[j], cv_ln_b[j],
                  cv_w_pw2[j], cv_b_pw2[j])
            y_l = conformer_conv(u_l, *cp)
            y_c = conformer_conv(u_c, *cp) if need_ctx_out else None
        h_lat = h_lat + mod_l[2] * rms_norm(y_l, norm_mix_post[i])
        v_l = modulate(rms_norm(h_lat, norm_mlp_pre[i]), mod_l[3], mod_l[4])
        h_lat = h_lat + mod_l[5] * rms_norm(squared_relu_mlp(v_l, w_mlp_in[i], w_mlp_out[i]), norm_mlp_post[i])
        if need_ctx_out:
            h_ctx = h_ctx + mod_c[2] * rms_norm(y_c, norm_mix_post[i])
            v_c = modulate(rms_norm(h_ctx, norm_mlp_pre[i]), mod_c[3], mod_c[4])
            h_ctx = h_ctx + mod_c[5] * rms_norm(squared_relu_mlp(v_c, w_mlp_in[i], w_mlp_out[i]), norm_mlp_post[i])
    return h_lat
```

```python
import math
from contextlib import ExitStack
import numpy as np
import ml_dtypes
import concourse.bass as bass
import concourse.mybir as mybir
from concourse.bass_utils import run_bass_kernel_spmd

F32 = mybir.dt.float32
BF16 = mybir.dt.bfloat16
AF = mybir.ActivationFunctionType
ALU = mybir.AluOpType
AX = mybir.AxisListType

D = 1024
S_LAT = 2048
S_CTX = 256
TPB = S_LAT + S_CTX
NB = 2
NTOK = NB * TPB
DFF = 4096
DEPTH = 4
EPS = 1e-6
LN_EPS = 1e-5
NCORES = 8


class T:
    __slots__ = ("name", "w", "rs")

    def __init__(self, name=""):
        self.name = name
        self.w = None
        self.rs = []


class Op:
    __slots__ = ("eng", "fn", "deps", "sig", "sigidx", "is_dma", "sem_i", "sem_val")


class Sched:
    ENGS = ("pe", "act", "dve", "pool", "sp")
    CENG = ("pe", "act", "dve", "pool")
    RING = 8

    def __init__(self, nc):
        self.nc = nc
        self.streams = {e: [] for e in self.ENGS}
        self.ndma = {e: 0 for e in self.ENGS}
        self.pending = {e: None for e in self.ENGS}
        self.dmas_since = []
        self.last = {e: None for e in self.CENG}

    def _mk(self, eng, fn, is_dma):
        o = Op()
        o.eng = eng
        o.fn = fn
        o.sig = False
        o.sigidx = 0
        o.is_dma = is_dma
        o.deps = []
        return o

    def _deps(self, op, reads, writes):
        deps = []
        for t in reads:
            if t.w is not None:
                deps.append(t.w)
        for t in writes:
            if t.w is not None:
                deps.append(t.w)
            deps.extend(t.rs)
        for t in reads:
            t.rs.append(op)
        for t in writes:
            t.w = op
            t.rs = []
        if self.pending[op.eng] is not None:
            deps.extend(self.pending[op.eng])
            self.pending[op.eng] = None
        out = []
        seen = set()
        for d in deps:
            if d is op or id(d) in seen:
                continue
            seen.add(id(d))
            if (not d.is_dma) and d.eng == op.eng and (not op.is_dma) and d.eng == "pe":
                continue
            out.append(d)
            if not d.is_dma:
                d.sig = True
        op.deps = out

    def op(self, eng, fn, reads=(), writes=()):
        o = self._mk(eng, fn, False)
        self._deps(o, reads, writes)
        self.streams[eng].append(o)
        self.last[eng] = o
        return o

    def dma(self, eng, fn, reads=(), writes=()):
        o = self._mk(eng, fn, True)
        n = self.ndma[eng]
        self.ndma[eng] = n + 1
        o.sem_i = n % self.RING
        o.sem_val = 16 * (n // self.RING + 1)
        self._deps(o, reads, writes)
        self.streams[eng].append(o)
        self.dmas_since.append(o)
        return o

    def barrier(self):
        deps = []
        for e in self.CENG:
            if self.last[e] is not None:
                deps.append(self.last[e])
        latest = {}
        for d in self.dmas_since:
            latest[(d.eng, d.sem_i)] = d
        deps.extend(latest.values())
        self.dmas_since = []
        for e in self.ENGS:
            prev = self.pending[e] or []
            self.pending[e] = list(prev) + list(deps)

    def emit(self):
        nc = self.nc
        with ExitStack() as es:
            csem = {e: es.enter_context(nc.semaphore("c_" + e)) for e in self.CENG}
            dsem = {}
            for e in ("sp", "act", "pool"):
                if self.ndma[e]:
                    dsem[e] = [es.enter_context(nc.semaphore("d_%s%d" % (e, i))) for i in range(self.RING)]
            for e in self.CENG:
                k = 0
                for o in self.streams[e]:
                    if (not o.is_dma) and o.sig:
                        k += 1
                        o.sigidx = k
            block = es.enter_context(nc.Block())

            def run(ename, eng):
                waited = {}
                for o in self.streams[ename]:
                    need = {}
                    for d in o.deps:
                        if d.is_dma:
                            s = dsem[d.eng][d.sem_i]
                            v = d.sem_val
                        else:
                            s = csem[d.eng]
                            v = d.sigidx
                        key = id(s)
                        if key not in need or need[key][1] < v:
                            need[key] = (s, v)
                    if o.is_dma and o.sem_val > 16:
                        s = dsem[ename][o.sem_i]
                        key = id(s)
                        v = o.sem_val - 16
                        if key not in need or need[key][1] < v:
                            need[key] = (s, v)
                    for key, (s, v) in need.items():
                        if waited.get(key, 0) >= v:
                            continue
                        eng.wait_ge(s, v)
                        waited[key] = v
                    inst = o.fn(eng)
                    if o.is_dma:
                        inst.then_inc(dsem[ename][o.sem_i], 16)
                    elif o.sig:
                        inst.then_inc(csem[ename], 1)
                if ename in dsem:
                    n = self.ndma[ename]
                    for i in range(self.RING):
                        cnt = (n - i + self.RING - 1) // self.RING if n > i else 0
                        if cnt > 0:
                            eng.wait_ge(dsem[ename][i], 16 * cnt)

            @block.tensor
            def _(eng):
                run("pe", eng)

            @block.scalar
            def _(eng):
                run("act", eng)

            @block.vector
            def _(eng):
                run("dve", eng)

            @block.gpsimd
            def _(eng):
                run("pool", eng)

            @block.sync
            def _(eng):
                run("sp", eng)


def _rope_tables():
    half = 16
    inv_freq = (np.float32(10000.0) ** (-np.arange(half, dtype=np.float32) / np.float32(half))).astype(np.float32)
    t = np.arange(S_LAT)
    rows = (t // 64).astype(np.float32)
    cols = (t % 64).astype(np.float32)
    cos = np.zeros((128, S_LAT), np.float32)
    sin = np.zeros((128, S_LAT), np.float32)
    for p in range(128):
        d = p % 64
        pos = rows if d < 32 else cols
        ang = (pos * inv_freq[d % 16]).astype(np.float32)
        cos[p] = np.cos(ang)
        sin[p] = np.sin(ang)
    Rm = np.zeros((128, 128), np.float32)
    for base in range(0, 128, 32):
        for j in range(32):
            if j < 16:
                Rm[base + j, base + j + 16] = -1.0
            else:
                Rm[base + j, base + j - 16] = 1.0
    return cos, sin, np.ascontiguousarray(Rm.T)


def _dft_tables(n):
    N = 2 * n
    t = np.arange(n, dtype=np.float64)[:, None]
    f = np.arange(n, dtype=np.float64)[None, :]
    th = 2.0 * np.pi * (f + 0.5) * t / N
    cf = np.cos(th)
    sf = np.sin(th)
    ci = (cf.T * (2.0 / N))
    si = (sf.T * (2.0 / N))
    return cf.astype(np.float32), sf.astype(np.float32), ci.astype(np.float32), si.astype(np.float32)


def _hyena_feats(n):
    bands = 16
    t = np.linspace(0.0, 1.0, n, dtype=np.float32)[:, None]
    w = (np.float32(2.0 * math.pi) * np.arange(n, dtype=np.float32)[:, None] / np.float32(n)).astype(np.float32)
    f = np.linspace(1e-4, bands - 1, bands, dtype=np.float32)[None, :]
    z = np.concatenate([t, np.cos(f * w), -np.sin(f * w)], axis=-1).astype(np.float32)
    deltas = np.abs(np.linspace(math.log(1e-2) / 1.5, math.log(1e-2) / 0.3, D, dtype=np.float32))
    decay = np.exp(-t * deltas[None, :]).astype(np.float32)
    return np.ascontiguousarray(z.T), decay


class B:
    pass


def build(n_layers=DEPTH, layers=None):
    nc = bass.Bass("TRN2", target_bir_lowering=False)
    S = Sched(nc)
    g = B()

    def din(name, shape, dt=F32):
        return nc.dram_tensor(name, list(shape), dt, kind="ExternalInput").ap()

    def dscr(name, shape, dt=F32):
        return nc.dram_tensor(name, list(shape), dt).ap()

    x2 = din("x2", [NB * S_LAT, D])
    ctx2 = din("ctx2", [NB * S_CTX, D])
    cT = din("cT", [128, 8, 3])
    w_mod = din("w_mod", [DEPTH, D, 6 * D])
    b_mod = din("b_mod", [DEPTH, 6 * D])
    n_mix_pre = din("norm_mix_pre", [DEPTH, D])
    n_mix_post = din("norm_mix_post", [DEPTH, D])
    n_mlp_pre = din("norm_mlp_pre", [DEPTH, D])
    n_mlp_post = din("norm_mlp_post", [DEPTH, D])
    w_mlp_in = din("w_mlp_in", [DEPTH, D, DFF])
    w_mlp_out = din("w_mlp_out", [DEPTH, DFF, D])
    attn_w_qkv = din("attn_w_qkv", [2, D, 3 * D])
    attn_w_out = din("attn_w_out", [2, D, D])
    attn_lambda = din("attn_lambda", [2, 256])
    attn_subln = din("attn_subln", [2, 128])
    hy_w_in = din("hy_w_in", [1, D, 3 * D])
    hy_b_in = din("hy_b_in", [1, 3 * D])
    hy_w_short = din("hy_w_short", [1, 3, 3 * D])
    hy_b_short = din("hy_b_short", [1, 3 * D])
    hy_filt_w1 = din("hy_filt_w1", [1, 33, 64])
    hy_filt_b1 = din("hy_filt_b1", [1, 64])
    hy_filt_w2 = din("hy_filt_w2", [1, 2, 64, 64])
    hy_filt_b2 = din("hy_filt_b2", [1, 2, 64])
    hy_filt_freq = din("hy_filt_freq", [1, 64])
    hy_filt_w_out = din("hy_filt_w_out", [1, 64, 2 * D])
    hy_bias = din("hy_bias", [1, D])
    hy_w_out = din("hy_w_out", [1, D, D])
    hy_b_out = din("hy_b_out", [1, D])
    cv_w_pw1 = din("cv_w_pw1", [1, D, 2 * D])
    cv_b_pw1 = din("cv_b_pw1", [1, 2 * D])
    cv_w_dw = din("cv_w_dw", [1, 31, D])
    cv_b_dw = din("cv_b_dw", [1, D])
    cv_ln_g = din("cv_ln_g", [1, D])
    cv_ln_b = din("cv_ln_b", [1, D])
    cv_w_pw2 = din("cv_w_pw2", [1, D, D])
    cv_b_pw2 = din("cv_b_pw2", [1, D])
    cv_b_pw1T = din("cv_b_pw1T", [128, 16])
    cv_w_dwT = din("cv_w_dwT", [128, 8, 31])
    cv_b_dwT = din("cv_b_dwT", [128, 8])
    cv_ln_gT = din("cv_ln_gT", [128, 8])
    cv_ln_bT = din("cv_ln_bT", [128, 8])
    hy_b_inT = din("hy_b_inT", [128, 24])
    hy_w_shortT = din("hy_w_shortT", [128, 24, 3])
    hy_b_shortT = din("hy_b_shortT", [128, 24])
    hy_biasT = din("hy_biasT", [128, 8])
    hy_fb1T = din("hy_fb1T", [64, 1])
    hy_fb2T = din("hy_fb2T", [64, 2])
    hy_ffreqT = din("hy_ffreqT", [64, 1])
    k_cos = din("k_cos", [128, S_LAT])
    k_sin = din("k_sin", [128, S_LAT])
    k_rm = din("k_rm", [128, 128])
    k_ident = din("k_ident", [128, 128])
    k_dft = {}
    k_feat = {}
    k_decay = {}
    for n in (S_LAT, S_CTX):
        k_dft[n] = [din("k_dft%d_%d" % (n, q), [n, n], BF16) for q in range(4)]
        k_feat[n] = din("k_feat%d" % n, [33, n])
        k_decay[n] = din("k_decay%d" % n, [n, D])

    out = nc.dram_tensor("out", [NB * S_LAT, D], F32, kind="ExternalOutput").ap()

    hcur = dscr("hcur", [NTOK, D])
    hmid = dscr("hmid", [NTOK, D])
    mixT = dscr("mixT", [D, NTOK], BF16)
    vT = dscr("vT", [D, NTOK], BF16)
    modbc = dscr("modbc", [6, 3, D])

    def tok_rows(tt):
        b = tt // 18
        st = tt % 18
        if st < 16:
            return b, False, b * S_LAT + st * 128, b * TPB + st * 128
        return b, True, b * S_CTX + (st - 16) * 128, b * TPB + S_LAT + (st - 16) * 128

    def h_src(layer, tt):
        b, is_ctx, r, gt = tok_rows(tt)
        if layer == g.first:
            return (ctx2 if is_ctx else x2)[r:r + 128, :]
        return hcur[gt:gt + 128, :]

    def sb(ph, name, shape, dt):
        return ph.enter_context(nc.sbuf_tensor(name, list(shape), dt))

    def ps(ph, name, shape, dt=F32):
        return ph.enter_context(nc.psum_tensor(name, list(shape), dt))

    uid = [0]

    def nm(s):
        uid[0] += 1
        return "%s_%d" % (s, uid[0])

    class Ring:
        def __init__(self, ph, name, shape, dt, n, psum=False):
            self.bufs = []
            for i in range(n):
                t = (ps if psum else sb)(ph, nm(name), shape, dt)
                self.bufs.append((t, T(name)))
            self.i = 0

        def next(self):
            r = self.bufs[self.i % len(self.bufs)]
            self.i += 1
            return r

    def rstd_chain(ss, t_ss, epst, inv_n, n=1):
        S.op("act", lambda e: e.activation(out=ss, in_=ss, func=AF.Ln, scale=inv_n, bias=epst), reads=[t_ss], writes=[t_ss])
        S.op("act", lambda e: e.activation(out=ss, in_=ss, func=AF.Exp, scale=-0.5), reads=[t_ss], writes=[t_ss])

    def load_w_bf16(wsb, t_w, wd, kt, ncols, c0=0):
        for k in range(kt):
            S.dma("pool", lambda e, k=k: e.dma_start(out=wsb[:, k, 0:ncols], in_=wd[k * 128:(k + 1) * 128, c0:c0 + ncols]), writes=[t_w])

    def phase_mod(i):
        with ExitStack() as ph:
            craw = sb(ph, nm("craw"), [128, 8, 3], F32)
            ce = sb(ph, nm("ce"), [128, 8, 3], F32)
            scT = sb(ph, nm("scT"), [128, 8, 3], F32)
            modv = sb(ph, nm("modv"), [3, 6 * D], F32)
            bm = sb(ph, nm("bm"), [3, 6 * D], F32)
            gv = sb(ph, nm("gv"), [3, 4, D], F32)
            cmb = sb(ph, nm("cmb"), [3, 6, D], F32)
            wr = Ring(ph, "wmodc", [128, 8, 512], F32, 2)
            pr = Ring(ph, "pmod", [128, 512], F32, 2, psum=True)
            t_c, t_sc, t_modv, t_bm, t_gv, t_cmb = T(), T(), T(), T(), T(), T()
            S.dma("sp", lambda e: e.dma_start(out=craw[:], in_=cT[:, :, :]), writes=[t_c])
            S.dma("sp", lambda e: e.dma_start(out=bm[:], in_=b_mod[i, :].partition_broadcast(3)), writes=[t_bm])
            for q, gsrc in enumerate((n_mix_pre, n_mix_post, n_mlp_pre, n_mlp_post)):
                S.dma("sp", lambda e, q=q, gsrc=gsrc: e.dma_start(out=gv[:, q, :], in_=gsrc[i, :].partition_broadcast(3)), writes=[t_gv])
            S.op("act", lambda e: e.activation(out=ce[:], in_=craw[:], func=AF.Exp, scale=-1.0), reads=[t_c], writes=[t_sc])
            S.op("dve", lambda e: e.tensor_scalar_add(out=ce[:], in0=ce[:], scalar1=1.0), reads=[t_sc], writes=[t_sc])
            S.op("dve", lambda e: e.reciprocal(out=ce[:], in_=ce[:]), reads=[t_sc], writes=[t_sc])
            S.op("dve", lambda e: e.tensor_mul(out=scT[:], in0=craw[:], in1=ce[:]), reads=[t_sc, t_c], writes=[t_sc])
            for j in range(12):
                wt, t_w = wr.next()
                pt, t_p = pr.next()
                S.dma("sp", lambda e, wt=wt, j=j: e.dma_start(out=wt[:], in_=w_mod[i, :, j * 512:(j + 1) * 512].rearrange("(k p) n -> p k n", p=128)), writes=[t_w])

                def mm(e, wt=wt, pt=pt):
                    for k in range(8):
                        r = e.matmul(pt[0:3, :], lhsT=scT[:, k, :], rhs=wt[:, k, :], start=(k == 0), stop=(k == 7))
                    return r
                S.op("pe", mm, reads=[t_sc, t_w], writes=[t_p])
                S.op("dve", lambda e, pt=pt, j=j: e.tensor_add(out=modv[:, j * 512:(j + 1) * 512], in0=pt[0:3, :], in1=bm[:, j * 512:(j + 1) * 512]), reads=[t_p, t_bm], writes=[t_modv])
            def stt(e, o, a, gq):
                return e.scalar_tensor_tensor(out=cmb[:, o, :], in0=modv[:, a * D:(a + 1) * D], scalar=1.0, in1=gv[:, gq, :], op0=ALU.add, op1=ALU.mult)
            S.op("dve", lambda e: stt(e, 0, 1, 0), reads=[t_modv, t_gv], writes=[t_cmb])
            S.op("dve", lambda e: e.tensor_copy(out=cmb[:, 1, :], in_=modv[:, 0:D]), reads=[t_modv], writes=[t_cmb])
            S.op("dve", lambda e: e.tensor_mul(out=cmb[:, 2, :], in0=modv[:, 2 * D:3 * D], in1=gv[:, 1, :]), reads=[t_modv, t_gv], writes=[t_cmb])
            S.op("dve", lambda e: stt(e, 3, 4, 2), reads=[t_modv, t_gv], writes=[t_cmb])
            S.op("dve", lambda e: e.tensor_copy(out=cmb[:, 4, :], in_=modv[:, 3 * D:4 * D]), reads=[t_modv], writes=[t_cmb])
            S.op("dve", lambda e: e.tensor_mul(out=cmb[:, 5, :], in0=modv[:, 5 * D:6 * D], in1=gv[:, 3, :]), reads=[t_modv, t_gv], writes=[t_cmb])
            for q in range(6):
                S.dma("sp", lambda e, q=q: e.dma_start(out=modbc[q, :, :], in_=cmb[:, q, :]), reads=[t_cmb])
            S.barrier()

    class PreMix:
        def __init__(self, ph, layer, ident, t_ident, epst):
            self.layer = layer
            self.ident = ident
            self.t_ident = t_ident
            self.epst = epst
            self.hr = Ring(ph, "pm_h", [128, D], F32, 4)
            self.jr = Ring(ph, "pm_junk", [128, D], BF16, 1)
            self.tr = Ring(ph, "pm_t", [128, D], F32, 2)
            self.ur = Ring(ph, "pm_u", [128, D], BF16, 4)
            self.sr = Ring(ph, "pm_ss", [128, 1], F32, 4)
            self.sr4 = Ring(ph, "pm_ss4", [128, 4], F32, 3)
            self.pr = Ring(ph, "pm_ps", [128, 8, 128], BF16, 2, psum=True)
            self.A = [sb(ph, nm("pm_A"), [128, D], F32) for _ in range(2)]
            self.Bm = [sb(ph, nm("pm_B"), [128, D], F32) for _ in range(2)]
            self.t_ab = [T(), T()]
            self.cur = None
            self.nload = 0

        def set_mod(self, row):
            if self.cur == row:
                return
            self.cur = row
            k = self.nload % 2
            self.nload += 1
            self.k = k
            S.dma("sp", lambda e: e.dma_start(out=self.A[k][:], in_=modbc[0, row, :].partition_broadcast(128)), writes=[self.t_ab[k]])
            S.dma("sp", lambda e: e.dma_start(out=self.Bm[k][:], in_=modbc[1, row, :].partition_broadcast(128)), writes=[self.t_ab[k]])

        def group(self, tts, dsts):
            b, is_ctx, r, gt = tok_rows(tts[0])
            self.set_mod(2 if is_ctx else b)
            k = self.k
            A, Bm, t_ab = self.A[k], self.Bm[k], self.t_ab[k]
            ss, t_ss = self.sr4.next()
            S.op("pool", lambda e: e.memset(ss[:], 0.0), writes=[t_ss])
            hts = []
            for tt in tts:
                ht, t_h = self.hr.next()
                src = h_src(self.layer, tt)
                S.dma("sp", lambda e, ht=ht, src=src: e.dma_start(out=ht[:], in_=src), writes=[t_h])
                hts.append((ht, t_h))
            for ti, (ht, t_h) in enumerate(hts):
                jk, t_j = self.jr.next()
                S.op("act", lambda e, jk=jk, ht=ht, ti=ti: e.activation(out=jk[:], in_=ht[:], func=AF.Square, accum_out=ss[:, ti:ti + 1]), reads=[t_h, t_ss], writes=[t_j, t_ss])
            rstd_chain(ss[:, 0:len(tts)], t_ss, self.epst[:], 1.0 / D)
            ubs = []
            for ti, (ht, t_h) in enumerate(hts):
                tm, t_t = self.tr.next()
                ub, t_u = self.ur.next()
                S.op("dve", lambda e, tm=tm, ht=ht, ti=ti: e.scalar_tensor_tensor(out=tm[:], in0=ht[:], scalar=ss[:, ti:ti + 1], in1=A[:], op0=ALU.mult, op1=ALU.mult), reads=[t_h, t_ss, t_ab], writes=[t_t])
                S.op("dve", lambda e, ub=ub, tm=tm: e.tensor_add(out=ub[:], in0=tm[:], in1=Bm[:]), reads=[t_t, t_ab], writes=[t_u])
                ubs.append((ub, t_u))
            for (ub, t_u), (dst, t_dst) in zip(ubs, dsts):
                pt, t_p = self.pr.next()

                def tr(e, pt=pt, ub=ub):
                    for kk in range(8):
                        r_ = e.transpose(out=pt[:, kk, :], in_=ub[:, kk * 128:(kk + 1) * 128], identity=self.ident[:])
                    return r_
                S.op("pe", tr, reads=[t_u, self.t_ident], writes=[t_p])
                S.op("act", lambda e, dst=dst, pt=pt: e.copy(out=dst, in_=pt[:]), reads=[t_p], writes=[t_dst])

        def tile(self, tt, dst, t_dst):
            b, is_ctx, r, gt = tok_rows(tt)
            self.set_mod(2 if is_ctx else b)
            k = self.k
            A, Bm, t_ab = self.A[k], self.Bm[k], self.t_ab[k]
            ht, t_h = self.hr.next()
            jk, t_j = self.jr.next()
            tm, t_t = self.tr.next()
            ub, t_u = self.ur.next()
            ss, t_ss = self.sr.next()
            pt, t_p = self.pr.next()
            src = h_src(self.layer, tt)
            S.dma("sp", lambda e: e.dma_start(out=ht[:], in_=src), writes=[t_h])
            S.op("pool", lambda e: e.memset(ss[:], 0.0), writes=[t_ss])
            S.op("act", lambda e: e.activation(out=jk[:], in_=ht[:], func=AF.Square, accum_out=ss[:]), reads=[t_h], writes=[t_j, t_ss])
            rstd_chain(ss[:], t_ss, self.epst[:], 1.0 / D)
            S.op("dve", lambda e: e.scalar_tensor_tensor(out=tm[:], in0=ht[:], scalar=ss[:, 0:1], in1=A[:], op0=ALU.mult, op1=ALU.mult), reads=[t_h, t_ss, t_ab], writes=[t_t])
            S.op("dve", lambda e: e.tensor_add(out=ub[:], in0=tm[:], in1=Bm[:]), reads=[t_t, t_ab], writes=[t_u])

            def tr(e):
                for kk in range(8):
                    r_ = e.transpose(out=pt[:, kk, :], in_=ub[:, kk * 128:(kk + 1) * 128], identity=self.ident[:])
                return r_
            S.op("pe", tr, reads=[t_u, self.t_ident], writes=[t_p])
            S.op("act", lambda e: e.copy(out=dst, in_=pt[:]), reads=[t_p], writes=[t_dst])

    def load_consts(ph):
        ident = sb(ph, nm("ident"), [128, 128], BF16)
        identf = sb(ph, nm("identf"), [128, 128], F32)
        epst = sb(ph, nm("eps"), [128, 1], F32)
        t_id = T()
        S.dma("sp", lambda e: e.dma_start(out=identf[:], in_=k_ident[:, :]), writes=[t_id])
        S.op("dve", lambda e: e.tensor_copy(out=ident[:], in_=identf[:]), reads=[t_id], writes=[t_id])
        S.op("pool", lambda e: e.memset(epst[:], EPS), writes=[t_id])
        return ident, t_id, epst

    def phase_attn(i, j, ctx_q):
        lam_init = 0.8 - 0.6 * math.exp(-0.3 * i)
        with ExitStack() as ph:
            ident, t_id, epst = load_consts(ph)
            Wqkv = sb(ph, nm("Wqkv"), [128, 8, 3 * D], BF16)
            t_w = T()
            load_w_bf16(Wqkv, t_w, attn_w_qkv[j], 8, 3 * D)
            cosT = sb(ph, nm("cosT"), [128, S_LAT], F32)
            sinT = sb(ph, nm("sinT"), [128, S_LAT], F32)
            Rm = sb(ph, nm("Rm"), [128, 128], F32)
            ones = sb(ph, nm("ones"), [128, 128], BF16)
            Sel = sb(ph, nm("Sel"), [128, 2, 128], F32)
            lamt = sb(ph, nm("lamt"), [128, 256], F32)
            lprod = sb(ph, nm("lprod"), [128, 128], F32)
            lsum = sb(ph, nm("lsum"), [128, 2], F32)
            neglam = sb(ph, nm("neglam"), [128, 1], F32)
            subg = sb(ph, nm("subg"), [128, 1], F32)
            t_k = T()
            t_lam = T()
            S.dma("sp", lambda e: e.dma_start(out=cosT[:], in_=k_cos[:, :]), writes=[t_k])
            S.dma("sp", lambda e: e.dma_start(out=sinT[:], in_=k_sin[:, :]), writes=[t_k])
            S.dma("sp", lambda e: e.dma_start(out=Rm[:], in_=k_rm[:, :]), writes=[t_k])
            S.op("pool", lambda e: e.memset(ones[:], 1.0), writes=[t_k])
            S.op("pool", lambda e: e.memset(Sel[:], 0.0), writes=[t_k])
            S.op("pool", lambda e: e.memset(Sel[0:64, 0, :], 1.0 / 64), writes=[t_k])
            S.op("pool", lambda e: e.memset(Sel[64:128, 1, :], 1.0 / 64), writes=[t_k])
            S.dma("sp", lambda e: e.dma_start(out=lamt[:], in_=attn_lambda[j, :].partition_broadcast(128)), writes=[t_lam])
            S.dma("sp", lambda e: e.dma_start(out=subg[:], in_=attn_subln[j, :].rearrange("(p o) -> p o", o=1)), writes=[t_lam])
            S.op("dve", lambda e: e.tensor_mul(out=lprod[:, 0:64], in0=lamt[:, 0:64], in1=lamt[:, 64:128]), reads=[t_lam], writes=[t_lam])
            S.op("dve", lambda e: e.tensor_mul(out=lprod[:, 64:128], in0=lamt[:, 128:192], in1=lamt[:, 192:256]), reads=[t_lam], writes=[t_lam])
            S.op("dve", lambda e: e.reduce_sum(out=lsum[:, 0:1], in_=lprod[:, 0:64], axis=AX.X), reads=[t_lam], writes=[t_lam])
            S.op("dve", lambda e: e.reduce_sum(out=lsum[:, 1:2], in_=lprod[:, 64:128], axis=AX.X), reads=[t_lam], writes=[t_lam])
            S.op("act", lambda e: e.activation(out=lsum[:], in_=lsum[:], func=AF.Exp), reads=[t_lam], writes=[t_lam])
            S.op("dve", lambda e: e.scalar_tensor_tensor(out=neglam[:], in0=lsum[:, 1:2], scalar=-lam_init, in1=lsum[:, 0:1], op0=ALU.add, op1=ALU.subtract), reads=[t_lam], writes=[t_lam])
            S.op("dve", lambda e: e.tensor_scalar_mul(out=subg[:], in0=subg[:], scalar1=1.0 - lam_init), reads=[t_lam], writes=[t_lam])

            uT = sb(ph, nm("uT"), [128, 8, TPB], BF16)
            V = sb(ph, nm("V"), [128, 18, D], BF16)
            t_uT = [T() for _ in range(18)]
            t_V = [T() for _ in range(18)]
            for b in range(NB):
                with ExitStack() as pa:
                    pm = PreMix(pa, i, ident, t_id, epst)
                    vr = Ring(pa, "pV", [128, 2, 512], F32, 2, psum=True)
                    pgroups = [[0, 1, 2, 3], [4, 5, 6, 7], [8, 9, 10, 11], [12, 13, 14, 15], [16, 17]]

                    def pmg(gi_):
                        sts_ = pgroups[gi_]
                        pm.group([b * 18 + s_ for s_ in sts_], [(uT[:, :, s_ * 128:(s_ + 1) * 128], t_uT[s_]) for s_ in sts_])
                    pmg(0)
                    for st in range(18):
                        if st % 4 == 0 and st // 4 + 1 < len(pgroups):
                            pmg(st // 4 + 1)
                        pv, t_pv = vr.next()

                        def mmv(e, pv=pv, st=st):
                            for n2 in range(2):
                                for k in range(8):
                                    r_ = e.matmul(pv[:, n2, :], lhsT=uT[:, k, st * 128:(st + 1) * 128], rhs=Wqkv[:, k, 2 * D + n2 * 512:2 * D + (n2 + 1) * 512], start=(k == 0), stop=(k == 7))
                            return r_
                        S.op("pe", mmv, reads=[t_uT[st], t_w], writes=[t_pv])
                        S.op("dve", lambda e, pv=pv, st=st: e.tensor_copy(out=V[:, st, :], in_=pv[:].rearrange("p a b -> p (a b)")), reads=[t_pv], writes=[t_V[st]])
                    S.barrier()
                with ExitStack() as pb:
                    qk_r = Ring(pb, "QKT", [128, 2, TPB], BF16, 2)
                    sps = Ring(pb, "Sps", [128, 2, 512], F32, 2, psum=True)
                    O = [ps(pb, nm("O"), [128, 512]) for _ in range(2)]
                    smb = ps(pb, nm("smb"), [128, 512])
                    bc = ps(pb, nm("bc"), [128, 512])
                    t_O = [T(), T()]
                    t_smb, t_bc = T(), T()
                    sms_r = Ring(pb, "smS", [128, 512], F32, 1)
                    qf_r = Ring(pb, "qf", [128, 512], F32, 2)
                    t1_r = Ring(pb, "t1", [128, 512], F32, 2)
                    t2_r = Ring(pb, "t2", [128, 512], F32, 2)
                    pt_r = Ring(pb, "PT", [128, 2, 512], BF16, 3)
                    r_r = Ring(pb, "rr", [128, 2, 512], F32, 2)
                    o_r = Ring(pb, "oo", [128, 3, 512], F32, 1)
                    o2_r = Ring(pb, "o2", [128, 512], BF16, 1)
                    rs_r = Ring(pb, "rs", [128, 512], F32, 1)
                    mo_r = Ring(pb, "mo", [128, 512], BF16, 2)
                    chunks = [(0, 512, True), (512, 512, True), (1024, 512, True), (1536, 512, True), (2048, 256, False)]
                    defq = []

                    def tick_defq():
                        for it_ in list(defq):
                            it_[0] -= 1
                            if it_[0] <= 0:
                                defq.remove(it_)
                                it_[1]()

                    def flush_defq():
                        for it_ in list(defq):
                            defq.remove(it_)
                            it_[1]()
                    for h in range(8):
                        qk, t_qk = qk_r.next()
                        t_qkc = [[T(), T()] for _ in chunks]
                        deferred = [None]

                        def flush():
                            if deferred[0] is not None:
                                deferred[0]()
                                deferred[0] = None
                        for ci, (c0, n, is_lat) in enumerate(chunks):
                            for which in (0, 1):
                                if which == 0 and (not is_lat) and (not ctx_q):
                                    continue
                                pp, t_pp = sps.next()
                                off = which * D + h * 128

                                def mmq(e, pp=pp, off=off, c0=c0, n=n):
                                    for k in range(8):
                                        r_ = e.matmul(pp[:, 0, 0:n], lhsT=Wqkv[:, k, off:off + 128], rhs=uT[:, k, c0:c0 + n], start=(k == 0), stop=(k == 7))
                                    return r_
                                sts = list(range(c0 // 128, (c0 + n) // 128))
                                S.op("pe", mmq, reads=[t_w] + [t_uT[s_] for s_ in sts], writes=[t_pp])
                                dst = qk[:, which, c0:c0 + n]
                                if is_lat:
                                    qf, t_qf = qf_r.next()
                                    S.op("act", lambda e, qf=qf, pp=pp: e.copy(out=qf[:], in_=pp[:, 0, :]), reads=[t_pp], writes=[t_qf])
                                    flush()

                                    def rope(pp=pp, t_pp=t_pp, qf=qf, t_qf=t_qf, c0=c0, dst=dst, t_d=t_qkc[ci][which]):
                                        t1, t_t1 = t1_r.next()
                                        t2, t_t2 = t2_r.next()
                                        S.op("pe", lambda e: e.matmul(pp[:, 1, :], lhsT=Rm[:], rhs=qf[:], start=True, stop=True), reads=[t_qf, t_k], writes=[t_pp])
                                        S.op("dve", lambda e: e.tensor_mul(out=t1[:], in0=qf[:], in1=cosT[:, c0:c0 + 512]), reads=[t_qf, t_k], writes=[t_t1])
                                        S.op("dve", lambda e: e.tensor_mul(out=t2[:], in0=pp[:, 1, :], in1=sinT[:, c0:c0 + 512]), reads=[t_pp, t_k], writes=[t_t2])
                                        S.op("pool", lambda e: e.tensor_tensor(out=dst, in0=t1[:], in1=t2[:], op=ALU.add), reads=[t_t1, t_t2], writes=[t_d, t_qk])
                                    deferred[0] = rope
                                else:
                                    S.op("act", lambda e, dst=dst, pp=pp, n=n: e.copy(out=dst, in_=pp[:, 0, 0:n]), reads=[t_pp], writes=[t_qkc[ci][which], t_qk])
                        flush()
                        for ci, (c0, n, is_lat) in enumerate(chunks):
                            if (not is_lat) and (not ctx_q):
                                continue
                            kts = list(range(18)) if is_lat else [16, 17]
                            t_kall = [t_qkc[c_][1] for c_ in range(len(chunks))]
                            pend = None
                            for kt in kts + [None]:
                                if kt is not None:
                                    sp_, t_sp = sps.next()

                                    def mms(e, sp_=sp_, kt=kt, c0=c0, n=n, qk=qk):
                                        for comp in (0, 1):
                                            p0 = comp * 64
                                            r_ = e.matmul(sp_[:, comp, 0:n], lhsT=qk[p0:p0 + 64, 1, kt * 128:(kt + 1) * 128], rhs=qk[p0:p0 + 64, 0, c0:c0 + n], start=True, stop=True, tile_position=(p0, 0))
                                        return r_
                                    S.op("pe", mms, reads=t_kall + [t_qkc[ci][0], t_qk], writes=[t_sp])
                                    ptt, t_pt = pt_r.next()
                                    S.op("act", lambda e, ptt=ptt, sp_=sp_, n=n: e.activation(out=ptt[:, :, 0:n], in_=sp_[:, :, 0:n], func=AF.Exp, scale=0.125), reads=[t_sp], writes=[t_pt])
                                    cur = (kt, ptt, t_pt)
                                else:
                                    cur = None
                                if pend is not None:
                                    pkt, pptt, pt_pt = pend
                                    first = (pkt == kts[0])
                                    lastp = (pkt == kts[-1])

                                    def mmpv(e, pkt=pkt, pptt=pptt, first=first, lastp=lastp, n=n, h=h):
                                        for comp in (0, 1):
                                            e.matmul(O[comp][:, 0:n], lhsT=V[:, pkt, h * 128:(h + 1) * 128], rhs=pptt[:, comp, 0:n], start=first, stop=lastp)
                                        for comp in (0, 1):
                                            r_ = e.matmul(smb[comp * 64:(comp + 1) * 64, 0:n], lhsT=ones[:, 0:64], rhs=pptt[:, comp, 0:n], start=first, stop=lastp, tile_position=(0, comp * 64), skip_group_check=True)
                                        return r_
                                    S.op("pe", mmpv, reads=[pt_pt, t_k, t_V[pkt]], writes=[t_O[0], t_O[1], t_smb])
                                pend = cur
                                tick_defq()
                            rr, t_rr = r_r.next()
                            oo, t_oo = o_r.next()
                            o2, t_o2 = o2_r.next()
                            rs, t_rs = rs_r.next()
                            mo, t_mo = mo_r.next()
                            smS, t_smS = sms_r.next()
                            g0 = b * TPB + c0
                            flush_defq()
                            for c_ in (0, 1):
                                S.op("act", lambda e, c_=c_, rr=rr, n=n: e.copy(out=rr[:, c_, 0:n], in_=O[c_][:, 0:n]), reads=[t_O[c_]], writes=[t_rr])
                            S.op("act", lambda e, smS=smS, n=n: e.copy(out=smS[:, 0:n], in_=smb[:, 0:n]), reads=[t_smb], writes=[t_smS])
                            S.op("dve", lambda e, smS=smS, n=n: e.reciprocal(out=smS[:, 0:n], in_=smS[:, 0:n]), reads=[t_smS], writes=[t_smS])

                            def tailA(rr=rr, t_rr=t_rr, oo=oo, t_oo=t_oo, o2=o2, t_o2=t_o2, smS=smS, t_smS=t_smS, n=n):
                                for c_ in (0, 1):
                                    S.op("pe", lambda e, c_=c_: e.matmul(bc[:, 0:n], lhsT=Sel[:, c_, :], rhs=smS[:, 0:n], start=True, stop=True), reads=[t_smS, t_k], writes=[t_bc])
                                    S.op("dve", lambda e, c_=c_: e.tensor_mul(out=oo[:, c_, 0:n], in0=rr[:, c_, 0:n], in1=bc[:, 0:n]), reads=[t_bc, t_rr], writes=[t_oo])
                                S.op("dve", lambda e: e.scalar_tensor_tensor(out=oo[:, 2, 0:n], in0=oo[:, 1, 0:n], scalar=neglam[:, 0:1], in1=oo[:, 0, 0:n], op0=ALU.mult, op1=ALU.add), reads=[t_oo, t_lam], writes=[t_oo])
                                S.op("pool", lambda e: e.tensor_tensor(out=o2[:, 0:n], in0=oo[:, 2, 0:n], in1=oo[:, 2, 0:n], op=ALU.mult), reads=[t_oo], writes=[t_o2])

                            def tailB(oo=oo, t_oo=t_oo, o2=o2, t_o2=t_o2, rs=rs, t_rs=t_rs, mo=mo, t_mo=t_mo, n=n, g0=g0, h=h):
                                S.op("pe", lambda e: e.matmul(bc[:, 0:n], lhsT=ones[:], rhs=o2[:, 0:n], start=True, stop=True), reads=[t_o2, t_k], writes=[t_bc])
                                S.op("act", lambda e: e.activation(out=rs[:, 0:n], in_=bc[:, 0:n], func=AF.Ln, scale=1.0 / 128, bias=epst[:]), reads=[t_bc, t_id], writes=[t_rs])
                                S.op("act", lambda e: e.activation(out=rs[:, 0:n], in_=rs[:, 0:n], func=AF.Exp, scale=-0.5), reads=[t_rs], writes=[t_rs])
                                S.op("dve", lambda e: e.scalar_tensor_tensor(out=mo[:, 0:n], in0=oo[:, 2, 0:n], scalar=subg[:, 0:1], in1=rs[:, 0:n], op0=ALU.mult, op1=ALU.mult), reads=[t_oo, t_rs, t_lam], writes=[t_mo])
                                S.dma("pool", lambda e: e.dma_start(out=mixT[h * 128:(h + 1) * 128, g0:g0 + n], in_=mo[:, 0:n]), reads=[t_mo])
                            defq.append([4, tailA])
                            defq.append([9, tailB])
                        flush_defq()
                    S.barrier()
            S.barrier()

    def phase_p3a(i, w_o, b_o, skip_ctx):
        with ExitStack() as ph:
            ident, t_id, epst = load_consts(ph)
            Wo = sb(ph, nm("Wo"), [128, 8, D], BF16)
            t_w = T()
            load_w_bf16(Wo, t_w, w_o, 8, D)
            bo = None
            if b_o is not None:
                bo = sb(ph, nm("bo"), [128, D], F32)
                S.dma("sp", lambda e: e.dma_start(out=bo[:], in_=b_o.partition_broadcast(128)), writes=[t_w])
            GAB = [[sb(ph, nm("GAB"), [128, D], F32) for _ in range(3)] for _ in range(2)]
            t_ab = [T(), T()]
            mx_r = Ring(ph, "mx", [128, 8, 512], BF16, 2)
            vt_r = Ring(ph, "vt", [128, 8, 512], BF16, 2)
            py_r = Ring(ph, "py", [128, 2, 512], F32, 3, psum=True)
            pt_r = Ring(ph, "ptr", [128, 8, 128], BF16, 2, psum=True)
            y_r = Ring(ph, "y", [128, D], F32, 8)
            h_r = Ring(ph, "hold", [128, D], F32, 8)
            hm_r = Ring(ph, "hm", [128, D], F32, 8)
            tm_r = Ring(ph, "tm", [128, D], F32, 3)
            vb_r = Ring(ph, "vb", [128, D], BF16, 8)
            jk_r = Ring(ph, "jk", [128, D], BF16, 2)
            ss_r = Ring(ph, "ss", [128, 8], F32, 6)

            glist = []
            nload = 0
            for b in range(NB):
                for seg in (0, 1):
                    if seg == 1 and skip_ctx:
                        continue
                    groups = [(g_ * 512, 512) for g_ in range(4)] if seg == 0 else [(S_LAT, 256)]
                    for gi, (c0, n) in enumerate(groups):
                        glist.append(dict(b=b, seg=seg, c0=c0, n=n, newmod=(gi == 0), k=nload % 2, row=(2 if seg == 1 else b)))
                    nload += 1

            def S0(gd):
                b, c0, n, k = gd["b"], gd["c0"], gd["n"], gd["k"]
                if gd["newmod"]:
                    for q3, q in enumerate((2, 3, 4)):
                        S.dma("sp", lambda e, q=q, dstt=GAB[k][q3], row=gd["row"]: e.dma_start(out=dstt[:], in_=modbc[q, row, :].partition_broadcast(128)), writes=[t_ab[k]])
                g0 = b * TPB + c0
                gd["g0"] = g0
                nti = n // 128
                mx, t_mx = mx_r.next()
                ss, t_ss = ss_r.next()
                gd["ss"], gd["t_ss"] = ss, t_ss
                S.dma("sp", lambda e, mx=mx, g0=g0, n=n: e.dma_start(out=mx[:, :, 0:n], in_=mixT[:, g0:g0 + n].rearrange("(k p) t -> p k t", p=128)), writes=[t_mx])
                S.op("pool", lambda e, ss=ss: e.memset(ss[:], 0.0), writes=[t_ss])
                tiles = []
                for ti in range(nti):
                    tt = b * 18 + (c0 // 128) + ti
                    py, t_py = py_r.next()
                    y, t_y = y_r.next()
                    ho, t_ho = h_r.next()
                    src = h_src(i, tt)
                    S.dma("sp", lambda e, ho=ho, src=src: e.dma_start(out=ho[:], in_=src), writes=[t_ho])

                    def mmy(e, py=py, mx=mx, ti=ti):
                        for n2 in range(2):
                            for kk in range(8):
                                r_ = e.matmul(py[:, n2, :], lhsT=mx[:, kk, ti * 128:(ti + 1) * 128], rhs=Wo[:, kk, n2 * 512:(n2 + 1) * 512], start=(kk == 0), stop=(kk == 7))
                        return r_
                    S.op("pe", mmy, reads=[t_mx, t_w], writes=[t_py])
                    pyf = py[:].rearrange("p a b -> p (a b)")
                    if bo is not None:
                        S.op("dve", lambda e, y=y, pyf=pyf: e.tensor_add(out=y[:], in0=pyf, in1=bo[:]), reads=[t_py, t_w], writes=[t_y])
                    else:
                        S.op("act", lambda e, y=y, pyf=pyf: e.copy(out=y[:], in_=pyf), reads=[t_py], writes=[t_y])
                    tiles.append(dict(tt=tt, y=y, t_y=t_y, ho=ho, t_ho=t_ho))
                gd["tiles"] = tiles

            def S12(gd):
                ss, t_ss, k, tiles = gd["ss"], gd["t_ss"], gd["k"], gd["tiles"]
                nti = len(tiles)
                for ti, tl in enumerate(tiles):
                    jk, t_jk = jk_r.next()
                    S.op("act", lambda e, jk=jk, y=tl["y"], ss=ss, ti=ti: e.activation(out=jk[:], in_=y[:], func=AF.Square, accum_out=ss[:, ti:ti + 1]), reads=[tl["t_y"], t_ss], writes=[t_jk, t_ss])
                rstd_chain(ss[:, 0:nti], t_ss, epst[:], 1.0 / D)
                for ti, tl in enumerate(tiles):
                    tm, t_tm = tm_r.next()
                    hm, t_hm = hm_r.next()
                    S.op("dve", lambda e, tm=tm, y=tl["y"], ss=ss, k=k, ti=ti: e.scalar_tensor_tensor(out=tm[:], in0=y[:], scalar=ss[:, ti:ti + 1], in1=GAB[k][0][:], op0=ALU.mult, op1=ALU.mult), reads=[tl["t_y"], t_ss, t_ab[k]], writes=[t_tm])
                    S.op("pool", lambda e, hm=hm, tm=tm, ho=tl["ho"]: e.tensor_tensor(out=hm[:], in0=tm[:], in1=ho[:], op=ALU.add), reads=[t_tm, tl["t_ho"]], writes=[t_hm])
                    gt = tok_rows(tl["tt"])[3]
                    S.dma("pool", lambda e, hm=hm, gt=gt: e.dma_start(out=hmid[gt:gt + 128, :], in_=hm[:]), reads=[t_hm])
                    tl["hm"], tl["t_hm"] = hm, t_hm

            def S34(gd):
                ss, t_ss, k, tiles = gd["ss"], gd["t_ss"], gd["k"], gd["tiles"]
                nti = len(tiles)
                for ti, tl in enumerate(tiles):
                    jk, t_jk = jk_r.next()
                    S.op("act", lambda e, jk=jk, hm=tl["hm"], ss=ss, ti=ti: e.activation(out=jk[:], in_=hm[:], func=AF.Square, accum_out=ss[:, 4 + ti:5 + ti]), reads=[tl["t_hm"], t_ss], writes=[t_jk, t_ss])
                rstd_chain(ss[:, 4:4 + nti], t_ss, epst[:], 1.0 / D)
                vbs = []
                for ti, tl in enumerate(tiles):
                    tm, t_tm = tm_r.next()
                    vb, t_vb = vb_r.next()
                    S.op("dve", lambda e, tm=tm, hm=tl["hm"], ss=ss, k=k, ti=ti: e.scalar_tensor_tensor(out=tm[:], in0=hm[:], scalar=ss[:, 4 + ti:5 + ti], in1=GAB[k][1][:], op0=ALU.mult, op1=ALU.mult), reads=[tl["t_hm"], t_ss, t_ab[k]], writes=[t_tm])
                    S.op("dve", lambda e, vb=vb, tm=tm, k=k: e.tensor_add(out=vb[:], in0=tm[:], in1=GAB[k][2][:]), reads=[t_tm, t_ab[k]], writes=[t_vb])
                    vbs.append((vb, t_vb))
                gd["vbs"] = vbs

            def S5(gd):
                vt, t_vt = vt_r.next()
                g0, n = gd["g0"], gd["n"]
                for ti, (vb, t_vb) in enumerate(gd["vbs"]):
                    ptr, t_ptr = pt_r.next()

                    def tr(e, ptr=ptr, vb=vb):
                        for kk in range(8):
                            r_ = e.transpose(out=ptr[:, kk, :], in_=vb[:, kk * 128:(kk + 1) * 128], identity=ident[:])
                        return r_
                    S.op("pe", tr, reads=[t_vb, t_id], writes=[t_ptr])
                    S.op("act", lambda e, vt=vt, ptr=ptr, ti=ti: e.copy(out=vt[:, :, ti * 128:(ti + 1) * 128], in_=ptr[:]), reads=[t_ptr], writes=[t_vt])
                S.dma("pool", lambda e, vt=vt, g0=g0, n=n: e.dma_start(out=vT[:, g0:g0 + n].rearrange("(k p) t -> p k t", p=128), in_=vt[:, :, 0:n]), reads=[t_vt])

            stages = (S0, S12, S34, S5)
            segs = []
            for gd in glist:
                if gd["newmod"]:
                    segs.append([])
                segs[-1].append(gd)
            for sg in segs:
                ng = len(sg)
                for it in range(ng + len(stages) - 1):
                    for si, fn in enumerate(stages):
                        gi = it - si
                        if 0 <= gi < ng:
                            fn(sg[gi])
            S.barrier()

    def phase_p3b(i, last):
        with ExitStack() as ph:
            ident, t_id, epst = load_consts(ph)
            Win = sb(ph, nm("Win"), [128, 8, DFF], BF16)
            Wout = sb(ph, nm("Wout"), [128, 32, D], BF16)
            t_winc = [T() for _ in range(4)]
            t_woutc = [T() for _ in range(4)]
            for cc in range(4):
                for k_ in range(8):
                    S.dma("pool", lambda e, k_=k_, cc=cc: e.dma_start(out=Win[:, k_, cc * 1024:(cc + 1) * 1024], in_=w_mlp_in[i][k_ * 128:(k_ + 1) * 128, cc * 1024:(cc + 1) * 1024]), writes=[t_winc[cc]])
            for cc in range(4):
                for k_ in range(8):
                    kk_ = cc * 8 + k_
                    S.dma("pool", lambda e, kk_=kk_: e.dma_start(out=Wout[:, kk_, :], in_=w_mlp_out[i][kk_ * 128:(kk_ + 1) * 128, :]), writes=[t_woutc[cc]])
            G2 = [sb(ph, nm("G2"), [128, D], F32) for _ in range(2)]
            t_ab = [T(), T()]
            vt_r = Ring(ph, "vt", [128, 8, 512], BF16, 2)
            hid = sb(ph, nm("hid"), [128, 32, 512], BF16)
            t_hid = [T() for _ in range(32)]
            p1_r = Ring(ph, "p1", [128, 512], F32, 2, psum=True)
            p2_r = Ring(ph, "p2", [128, 2, 512], F32, 2, psum=True)
            rl_r = Ring(ph, "rl", [128, 512], F32, 2)
            m_r = Ring(ph, "m", [128, D], F32, 1)
            hm_r = Ring(ph, "hm", [128, D], F32, 1)
            tm_r = Ring(ph, "tm", [128, D], F32, 1)
            ho_r = Ring(ph, "ho", [128, D], F32, 1)
            jk_r = Ring(ph, "jk", [128, D], BF16, 1)
            ss_r = Ring(ph, "ss", [128, 1], F32, 4)
            nload = [0]
            for b in range(NB):
                for seg in (0, 1):
                    if seg == 1 and last:
                        continue
                    row = 2 if seg == 1 else b
                    k = nload[0] % 2
                    nload[0] += 1
                    S.dma("sp", lambda e, k=k, row=row: e.dma_start(out=G2[k][:], in_=modbc[5, row, :].partition_broadcast(128)), writes=[t_ab[k]])
                    groups = [(g_ * 512, 512) for g_ in range(4)] if seg == 0 else [(S_LAT, 256)]
                    for (c0, n) in groups:
                        g0 = b * TPB + c0
                        vt, t_vt = vt_r.next()
                        S.dma("sp", lambda e, vt=vt, g0=g0, n=n: e.dma_start(out=vt[:, :, 0:n], in_=vT[:, g0:g0 + n].rearrange("(k p) t -> p k t", p=128)), writes=[t_vt])
                        for f in range(32):
                            p1, t_p1 = p1_r.next()
                            rl, t_rl = rl_r.next()

                            def mm1(e, p1=p1, vt=vt, f=f, n=n):
                                for kk in range(8):
                                    r_ = e.matmul(p1[:, 0:n], lhsT=Win[:, kk, f * 128:(f + 1) * 128], rhs=vt[:, kk, 0:n], start=(kk == 0), stop=(kk == 7))
                                return r_
                            S.op("pe", mm1, reads=[t_vt, t_winc[f // 8]], writes=[t_p1])
                            S.op("act", lambda e, rl=rl, p1=p1, n=n: e.activation(out=rl[:, 0:n], in_=p1[:, 0:n], func=AF.Relu), reads=[t_p1], writes=[t_rl])
                            S.op("dve", lambda e, rl=rl, f=f, n=n: e.tensor_mul(out=hid[:, f, 0:n], in0=rl[:, 0:n], in1=rl[:, 0:n]), reads=[t_rl], writes=[t_hid[f]])
                        for ti in range(n // 128):
                            tt = b * 18 + (c0 // 128) + ti
                            gt = tok_rows(tt)[3]
                            p2, t_p2 = p2_r.next()
                            m, t_m = m_r.next()
                            hm, t_hm = hm_r.next()
                            tm, t_tm = tm_r.next()
                            ho, t_ho = ho_r.next()
                            jk, t_jk = jk_r.next()
                            ss, t_ss = ss_r.next()
                            S.dma("sp", lambda e, hm=hm, gt=gt: e.dma_start(out=hm[:], in_=hmid[gt:gt + 128, :]), writes=[t_hm])

                            def mm2(e, p2=p2, ti=ti):
                                for n2 in range(2):
                                    for f in range(32):
                                        r_ = e.matmul(p2[:, n2, :], lhsT=hid[:, f, ti * 128:(ti + 1) * 128], rhs=Wout[:, f, n2 * 512:(n2 + 1) * 512], start=(f == 0), stop=(f == 31))
                                return r_
                            S.op("pe", mm2, reads=t_hid + t_woutc, writes=[t_p2])
                            S.op("act", lambda e, m=m, p2=p2: e.copy(out=m[:], in_=p2[:].rearrange("p a b -> p (a b)")), reads=[t_p2], writes=[t_m])
                            S.op("pool", lambda e, ss=ss: e.memset(ss[:], 0.0), writes=[t_ss])
                            S.op("act", lambda e, jk=jk, m=m, ss=ss: e.activation(out=jk[:], in_=m[:], func=AF.Square, accum_out=ss[:, 0:1]), reads=[t_m], writes=[t_jk, t_ss])
                            rstd_chain(ss[:, 0:1], t_ss, epst[:], 1.0 / D)
                            S.op("dve", lambda e, tm=tm, m=m, ss=ss, k=k: e.scalar_tensor_tensor(out=tm[:], in0=m[:], scalar=ss[:, 0:1], in1=G2[k][:], op0=ALU.mult, op1=ALU.mult), reads=[t_m, t_ss, t_ab[k]], writes=[t_tm])
                            S.op("pool", lambda e, ho=ho, tm=tm, hm=hm: e.tensor_tensor(out=ho[:], in0=tm[:], in1=hm[:], op=ALU.add), reads=[t_tm, t_hm], writes=[t_ho])
                            if last:
                                r0 = b * S_LAT + c0 + ti * 128
                                S.dma("pool", lambda e, ho=ho, r0=r0: e.dma_start(out=out[r0:r0 + 128, :], in_=ho[:]), reads=[t_ho])
                            else:
                                S.dma("pool", lambda e, ho=ho, gt=gt: e.dma_start(out=hcur[gt:gt + 128, :], in_=ho[:]), reads=[t_ho])
            S.barrier()


    def token_groups(skip_ctx=False):
        for b in range(NB):
            for seg in (0, 1):
                if seg == 1 and skip_ctx:
                    continue
                groups = [(g_ * 512, 512) for g_ in range(4)] if seg == 0 else [(S_LAT, 256)]
                for (c0, n) in groups:
                    yield b, seg, c0, n

    def phase_proj_fm(i, w_d, nf, biasT_d, post):
        with ExitStack() as ph:
            ident, t_id, epst = load_consts(ph)
            W = sb(ph, nm("Wp"), [128, 8, nf * 128], BF16)
            t_w = T()
            load_w_bf16(W, t_w, w_d, 8, nf * 128)
            bT = sb(ph, nm("bT"), [128, nf], F32)
            S.dma("sp", lambda e: e.dma_start(out=bT[:], in_=biasT_d[:, :]), writes=[t_w])
            pm = PreMix(ph, i, ident, t_id, epst)
            ug_r = Ring(ph, "ug", [128, 8, 512], BF16, 3)
            pp_r = Ring(ph, "pp", [128, 512], F32, 4, psum=True)
            st = post("init", ph)
            gl = list(token_groups())

            def premix_group(gi):
                (b, seg, c0, n) = gl[gi]
                ug, t_ug = ug_r.next()
                tts = [b * 18 + c0 // 128 + ti for ti in range(n // 128)]
                pm.group(tts, [(ug[:, :, ti * 128:(ti + 1) * 128], t_ug) for ti in range(n // 128)])
                return ug, t_ug
            nxt = premix_group(0)
            for gi, (b, seg, c0, n) in enumerate(gl):
                g0 = b * TPB + c0
                ug, t_ug = nxt
                if gi + 1 < len(gl):
                    nxt = premix_group(gi + 1)
                for f in range(nf):
                    pp, t_pp = pp_r.next()

                    def mm(e, pp=pp, ug=ug, f=f, n=n):
                        for k in range(8):
                            r_ = e.matmul(pp[:, 0:n], lhsT=W[:, k, f * 128:(f + 1) * 128], rhs=ug[:, k, 0:n], start=(k == 0), stop=(k == 7))
                        return r_
                    S.op("pe", mm, reads=[t_ug, t_w], writes=[t_pp])
                    post("tile", st, f, pp, t_pp, bT, t_w, n, g0)
            S.barrier()

    gluT = None
    convT = None

    def phase_conformer(i):
        gluT_ = dscr(nm("gluT"), [D, NTOK], BF16)
        convT_ = dscr(nm("convT"), [D, NTOK])

        def post(kind, st, *a):
            if kind == "init":
                ph = st
                return dict(a_r=Ring(ph, "ca", [128, 8, 512], F32, 2), sg_r=Ring(ph, "csg", [128, 512], F32, 2),
                            gl_r=Ring(ph, "cgl", [128, 512], BF16, 2), cur=[None, None])
            f, pp, t_pp, bT, t_w, n, g0 = a
            if f < 8:
                if f == 0:
                    st["cur"] = st["a_r"].next()
                at, t_a = st["cur"]
                S.op("act", lambda e, at=at, pp=pp, f=f, n=n: e.activation(out=at[:, f, 0:n], in_=pp[:, 0:n], func=AF.Identity, bias=bT[:, f:f + 1]), reads=[t_pp, t_w], writes=[t_a])
            else:
                at, t_a = st["cur"]
                sg, t_sg = st["sg_r"].next()
                gl, t_gl = st["gl_r"].next()
                fa = f - 8
                S.op("act", lambda e, sg=sg, pp=pp, f=f, n=n: e.activation(out=sg[:, 0:n], in_=pp[:, 0:n], func=AF.Sigmoid, bias=bT[:, f:f + 1]), reads=[t_pp, t_w], writes=[t_sg])
                S.op("dve", lambda e, gl=gl, at=at, sg=sg, fa=fa, n=n: e.tensor_mul(out=gl[:, 0:n], in0=at[:, fa, 0:n], in1=sg[:, 0:n]), reads=[t_a, t_sg], writes=[t_gl])
                S.dma("pool", lambda e, gl=gl, fa=fa, g0=g0, n=n: e.dma_start(out=gluT_[fa * 128:(fa + 1) * 128, g0:g0 + n], in_=gl[:, 0:n]), reads=[t_gl])
        phase_proj_fm(i, cv_w_pw1[0], 16, cv_b_pw1T, post)

        with ExitStack() as ph:
            wdw = sb(ph, nm("wdw"), [128, 8, 31], F32)
            bdw = sb(ph, nm("bdw"), [128, 8], F32)
            identf = sb(ph, nm("identf"), [128, 128], F32)
            t_w = T()
            S.dma("sp", lambda e: e.dma_start(out=wdw[:], in_=cv_w_dwT[:, :, :]), writes=[t_w])
            S.dma("sp", lambda e: e.dma_start(out=bdw[:], in_=cv_b_dwT[:, :]), writes=[t_w])
            S.dma("sp", lambda e: e.dma_start(out=identf[:], in_=k_ident[:, :]), writes=[t_w])
            dg_r = Ring(ph, "dg", [128, 31, 128], BF16, 2)
            rings = {}
            for n in (S_LAT, S_CTX):
                bufs = Ring(ph, "cbuf%d" % n, [128, n + 32], BF16, 3)
                for (bt, t_b) in bufs.bufs:
                    S.op("pool", lambda e, bt=bt: e.memset(bt[:, 0:15], 0.0), writes=[t_b])
                    S.op("pool", lambda e, bt=bt, n=n: e.memset(bt[:, n + 15:n + 32], 0.0), writes=[t_b])
                rings[n] = bufs
            pc_r = Ring(ph, "cps", [128, 512], F32, 4, psum=True)
            ac_r = Ring(ph, "cacc", [128, 512], F32, 4)
            for f in range(8):
                dg, t_dg = dg_r.next()
                for jt in range(31):
                    S.op("dve", lambda e, dg=dg, f=f, jt=jt: e.tensor_scalar_mul(out=dg[:, jt, :], in0=identf[:], scalar1=wdw[:, f, jt:jt + 1]), reads=[t_w], writes=[t_dg])
                for b in range(NB):
                    for (c0, n) in ((0, S_LAT), (S_LAT, S_CTX)):
                        g0 = b * TPB + c0
                        bt, t_b = rings[n].next()
                        S.dma("sp", lambda e, bt=bt, f=f, g0=g0, n=n: e.dma_start(out=bt[:, 15:15 + n], in_=gluT_[f * 128:(f + 1) * 128, g0:g0 + n]), writes=[t_b])
                        cw = min(512, n)
                        for tc in range(n // cw):
                            pc, t_pc = pc_r.next()
                            acc, t_acc = ac_r.next()

                            def mmc(e, pc=pc, bt=bt, dg=dg, tc=tc, cw=cw):
                                for jt in range(31):
                                    r_ = e.matmul(pc[:, 0:cw], lhsT=dg[:, jt, :], rhs=bt[:, tc * cw + jt:tc * cw + jt + cw], start=(jt == 0), stop=(jt == 30))
                                return r_
                            S.op("pe", mmc, reads=[t_b, t_dg], writes=[t_pc])
                            S.op("act", lambda e, acc=acc, pc=pc, f=f, cw=cw: e.activation(out=acc[:, 0:cw], in_=pc[:, 0:cw], func=AF.Identity, bias=bdw[:, f:f + 1]), reads=[t_pc, t_w], writes=[t_acc])
                            S.dma("pool", lambda e, acc=acc, f=f, g0=g0, tc=tc, cw=cw: e.dma_start(out=convT_[f * 128:(f + 1) * 128, g0 + tc * cw:g0 + (tc + 1) * cw], in_=acc[:, 0:cw]), reads=[t_acc])
            S.barrier()

        with ExitStack() as ph:
            lg = sb(ph, nm("lg"), [128, 8], F32)
            lb = sb(ph, nm("lb"), [128, 8], F32)
            onesb = sb(ph, nm("onesb"), [128, 128], BF16)
            epsl = sb(ph, nm("epsl"), [128, 1], F32)
            t_w = T()
            S.dma("sp", lambda e: e.dma_start(out=lg[:], in_=cv_ln_gT[:, :]), writes=[t_w])
            S.dma("sp", lambda e: e.dma_start(out=lb[:], in_=cv_ln_bT[:, :]), writes=[t_w])
            S.op("pool", lambda e: e.memset(onesb[:], 1.0), writes=[t_w])
            S.op("pool", lambda e: e.memset(epsl[:], LN_EPS), writes=[t_w])
            y_r = Ring(ph, "ly", [128, 8, 512], F32, 2)
            yb_r = Ring(ph, "lyb", [128, 8, 512], BF16, 2)
            sq_r = Ring(ph, "lsq", [128, 8, 512], BF16, 2)
            p_r = Ring(ph, "lps", [128, 2, 512], F32, 2, psum=True)
            mean_r = Ring(ph, "lmean", [128, 512], F32, 2)
            rstd_r = Ring(ph, "lrstd", [128, 512], F32, 2)
            msq_r = Ring(ph, "lmsq", [128, 512], F32, 2)
            zc_r = Ring(ph, "lzc", [128, 512], F32, 3)
            mo_r = Ring(ph, "lmo", [128, 512], BF16, 3)
            gl = list(token_groups())

            def stA(gi):
                (b, seg, c0, n) = gl[gi]
                g0 = b * TPB + c0
                y, t_y = y_r.next()
                yb, t_yb = yb_r.next()
                sq, t_sq = sq_r.next()
                pst, t_ps = p_r.next()
                mean, t_mean = mean_r.next()
                rstd, t_rstd = rstd_r.next()
                msq, t_msq = msq_r.next()
                S.dma("sp", lambda e: e.dma_start(out=y[:, :, 0:n], in_=convT_[:, g0:g0 + n].rearrange("(k p) t -> p k t", p=128)), writes=[t_y])
                S.op("act", lambda e: e.copy(out=yb[:, :, 0:n], in_=y[:, :, 0:n]), reads=[t_y], writes=[t_yb])
                S.op("act", lambda e: e.activation(out=sq[:, :, 0:n], in_=y[:, :, 0:n], func=AF.Square), reads=[t_y], writes=[t_sq])

                def mms(e):
                    for k in range(8):
                        e.matmul(pst[:, 0, 0:n], lhsT=onesb[:], rhs=yb[:, k, 0:n], start=(k == 0), stop=(k == 7))
                    for k in range(8):
                        r_ = e.matmul(pst[:, 1, 0:n], lhsT=onesb[:], rhs=sq[:, k, 0:n], start=(k == 0), stop=(k == 7))
                    return r_
                S.op("pe", mms, reads=[t_yb, t_sq, t_w], writes=[t_ps])
                S.op("act", lambda e: e.mul(out=mean[:, 0:n], in_=pst[:, 0, 0:n], mul=1.0 / D), reads=[t_ps], writes=[t_mean])
                S.op("dve", lambda e: e.tensor_mul(out=msq[:, 0:n], in0=mean[:, 0:n], in1=mean[:, 0:n]), reads=[t_mean], writes=[t_msq])
                S.op("dve", lambda e: e.scalar_tensor_tensor(out=rstd[:, 0:n], in0=pst[:, 1, 0:n], scalar=1.0 / D, in1=msq[:, 0:n], op0=ALU.mult, op1=ALU.subtract), reads=[t_ps, t_msq], writes=[t_rstd])
                S.op("act", lambda e: e.activation(out=rstd[:, 0:n], in_=rstd[:, 0:n], func=AF.Ln, bias=epsl[:]), reads=[t_rstd, t_w], writes=[t_rstd])
                S.op("act", lambda e: e.activation(out=rstd[:, 0:n], in_=rstd[:, 0:n], func=AF.Exp, scale=-0.5), reads=[t_rstd], writes=[t_rstd])
                return dict(y=y, t_y=t_y, mean=mean, t_mean=t_mean, rstd=rstd, t_rstd=t_rstd, n=n, g0=g0)

            def stB(st):
                y, t_y, mean, t_mean, rstd, t_rstd, n, g0 = (st[k_] for k_ in ("y", "t_y", "mean", "t_mean", "rstd", "t_rstd", "n", "g0"))
                for k in range(8):
                    zc, t_zc = zc_r.next()
                    mo, t_mo = mo_r.next()
                    S.op("dve", lambda e, zc=zc, k=k: e.tensor_sub(out=zc[:, 0:n], in0=y[:, k, 0:n], in1=mean[:, 0:n]), reads=[t_y, t_mean], writes=[t_zc])
                    S.op("dve", lambda e, zc=zc: e.tensor_mul(out=zc[:, 0:n], in0=zc[:, 0:n], in1=rstd[:, 0:n]), reads=[t_zc, t_rstd], writes=[t_zc])
                    S.op("act", lambda e, mo=mo, zc=zc, k=k: e.activation(out=mo[:, 0:n], in_=zc[:, 0:n], func=AF.Silu, scale=lg[:, k:k + 1], bias=lb[:, k:k + 1]), reads=[t_zc, t_w], writes=[t_mo])
                    S.dma("pool", lambda e, mo=mo, k=k: e.dma_start(out=mixT[k * 128:(k + 1) * 128, g0:g0 + n], in_=mo[:, 0:n]), reads=[t_mo])
            prev = None
            for gi in range(len(gl)):
                cur = stA(gi)
                if prev is not None:
                    stB(prev)
                prev = cur
            stB(prev)
            S.barrier()

    TWO_PI = 2.0 * math.pi
    MAGIC = 12582912.0

    def phase_hy_filters(n, hsd):
        nt = n // 128
        cw = min(512, n)
        ncw = n // cw
        with ExitStack() as ph:
            feat = sb(ph, nm("feat"), [33, n], F32)
            w1 = sb(ph, nm("fw1"), [33, 64], F32)
            w2 = sb(ph, nm("fw2"), [64, 2, 64], F32)
            wo = sb(ph, nm("fwo"), [64, 2 * D], F32)
            fb1 = sb(ph, nm("fb1"), [64, 1], F32)
            fb2 = sb(ph, nm("fb2"), [64, 2], F32)
            ffr = sb(ph, nm("ffr"), [64, 1], F32)
            hdn = [sb(ph, nm("hdn"), [64, n], F32) for _ in range(2)]
            t_hdn = [T(), T()]
            t_c = T()
            S.dma("sp", lambda e: e.dma_start(out=feat[:], in_=k_feat[n][:, :]), writes=[t_c])
            S.dma("sp", lambda e: e.dma_start(out=w1[:], in_=hy_filt_w1[0, :, :]), writes=[t_c])
            for l_ in range(2):
                S.dma("sp", lambda e, l_=l_: e.dma_start(out=w2[:, l_, :], in_=hy_filt_w2[0, l_, :, :]), writes=[t_c])
            S.dma("sp", lambda e: e.dma_start(out=wo[:], in_=hy_filt_w_out[0, :, :]), writes=[t_c])
            S.dma("sp", lambda e: e.dma_start(out=fb1[:], in_=hy_fb1T[:, :]), writes=[t_c])
            S.dma("sp", lambda e: e.dma_start(out=fb2[:], in_=hy_fb2T[:, :]), writes=[t_c])
            S.dma("sp", lambda e: e.dma_start(out=ffr[:], in_=hy_ffreqT[:, :]), writes=[t_c])
            pp_r = Ring(ph, "fpp", [128, 512], F32, 2, psum=True)
            arg_r = Ring(ph, "farg", [64, 512], F32, 2)
            tq_r = Ring(ph, "ftq", [64, 512], F32, 2)
            for l_ in range(3):
                dst, t_dst = hdn[l_ % 2], t_hdn[l_ % 2]
                for c in range(ncw):
                    pp, t_pp = pp_r.next()
                    arg, t_arg = arg_r.next()
                    tq, t_tq = tq_r.next()
                    if l_ == 0:
                        S.op("pe", lambda e, pp=pp, c=c: e.matmul(pp[0:64, 0:cw], lhsT=w1[0:33, :], rhs=feat[0:33, c * cw:(c + 1) * cw], start=True, stop=True), reads=[t_c], writes=[t_pp])
                        bcol = fb1[:, 0:1]
                    else:
                        src, t_src = hdn[(l_ - 1) % 2], t_hdn[(l_ - 1) % 2]
                        S.op("pe", lambda e, pp=pp, c=c, src=src, l_=l_: e.matmul(pp[0:64, 0:cw], lhsT=w2[:, l_ - 1, :], rhs=src[:, c * cw:(c + 1) * cw], start=True, stop=True), reads=[t_c, t_src], writes=[t_pp])
                        bcol = fb2[:, l_ - 1:l_]
                    S.op("dve", lambda e, arg=arg, pp=pp, bcol=bcol: e.tensor_scalar(out=arg[:, 0:cw], in0=pp[0:64, 0:cw], scalar1=bcol, scalar2=ffr[:, 0:1], op0=ALU.add, op1=ALU.mult), reads=[t_pp, t_c], writes=[t_arg])
                    S.op("dve", lambda e, tq=tq, arg=arg: e.tensor_scalar(out=tq[:, 0:cw], in0=arg[:, 0:cw], scalar1=1.0 / TWO_PI, scalar2=MAGIC, op0=ALU.mult, op1=ALU.add), reads=[t_arg], writes=[t_tq])
                    S.op("dve", lambda e, tq=tq: e.tensor_scalar_add(out=tq[:, 0:cw], in0=tq[:, 0:cw], scalar1=-MAGIC), reads=[t_tq], writes=[t_tq])
                    S.op("dve", lambda e, tq=tq, arg=arg: e.scalar_tensor_tensor(out=arg[:, 0:cw], in0=tq[:, 0:cw], scalar=-TWO_PI, in1=arg[:, 0:cw], op0=ALU.mult, op1=ALU.add), reads=[t_tq, t_arg], writes=[t_arg])
                    S.op("dve", lambda e, arg=arg: e.tensor_scalar(out=arg[:, 0:cw], in0=arg[:, 0:cw], scalar1=-3.14159, scalar2=3.14159, op0=ALU.max, op1=ALU.min), reads=[t_arg], writes=[t_arg])
                    S.op("act", lambda e, dst=dst, arg=arg, c=c: e.activation(out=dst[:, c * cw:(c + 1) * cw], in_=arg[:, 0:cw], func=AF.Sin), reads=[t_arg], writes=[t_dst])
            fin, t_fin = hdn[0], t_hdn[0]
            dk_r = Ring(ph, "fdk", [128, D], F32, 2)
            hf_r = Ring(ph, "fhf", [128, 2, D], F32, 2)
            hs_r = Ring(ph, "fhs", [128, 2, D], BF16, 2)
            for tt in range(nt):
                dk, t_dk = dk_r.next()
                hf, t_hf = hf_r.next()
                hs, t_hs = hs_r.next()
                S.dma("sp", lambda e, dk=dk, tt=tt: e.dma_start(out=dk[:], in_=k_decay[n][tt * 128:(tt + 1) * 128, :]), writes=[t_dk])
                for cc in range(4):
                    pp, t_pp = pp_r.next()
                    S.op("pe", lambda e, pp=pp, tt=tt, cc=cc: e.matmul(pp[:, :], lhsT=fin[:, tt * 128:(tt + 1) * 128], rhs=wo[:, cc * 512:(cc + 1) * 512], start=True, stop=True), reads=[t_fin, t_c], writes=[t_pp])
                    S.op("dve", lambda e, pp=pp, hf=hf, dk=dk, cc=cc: e.tensor_mul(out=hf[:, cc // 2, (cc % 2) * 512:(cc % 2 + 1) * 512], in0=pp[:, :], in1=dk[:, (cc % 2) * 512:(cc % 2 + 1) * 512]), reads=[t_pp, t_dk], writes=[t_hf])
                if tt == 0:
                    S.op("dve", lambda e, hf=hf: e.memset(hf[0:1, 1, :], 0.0), reads=[t_hf], writes=[t_hf])
                S.op("dve", lambda e, hf=hf, hs=hs: e.tensor_add(out=hs[:, 0, :], in0=hf[:, 0, :], in1=hf[:, 1, :]), reads=[t_hf], writes=[t_hs])
                S.op("dve", lambda e, hf=hf, hs=hs: e.tensor_sub(out=hs[:, 1, :], in0=hf[:, 0, :], in1=hf[:, 1, :]), reads=[t_hf], writes=[t_hs])
                for q in range(2):
                    S.dma("pool", lambda e, hs=hs, q=q, tt=tt: e.dma_start(out=hsd[q, tt * 128:(tt + 1) * 128, :], in_=hs[:, q, :]), reads=[t_hs])
            S.barrier()

    def load_tab(tab, t_tab, src, nt):
        for a in range(nt):
            S.dma("sp" if a % 2 == 0 else "pool", lambda e, a=a: e.dma_start(out=tab[:, a, :], in_=src[a * 128:(a + 1) * 128, :]), writes=[t_tab])

    def phase_hy_khat(n, hsd, KAB):
        nt = n // 128
        with ExitStack() as ph:
            tab = sb(ph, nm("ktab"), [128, nt, n], BF16)
            dat = sb(ph, nm("kdat"), [128, nt, D], BF16)
            t_tab, t_dat = T(), T()
            pp_r = Ring(ph, "kpp", [128, 512], F32, 2, psum=True)
            o_r = Ring(ph, "ko", [128, 512], F32, 3)
            for q in range(2):
                load_tab(tab, t_tab, k_dft[n][q], nt)
                S.dma("sp", lambda e, q=q: e.dma_start(out=dat[:], in_=hsd[q, :, :].rearrange("(a p) c -> p a c", p=128)), writes=[t_dat])
                for ft in range(nt):
                    for cc in range(2):
                        pp, t_pp = pp_r.next()
                        ot, t_o = o_r.next()

                        def mm(e, pp=pp, ft=ft, cc=cc):
                            for a in range(nt):
                                r_ = e.matmul(pp[:, :], lhsT=tab[:, a, ft * 128:(ft + 1) * 128], rhs=dat[:, a, cc * 512:(cc + 1) * 512], start=(a == 0), stop=(a == nt - 1))
                            return r_
                        S.op("pe", mm, reads=[t_tab, t_dat], writes=[t_pp])
                        S.op("act", lambda e, ot=ot, pp=pp: e.copy(out=ot[:], in_=pp[:, :]), reads=[t_pp], writes=[t_o])
                        S.dma("pool", lambda e, ot=ot, q=q, ft=ft, cc=cc: e.dma_start(out=KAB[q, ft * 128:(ft + 1) * 128, cc * 512:(cc + 1) * 512], in_=ot[:]), reads=[t_o])
            S.barrier()

    def phase_hy_conv3(zT_, x0T_, v1T_, v1tok):
        with ExitStack() as ph:
            ident, t_id, epst = load_consts(ph)
            identf = sb(ph, nm("identf2"), [128, 128], F32)
            ws = sb(ph, nm("hws"), [128, 24, 3], F32)
            bs = sb(ph, nm("hbs"), [128, 24], F32)
            dg = sb(ph, nm("hdg"), [128, 24, 3, 128], BF16)
            t_w = T()
            S.dma("sp", lambda e: e.dma_start(out=ws[:], in_=hy_w_shortT[:, :, :]), writes=[t_w])
            S.dma("sp", lambda e: e.dma_start(out=bs[:], in_=hy_b_shortT[:, :]), writes=[t_w])
            S.dma("sp", lambda e: e.dma_start(out=identf[:], in_=k_ident[:, :]), writes=[t_w])
            for fi in range(24):
                for jt in range(3):
                    S.op("dve", lambda e, fi=fi, jt=jt: e.tensor_scalar_mul(out=dg[:, fi, jt, :], in0=identf[:], scalar1=ws[:, fi, jt:jt + 1]), reads=[t_w], writes=[t_w])
            rings = {}
            for n in (S_LAT, S_CTX):
                bufs = Ring(ph, "hbuf%d" % n, [128, n + 2], BF16, 6)
                for (bt, t_b) in bufs.bufs:
                    S.op("pool", lambda e, bt=bt: e.memset(bt[:, 0:1], 0.0), writes=[t_b])
                    S.op("pool", lambda e, bt=bt, n=n: e.memset(bt[:, n + 1:n + 2], 0.0), writes=[t_b])
                rings[n] = (bufs, Ring(ph, "hacc%d" % n, [128, n], F32, 4), Ring(ph, "hv1_%d" % n, [128, n], F32, 2), Ring(ph, "hv1b_%d" % n, [128, n], BF16, 2))
            pc_r = Ring(ph, "hcps", [128, 512], F32, 4, psum=True)
            ptr_r = Ring(ph, "hptr", [128, 8, 128], BF16, 2, psum=True)
            vt_r = Ring(ph, "hvt", [128, 8, 128], BF16, 2)
            for f in range(8):
                for b in range(NB):
                    for (c0, n) in ((0, S_LAT), (S_LAT, S_CTX)):
                        g0 = b * TPB + c0
                        cw = min(512, n)
                        accs = []
                        for part in range(3):
                            fi = part * 8 + f
                            bt, t_b = rings[n][0].next()
                            acc, t_acc = rings[n][1].next()
                            S.dma("sp", lambda e, bt=bt, fi=fi, g0=g0, n=n: e.dma_start(out=bt[:, 1:1 + n], in_=zT_[fi * 128:(fi + 1) * 128, g0:g0 + n]), writes=[t_b])
                            for tc in range(n // cw):
                                pc, t_pc = pc_r.next()

                                def mmc(e, pc=pc, bt=bt, fi=fi, tc=tc, cw=cw):
                                    for jt in range(3):
                                        r_ = e.matmul(pc[:, 0:cw], lhsT=dg[:, fi, jt, :], rhs=bt[:, tc * cw + jt:tc * cw + jt + cw], start=(jt == 0), stop=(jt == 2))
                                    return r_
                                S.op("pe", mmc, reads=[t_b, t_w], writes=[t_pc])
                                S.op("act", lambda e, acc=acc, pc=pc, fi=fi, tc=tc, cw=cw: e.activation(out=acc[:, tc * cw:(tc + 1) * cw], in_=pc[:, 0:cw], func=AF.Identity, bias=bs[:, fi:fi + 1]), reads=[t_pc, t_w], writes=[t_acc])
                            accs.append((acc, t_acc))
                        (x0c, t_x0), (x1c, t_x1), (vc, t_vc) = accs
                        v1, t_v1 = rings[n][2].next()
                        v1b, t_v1b = rings[n][3].next()
                        S.dma("pool", lambda e, x0c=x0c, f=f, g0=g0, n=n: e.dma_start(out=x0T_[f * 128:(f + 1) * 128, g0:g0 + n], in_=x0c[:]), reads=[t_x0])
                        S.op("dve", lambda e, v1=v1, vc=vc, x1c=x1c: e.tensor_mul(out=v1[:], in0=vc[:], in1=x1c[:]), reads=[t_vc, t_x1], writes=[t_v1])
                        S.op("dve", lambda e, v1b=v1b, v1=v1: e.tensor_copy(out=v1b[:], in_=v1[:]), reads=[t_v1], writes=[t_v1b])
                        S.dma("pool", lambda e, v1=v1, f=f, g0=g0, n=n: e.dma_start(out=v1T_[f * 128:(f + 1) * 128, g0:g0 + n], in_=v1[:]), reads=[t_v1])
                        for t0 in range(0, n // 128, 8):
                            na = min(8, n // 128 - t0)
                            ptr, t_ptr = ptr_r.next()
                            vt, t_vt = vt_r.next()

                            def tr(e, ptr=ptr, v1b=v1b, t0=t0, na=na):
                                for a_ in range(na):
                                    r_ = e.transpose(out=ptr[:, a_, :], in_=v1b[:, (t0 + a_) * 128:(t0 + a_ + 1) * 128], identity=ident[:])
                                return r_
                            S.op("pe", tr, reads=[t_v1b, t_id], writes=[t_ptr])
                            S.op("dve", lambda e, vt=vt, ptr=ptr, na=na: e.tensor_copy(out=vt[:, 0:na, :], in_=ptr[:, 0:na, :]), reads=[t_ptr], writes=[t_vt])
                            r0 = g0 + t0 * 128
                            S.dma("pool", lambda e, vt=vt, r0=r0, na=na, f=f: e.dma_start(out=v1tok[r0:r0 + na * 128, f * 128:(f + 1) * 128].rearrange("(a p) c -> p a c", p=128), in_=vt[:, 0:na, :]), reads=[t_vt])
            S.barrier()

    def phase_hy_fwd(n, c0, KAB, Yh, v1tok):
        nt = n // 128
        with ExitStack() as ph:
            Ct = sb(ph, nm("Ct"), [128, nt, n], BF16)
            St = sb(ph, nm("St"), [128, nt, n], BF16)
            t_tab = T()
            load_tab(Ct, t_tab, k_dft[n][0], nt)
            load_tab(St, t_tab, k_dft[n][1], nt)
            v_r = Ring(ph, "fv", [128, nt, 512], BF16, 2)
            pp_r = Ring(ph, "fpp", [128, 2, 512], F32, 2, psum=True)
            k_r = Ring(ph, "fk", [128, 2, 512], F32, 2)
            m_r = Ring(ph, "fm", [128, 4, 512], F32, 2)
            y_r = Ring(ph, "fy", [128, 2, 512], BF16, 2)
            for b in range(NB):
                for cc in range(2):
                    v, t_v = v_r.next()
                    r0 = b * TPB + c0
                    S.dma("sp", lambda e, v=v, r0=r0, cc=cc: e.dma_start(out=v[:], in_=v1tok[r0:r0 + n, cc * 512:(cc + 1) * 512].rearrange("(a p) c -> p a c", p=128)), writes=[t_v])
                    for ft in range(nt):
                        pp, t_pp = pp_r.next()
                        kk, t_kk = k_r.next()
                        mm_, t_mm = m_r.next()
                        yy, t_yy = y_r.next()
                        for q in range(2):
                            S.dma("sp", lambda e, kk=kk, q=q, ft=ft, cc=cc: e.dma_start(out=kk[:, q, :], in_=KAB[q, ft * 128:(ft + 1) * 128, cc * 512:(cc + 1) * 512]), writes=[t_kk])

                        def mm(e, pp=pp, v=v, ft=ft):
                            for q, tab in enumerate((Ct, St)):
                                for a in range(nt):
                                    r_ = e.matmul(pp[:, q, :], lhsT=tab[:, a, ft * 128:(ft + 1) * 128], rhs=v[:, a, :], start=(a == 0), stop=(a == nt - 1))
                            return r_
                        S.op("pe", mm, reads=[t_tab, t_v], writes=[t_pp])
                        for mi, (pq, kq) in enumerate(((0, 0), (1, 1), (0, 1), (1, 0))):
                            S.op("dve", lambda e, mm_=mm_, pp=pp, kk=kk, mi=mi, pq=pq, kq=kq: e.tensor_mul(out=mm_[:, mi, :], in0=pp[:, pq, :], in1=kk[:, kq, :]), reads=[t_pp, t_kk], writes=[t_mm])
                        S.op("pool", lambda e, yy=yy, mm_=mm_: e.tensor_tensor(out=yy[:, 0, :], in0=mm_[:, 0, :], in1=mm_[:, 1, :], op=ALU.subtract), reads=[t_mm], writes=[t_yy])
                        S.op("pool", lambda e, yy=yy, mm_=mm_: e.tensor_tensor(out=yy[:, 1, :], in0=mm_[:, 2, :], in1=mm_[:, 3, :], op=ALU.add), reads=[t_mm], writes=[t_yy])
                        for q in range(2):
                            S.dma("pool", lambda e, yy=yy, q=q, b=b, ft=ft, cc=cc: e.dma_start(out=Yh[q, b, ft * 128:(ft + 1) * 128, cc * 512:(cc + 1) * 512], in_=yy[:, q, :]), reads=[t_yy])
            S.barrier()

    def phase_hy_inv(n, c0, Yh, x0T_, v1T_):
        nt = n // 128
        cw = min(512, n)
        with ExitStack() as ph:
            Ci = sb(ph, nm("Ci"), [128, nt, n], BF16)
            Si = sb(ph, nm("Si"), [128, nt, n], BF16)
            t_tab = T()
            load_tab(Ci, t_tab, k_dft[n][2], nt)
            load_tab(Si, t_tab, k_dft[n][3], nt)
            dbias = sb(ph, nm("dbias"), [128, 8], F32)
            S.dma("sp", lambda e: e.dma_start(out=dbias[:], in_=hy_biasT[:, :]), writes=[t_tab])
            y_r = Ring(ph, "iy", [128, 2, nt, 128], BF16, 2)
            pp_r = Ring(ph, "ipp", [128, 512], F32, 2, psum=True)
            vx_r = Ring(ph, "ivx", [128, 2, 512], F32, 2)
            t_r = Ring(ph, "it", [128, 512], F32, 2)
            mo_r = Ring(ph, "imo", [128, 512], BF16, 2)
            for b in range(NB):
                for ct in range(8):
                    yy, t_yy = y_r.next()
                    for q in range(2):
                        S.dma("sp", lambda e, yy=yy, q=q, b=b, ct=ct: e.dma_start(out=yy[:, q, :, :], in_=Yh[q, b, :, ct * 128:(ct + 1) * 128].rearrange("(a p) c -> p a c", p=128)), writes=[t_yy])
                    for tc in range(n // cw):
                        g0 = b * TPB + c0 + tc * cw
                        pp, t_pp = pp_r.next()
                        vx, t_vx = vx_r.next()
                        tq, t_tq = t_r.next()
                        mo, t_mo = mo_r.next()
                        S.dma("sp", lambda e, vx=vx, ct=ct, g0=g0: e.dma_start(out=vx[:, 0, 0:cw], in_=v1T_[ct * 128:(ct + 1) * 128, g0:g0 + cw]), writes=[t_vx])
                        S.dma("sp", lambda e, vx=vx, ct=ct, g0=g0: e.dma_start(out=vx[:, 1, 0:cw], in_=x0T_[ct * 128:(ct + 1) * 128, g0:g0 + cw]), writes=[t_vx])

                        def mm(e, pp=pp, yy=yy, tc=tc):
                            k_ = 0
                            for q, tab in enumerate((Ci, Si)):
                                for a in range(nt):
                                    r_ = e.matmul(pp[:, 0:cw], lhsT=yy[:, q, a, :], rhs=tab[:, a, tc * cw:(tc + 1) * cw], start=(k_ == 0), stop=(k_ == 2 * nt - 1))
                                    k_ += 1
                            return r_
                        S.op("pe", mm, reads=[t_tab, t_yy], writes=[t_pp])
                        S.op("dve", lambda e, tq=tq, vx=vx, pp=pp, ct=ct: e.scalar_tensor_tensor(out=tq[:, 0:cw], in0=vx[:, 0, 0:cw], scalar=dbias[:, ct:ct + 1], in1=pp[:, 0:cw], op0=ALU.mult, op1=ALU.add), reads=[t_vx, t_pp, t_tab], writes=[t_tq])
                        S.op("dve", lambda e, mo=mo, tq=tq, vx=vx: e.tensor_mul(out=mo[:, 0:cw], in0=tq[:, 0:cw], in1=vx[:, 1, 0:cw]), reads=[t_tq, t_vx], writes=[t_mo])
                        S.dma("pool", lambda e, mo=mo, ct=ct, g0=g0: e.dma_start(out=mixT[ct * 128:(ct + 1) * 128, g0:g0 + cw], in_=mo[:, 0:cw]), reads=[t_mo])
            S.barrier()

    def phase_hyena(i):
        zT_ = dscr(nm("zT"), [3 * D, NTOK], BF16)
        x0T_ = dscr(nm("x0T"), [D, NTOK])
        v1T_ = dscr(nm("v1T"), [D, NTOK])
        v1tok = dscr(nm("v1tok"), [NTOK, D], BF16)
        hsd = {n: dscr(nm("hsd"), [2, n, D], BF16) for n in (S_LAT, S_CTX)}
        KAB = {n: dscr(nm("KAB"), [2, n, D]) for n in (S_LAT, S_CTX)}
        Yh = {n: dscr(nm("Yh"), [2, NB, n, D], BF16) for n in (S_LAT, S_CTX)}
        for n in (S_LAT, S_CTX):
            phase_hy_filters(n, hsd[n])
            phase_hy_khat(n, hsd[n], KAB[n])

        def post(kind, st, *a):
            if kind == "init":
                return dict(z_r=Ring(st, "hz", [128, 512], BF16, 3))
            f, pp, t_pp, bT, t_w, n, g0 = a
            zt, t_z = st["z_r"].next()
            S.op("act", lambda e, zt=zt, pp=pp, f=f, n=n: e.activation(out=zt[:, 0:n], in_=pp[:, 0:n], func=AF.Identity, bias=bT[:, f:f + 1]), reads=[t_pp, t_w], writes=[t_z])
            S.dma("pool", lambda e, zt=zt, f=f, g0=g0, n=n: e.dma_start(out=zT_[f * 128:(f + 1) * 128, g0:g0 + n], in_=zt[:, 0:n]), reads=[t_z])
        phase_proj_fm(i, hy_w_in[0], 24, hy_b_inT, post)
        phase_hy_conv3(zT_, x0T_, v1T_, v1tok)
        for (n, c0) in ((S_LAT, 0), (S_CTX, S_LAT)):
            phase_hy_fwd(n, c0, KAB[n], Yh[n], v1tok)
            phase_hy_inv(n, c0, Yh[n], x0T_, v1T_)

    g.hooks = dict(phase_mod=phase_mod, phase_attn=phase_attn, phase_p3a=phase_p3a, phase_p3b=phase_p3b)
    lay = list(layers) if layers is not None else list(range(n_layers))
    g.first = lay[0]
    for i in lay:
        last = (i == lay[-1])
        kind, j = i % 3, i // 3
        phase_mod(i)
        if kind == 0:
            phase_attn(i, j, ctx_q=not last)
            phase_p3a(i, attn_w_out[j], None, skip_ctx=last)
        elif kind == 1:
            phase_hyena(i)
            phase_p3a(i, hy_w_out[0], hy_b_out[0, :], skip_ctx=last)
        else:
            phase_conformer(i)
            phase_p3a(i, cv_w_pw2[0], cv_b_pw2[0, :], skip_ctx=last)
        phase_p3b(i, last)
    S.emit()
    return nc


_CACHE = {}


def _consts():
    if "c" in _CACHE:
        return _CACHE["c"]
    cos, sin, rmT = _rope_tables()
    c = {"k_cos": cos, "k_sin": sin, "k_rm": rmT, "k_ident": np.eye(128, dtype=np.float32)}
    for n in (S_LAT, S_CTX):
        tabs = _dft_tables(n)
        for q in range(4):
            c["k_dft%d_%d" % (n, q)] = tabs[q].astype(ml_dtypes.bfloat16)
        zt, decay = _hyena_feats(n)
        c["k_feat%d" % n] = zt
        c["k_decay%d" % n] = decay
    _CACHE["c"] = c
    return c


def kernel(n_layers=DEPTH, cores=NCORES, trace=False, layers=None, h_init=None, **inputs):
    f = lambda a: np.ascontiguousarray(np.asarray(a, dtype=np.float32))
    x = f(inputs["x"])
    c = f(inputs["c"])
    ctx = f(inputs["ctx"])
    c_ctx = f(inputs["c_ctx"])
    nc = build(n_layers, layers)
    consts = _consts()
    shared = dict(consts)
    for k_ in ("w_mod", "b_mod", "norm_mix_pre", "norm_mix_post", "norm_mlp_pre", "norm_mlp_post", "w_mlp_in", "w_mlp_out",
               "attn_w_qkv", "attn_w_out", "attn_subln", "hy_w_in", "hy_b_in", "hy_w_short", "hy_b_short", "hy_filt_w1",
               "hy_filt_b1", "hy_filt_w2", "hy_filt_b2", "hy_filt_freq", "hy_filt_w_out", "hy_bias", "hy_w_out", "hy_b_out",
               "cv_w_pw1", "cv_b_pw1", "cv_w_dw", "cv_b_dw", "cv_ln_g", "cv_ln_b", "cv_w_pw2", "cv_b_pw2"):
        shared[k_] = f(inputs[k_])
    shared["attn_lambda"] = f(inputs["attn_lambda"]).reshape(2, 256)
    colT = lambda v, nt: np.ascontiguousarray(f(v).reshape(nt, 128).T)
    shared["cv_b_pw1T"] = colT(inputs["cv_b_pw1"][0], 16)
    shared["cv_w_dwT"] = np.ascontiguousarray(f(inputs["cv_w_dw"][0]).reshape(31, 8, 128).transpose(2, 1, 0))
    shared["cv_b_dwT"] = colT(inputs["cv_b_dw"][0], 8)
    shared["cv_ln_gT"] = colT(inputs["cv_ln_g"][0], 8)
    shared["cv_ln_bT"] = colT(inputs["cv_ln_b"][0], 8)
    shared["hy_b_inT"] = colT(inputs["hy_b_in"][0], 24)
    shared["hy_w_shortT"] = np.ascontiguousarray(f(inputs["hy_w_short"][0]).reshape(3, 24, 128).transpose(2, 1, 0))
    shared["hy_b_shortT"] = colT(inputs["hy_b_short"][0], 24)
    shared["hy_biasT"] = colT(inputs["hy_bias"][0], 8)
    shared["hy_fb1T"] = np.ascontiguousarray(f(inputs["hy_filt_b1"][0]).reshape(1, 64).T)
    shared["hy_fb2T"] = np.ascontiguousarray(f(inputs["hy_filt_b2"][0]).reshape(2, 64).T)
    shared["hy_ffreqT"] = np.ascontiguousarray(f(inputs["hy_filt_freq"][0]).reshape(1, 64).T)
    in_maps = []
    for core in range(cores):
        b0 = core * NB
        m = dict(shared)
        m["x2"] = x[b0:b0 + NB].reshape(NB * S_LAT, D)
        m["ctx2"] = ctx[b0:b0 + NB].reshape(NB * S_CTX, D)
        cm = np.stack([c[b0], c[b0 + 1], c_ctx], axis=0)
        m["cT"] = np.ascontiguousarray(cm.reshape(3, 8, 128).transpose(2, 1, 0))
        in_maps.append(m)
    res = run_bass_kernel_spmd(nc, in_maps, core_ids=list(range(cores)), **({'trace': True} if trace else {}))
    _CACHE['res'] = res
    outs = [np.asarray(r["out"]).reshape(NB, S_LAT, D) for r in res.results]
    return np.concatenate(outs, axis=0).astype(np.float32)
```

```python
import math
from contextlib import ExitStack
import numpy as np
import ml_dtypes
import concourse.bass as bass
import concourse.mybir as mybir
from concourse.bass_utils import run_bass_kernel_spmd

F32 = mybir.dt.float32
BF16 = mybir.dt.bfloat16
AF = mybir.ActivationFunctionType
ALU = mybir.AluOpType
AX = mybir.AxisListType

D = 1024
S_LAT = 2048
S_CTX = 256
TPB = S_LAT + S_CTX
NB = 2
NTOK = NB * TPB
DFF = 4096
DEPTH = 4
EPS = 1e-6
LN_EPS = 1e-5
NCORES = 8


class T:
    __slots__ = ("name", "w", "rs")

    def __init__(self, name=""):
        self.name = name
        self.w = None
        self.rs = []


class Op:
    __slots__ = ("eng", "fn", "deps", "sig", "sigidx", "is_dma", "sem_i", "sem_val")


class Sched:
    ENGS = ("pe", "act", "dve", "pool", "sp")
    CENG = ("pe", "act", "dve", "pool")
    RING = 8

    def __init__(self, nc):
        self.nc = nc
        self.streams = {e: [] for e in self.ENGS}
        self.ndma = {e: 0 for e in self.ENGS}
        self.pending = {e: None for e in self.ENGS}
        self.dmas_since = []
        self.last = {e: None for e in self.CENG}

    def _mk(self, eng, fn, is_dma):
        o = Op()
        o.eng = eng
        o.fn = fn
        o.sig = False
        o.sigidx = 0
        o.is_dma = is_dma
        o.deps = []
        return o

    def _deps(self, op, reads, writes):
        deps = []
        for t in reads:
            if t.w is not None:
                deps.append(t.w)
        for t in writes:
            if t.w is not None:
                deps.append(t.w)
            deps.extend(t.rs)
        for t in reads:
            t.rs.append(op)
        for t in writes:
            t.w = op
            t.rs = []
        if self.pending[op.eng] is not None:
            deps.extend(self.pending[op.eng])
            self.pending[op.eng] = None
        out = []
        seen = set()
        for d in deps:
            if d is op or id(d) in seen:
                continue
            seen.add(id(d))
            if (not d.is_dma) and d.eng == op.eng and (not op.is_dma) and d.eng == "pe":
                continue
            out.append(d)
            if not d.is_dma:
                d.sig = True
        op.deps = out

    def op(self, eng, fn, reads=(), writes=()):
        o = self._mk(eng, fn, False)
        self._deps(o, reads, writes)
        self.streams[eng].append(o)
        self.last[eng] = o
        return o

    def dma(self, eng, fn, reads=(), writes=()):
        o = self._mk(eng, fn, True)
        n = self.ndma[eng]
        self.ndma[eng] = n + 1
        o.sem_i = n % self.RING
        o.sem_val = 16 * (n // self.RING + 1)
        self._deps(o, reads, writes)
        self.streams[eng].append(o)
        self.dmas_since.append(o)
        return o

    def barrier(self):
        deps = []
        for e in self.CENG:
            if self.last[e] is not None:
                deps.append(self.last[e])
        latest = {}
        for d in self.dmas_since:
            latest[(d.eng, d.sem_i)] = d
        deps.extend(latest.values())
        self.dmas_since = []
        for e in self.ENGS:
            prev = self.pending[e] or []
            self.pending[e] = list(prev) + list(deps)

    def emit(self):
        nc = self.nc
        with ExitStack() as es:
            csem = {e: es.enter_context(nc.semaphore("c_" + e)) for e in self.CENG}
            dsem = {}
            for e in ("sp", "act", "pool"):
                if self.ndma[e]:
                    dsem[e] = [es.enter_context(nc.semaphore("d_%s%d" % (e, i))) for i in range(self.RING)]
            for e in self.CENG:
                k = 0
                for o in self.streams[e]:
                    if (not o.is_dma) and o.sig:
                        k += 1
                        o.sigidx = k
            block = es.enter_context(nc.Block())

            def run(ename, eng):
                waited = {}
                for o in self.streams[ename]:
                    need = {}
                    for d in o.deps:
                        if d.is_dma:
                            s = dsem[d.eng][d.sem_i]
                            v = d.sem_val
                        else:
                            s = csem[d.eng]
                            v = d.sigidx
                        key = id(s)
                        if key not in need or need[key][1] < v:
                            need[key] = (s, v)
                    if o.is_dma and o.sem_val > 16:
                        s = dsem[ename][o.sem_i]
                        key = id(s)
                        v = o.sem_val - 16
                        if key not in need or need[key][1] < v:
                            need[key] = (s, v)
                    for key, (s, v) in need.items():
                        if waited.get(key, 0) >= v:
                            continue
                        eng.wait_ge(s, v)
                        waited[key] = v
                    inst = o.fn(eng)
                    if o.is_dma:
                        inst.then_inc(dsem[ename][o.sem_i], 16)
                    elif o.sig:
                        inst.then_inc(csem[ename], 1)
                if ename in dsem:
                    n = self.ndma[ename]
                    for i in range(self.RING):
                        cnt = (n - i + self.RING - 1) // self.RING if n > i else 0
                        if cnt > 0:
                            eng.wait_ge(dsem[ename][i], 16 * cnt)

            @block.tensor
            def _(eng):
                run("pe", eng)

            @block.scalar
            def _(eng):
                run("act", eng)

            @block.vector
            def _(eng):
                run("dve", eng)

            @block.gpsimd
            def _(eng):
                run("pool", eng)

            @block.sync
            def _(eng):
                run("sp", eng)


def _rope_tables():
    half = 16
    inv_freq = (np.float32(10000.0) ** (-np.arange(half, dtype=np.float32) / np.float32(half))).astype(np.float32)
    t = np.arange(S_LAT)
    rows = (t // 64).astype(np.float32)
    cols = (t % 64).astype(np.float32)
    cos = np.zeros((128, S_LAT), np.float32)
    sin = np.zeros((128, S_LAT), np.float32)
    for p in range(128):
        d = p % 64
        pos = rows if d < 32 else cols
        ang = (pos * inv_freq[d % 16]).astype(np.float32)
        cos[p] = np.cos(ang)
        sin[p] = np.sin(ang)
    Rm = np.zeros((128, 128), np.float32)
    for base in range(0, 128, 32):
        for j in range(32):
            if j < 16:
                Rm[base + j, base + j + 16] = -1.0
            else:
                Rm[base + j, base + j - 16] = 1.0
    return cos, sin, np.ascontiguousarray(Rm.T)


def _dft_tables(n):
    N = 2 * n
    t = np.arange(n, dtype=np.float64)[:, None]
    f = np.arange(n, dtype=np.float64)[None, :]
    th = 2.0 * np.pi * (f + 0.5) * t / N
    cf = np.cos(th)
    sf = np.sin(th)
    ci = (cf.T * (2.0 / N))
    si = (sf.T * (2.0 / N))
    return cf.astype(np.float32), sf.astype(np.float32), ci.astype(np.float32), si.astype(np.float32)


def _hyena_feats(n):
    bands = 16
    t = np.linspace(0.0, 1.0, n, dtype=np.float32)[:, None]
    w = (np.float32(2.0 * math.pi) * np.arange(n, dtype=np.float32)[:, None] / np.float32(n)).astype(np.float32)
    f = np.linspace(1e-4, bands - 1, bands, dtype=np.float32)[None, :]
    z = np.concatenate([t, np.cos(f * w), -np.sin(f * w)], axis=-1).astype(np.float32)
    deltas = np.abs(np.linspace(math.log(1e-2) / 1.5, math.log(1e-2) / 0.3, D, dtype=np.float32))
    decay = np.exp(-t * deltas[None, :]).astype(np.float32)
    return np.ascontiguousarray(z.T), decay


class B:
    pass


def build(n_layers=DEPTH, layers=None):
    nc = bass.Bass("TRN2", target_bir_lowering=False)
    S = Sched(nc)
    g = B()

    def din(name, shape, dt=F32):
        return nc.dram_tensor(name, list(shape), dt, kind="ExternalInput").ap()

    def dscr(name, shape, dt=F32):
        return nc.dram_tensor(name, list(shape), dt).ap()

    x2 = din("x2", [NB * S_LAT, D])
    ctx2 = din("ctx2", [NB * S_CTX, D])
    cT = din("cT", [128, 8, 3])
    w_mod = din("w_mod", [DEPTH, D, 6 * D])
    b_mod = din("b_mod", [DEPTH, 6 * D])
    n_mix_pre = din("norm_mix_pre", [DEPTH, D])
    n_mix_post = din("norm_mix_post", [DEPTH, D])
    n_mlp_pre = din("norm_mlp_pre", [DEPTH, D])
    n_mlp_post = din("norm_mlp_post", [DEPTH, D])
    w_mlp_in = din("w_mlp_in", [DEPTH, D, DFF])
    w_mlp_out = din("w_mlp_out", [DEPTH, DFF, D])
    attn_w_qkv = din("attn_w_qkv", [2, D, 3 * D])
    attn_w_out = din("attn_w_out", [2, D, D])
    attn_lambda = din("attn_lambda", [2, 256])
    attn_subln = din("attn_subln", [2, 128])
    hy_w_in = din("hy_w_in", [1, D, 3 * D])
    hy_b_in = din("hy_b_in", [1, 3 * D])
    hy_w_short = din("hy_w_short", [1, 3, 3 * D])
    hy_b_short = din("hy_b_short", [1, 3 * D])
    hy_filt_w1 = din("hy_filt_w1", [1, 33, 64])
    hy_filt_b1 = din("hy_filt_b1", [1, 64])
    hy_filt_w2 = din("hy_filt_w2", [1, 2, 64, 64])
    hy_filt_b2 = din("hy_filt_b2", [1, 2, 64])
    hy_filt_freq = din("hy_filt_freq", [1, 64])
    hy_filt_w_out = din("hy_filt_w_out", [1, 64, 2 * D])
    hy_bias = din("hy_bias", [1, D])
    hy_w_out = din("hy_w_out", [1, D, D])
    hy_b_out = din("hy_b_out", [1, D])
    cv_w_pw1 = din("cv_w_pw1", [1, D, 2 * D])
    cv_b_pw1 = din("cv_b_pw1", [1, 2 * D])
    cv_w_dw = din("cv_w_dw", [1, 31, D])
    cv_b_dw = din("cv_b_dw", [1, D])
    cv_ln_g = din("cv_ln_g", [1, D])
    cv_ln_b = din("cv_ln_b", [1, D])
    cv_w_pw2 = din("cv_w_pw2", [1, D, D])
    cv_b_pw2 = din("cv_b_pw2", [1, D])
    cv_b_pw1T = din("cv_b_pw1T", [128, 16])
    cv_w_dwT = din("cv_w_dwT", [128, 8, 31])
    cv_b_dwT = din("cv_b_dwT", [128, 8])
    cv_ln_gT = din("cv_ln_gT", [128, 8])
    cv_ln_bT = din("cv_ln_bT", [128, 8])
    hy_b_inT = din("hy_b_inT", [128, 24])
    hy_w_shortT = din("hy_w_shortT", [128, 24, 3])
    hy_b_shortT = din("hy_b_shortT", [128, 24])
    hy_biasT = din("hy_biasT", [128, 8])
    hy_fb1T = din("hy_fb1T", [64, 1])
    hy_fb2T = din("hy_fb2T", [64, 2])
    hy_ffreqT = din("hy_ffreqT", [64, 1])
    k_cos = din("k_cos", [128, S_LAT])
    k_sin = din("k_sin", [128, S_LAT])
    k_rm = din("k_rm", [128, 128])
    k_ident = din("k_ident", [128, 128])
    k_sel = din("k_sel", [128, 2, 128])
    k_dft = {}
    k_feat = {}
    k_decay = {}
    for n in (S_LAT, S_CTX):
        k_dft[n] = [din("k_dft%d_%d" % (n, q), [n, n], BF16) for q in range(4)]
        k_feat[n] = din("k_feat%d" % n, [33, n])
        k_decay[n] = din("k_decay%d" % n, [n, D])

    out = nc.dram_tensor("out", [NB * S_LAT, D], F32, kind="ExternalOutput").ap()

    hcur = dscr("hcur", [NTOK, D])
    hmid = dscr("hmid", [NTOK, D])
    mixT = dscr("mixT", [D, NTOK], BF16)
    vT = dscr("vT", [D, NTOK], BF16)
    modbc = dscr("modbc", [6, 3, D])

    def tok_rows(tt):
        b = tt // 18
        st = tt % 18
        if st < 16:
            return b, False, b * S_LAT + st * 128, b * TPB + st * 128
        return b, True, b * S_CTX + (st - 16) * 128, b * TPB + S_LAT + (st - 16) * 128

    def h_src(layer, tt):
        b, is_ctx, r, gt = tok_rows(tt)
        if layer == g.first:
            return (ctx2 if is_ctx else x2)[r:r + 128, :]
        return hcur[gt:gt + 128, :]

    def sb(ph, name, shape, dt):
        return ph.enter_context(nc.sbuf_tensor(name, list(shape), dt))

    def ps(ph, name, shape, dt=F32):
        return ph.enter_context(nc.psum_tensor(name, list(shape), dt))

    uid = [0]

    def nm(s):
        uid[0] += 1
        return "%s_%d" % (s, uid[0])

    class Ring:
        def __init__(self, ph, name, shape, dt, n, psum=False):
            self.bufs = []
            for i in range(n):
                t = (ps if psum else sb)(ph, nm(name), shape, dt)
                self.bufs.append((t, T(name)))
            self.i = 0

        def next(self):
            r = self.bufs[self.i % len(self.bufs)]
            self.i += 1
            return r

    def rstd_chain(ss, t_ss, epst, inv_n, n=1):
        S.op("act", lambda e: e.activation(out=ss, in_=ss, func=AF.Ln, scale=inv_n, bias=epst), reads=[t_ss], writes=[t_ss])
        S.op("act", lambda e: e.activation(out=ss, in_=ss, func=AF.Exp, scale=-0.5), reads=[t_ss], writes=[t_ss])

    def load_w_bf16(wsb, t_w, wd, kt, ncols, c0=0):
        for k in range(kt):
            S.dma("pool", lambda e, k=k: e.dma_start(out=wsb[:, k, 0:ncols], in_=wd[k * 128:(k + 1) * 128, c0:c0 + ncols]), writes=[t_w])

    def phase_mod(i):
        with ExitStack() as ph:
            craw = sb(ph, nm("craw"), [128, 8, 3], F32)
            ce = sb(ph, nm("ce"), [128, 8, 3], F32)
            scT = sb(ph, nm("scT"), [128, 8, 3], F32)
            modv = sb(ph, nm("modv"), [3, 6 * D], F32)
            bm = sb(ph, nm("bm"), [3, 6 * D], F32)
            gv = sb(ph, nm("gv"), [3, 4, D], F32)
            cmb = sb(ph, nm("cmb"), [3, 6, D], F32)
            wr = Ring(ph, "wmodc", [128, 8, 512], F32, 2)
            pr = Ring(ph, "pmod", [128, 512], F32, 2, psum=True)
            t_c, t_sc, t_modv, t_bm, t_gv, t_cmb = T(), T(), T(), T(), T(), T()
            S.dma("sp", lambda e: e.dma_start(out=craw[:], in_=cT[:, :, :]), writes=[t_c])
            S.dma("sp", lambda e: e.dma_start(out=bm[:], in_=b_mod[i, :].partition_broadcast(3)), writes=[t_bm])
            for q, gsrc in enumerate((n_mix_pre, n_mix_post, n_mlp_pre, n_mlp_post)):
                S.dma("sp", lambda e, q=q, gsrc=gsrc: e.dma_start(out=gv[:, q, :], in_=gsrc[i, :].partition_broadcast(3)), writes=[t_gv])
            S.op("act", lambda e: e.activation(out=ce[:], in_=craw[:], func=AF.Exp, scale=-1.0), reads=[t_c], writes=[t_sc])
            S.op("dve", lambda e: e.tensor_scalar_add(out=ce[:], in0=ce[:], scalar1=1.0), reads=[t_sc], writes=[t_sc])
            S.op("dve", lambda e: e.reciprocal(out=ce[:], in_=ce[:]), reads=[t_sc], writes=[t_sc])
            S.op("dve", lambda e: e.tensor_mul(out=scT[:], in0=craw[:], in1=ce[:]), reads=[t_sc, t_c], writes=[t_sc])
            for j in range(12):
                wt, t_w = wr.next()
                pt, t_p = pr.next()
                S.dma("sp", lambda e, wt=wt, j=j: e.dma_start(out=wt[:], in_=w_mod[i, :, j * 512:(j + 1) * 512].rearrange("(k p) n -> p k n", p=128)), writes=[t_w])

                def mm(e, wt=wt, pt=pt):
                    for k in range(8):
                        r = e.matmul(pt[0:3, :], lhsT=scT[:, k, :], rhs=wt[:, k, :], start=(k == 0), stop=(k == 7))
                    return r
                S.op("pe", mm, reads=[t_sc, t_w], writes=[t_p])
                S.op("dve", lambda e, pt=pt, j=j: e.tensor_add(out=modv[:, j * 512:(j + 1) * 512], in0=pt[0:3, :], in1=bm[:, j * 512:(j + 1) * 512]), reads=[t_p, t_bm], writes=[t_modv])
            def stt(e, o, a, gq):
                return e.scalar_tensor_tensor(out=cmb[:, o, :], in0=modv[:, a * D:(a + 1) * D], scalar=1.0, in1=gv[:, gq, :], op0=ALU.add, op1=ALU.mult)
            S.op("dve", lambda e: stt(e, 0, 1, 0), reads=[t_modv, t_gv], writes=[t_cmb])
            S.op("dve", lambda e: e.tensor_copy(out=cmb[:, 1, :], in_=modv[:, 0:D]), reads=[t_modv], writes=[t_cmb])
            S.op("dve", lambda e: e.tensor_mul(out=cmb[:, 2, :], in0=modv[:, 2 * D:3 * D], in1=gv[:, 1, :]), reads=[t_modv, t_gv], writes=[t_cmb])
            S.op("dve", lambda e: stt(e, 3, 4, 2), reads=[t_modv, t_gv], writes=[t_cmb])
            S.op("dve", lambda e: e.tensor_copy(out=cmb[:, 4, :], in_=modv[:, 3 * D:4 * D]), reads=[t_modv], writes=[t_cmb])
            S.op("dve", lambda e: e.tensor_mul(out=cmb[:, 5, :], in0=modv[:, 5 * D:6 * D], in1=gv[:, 3, :]), reads=[t_modv, t_gv], writes=[t_cmb])
            for q in range(6):
                S.dma("sp", lambda e, q=q: e.dma_start(out=modbc[q, :, :], in_=cmb[:, q, :]), reads=[t_cmb])
            S.barrier()

    class PreMix:
        def __init__(self, ph, layer, ident, t_ident, epst):
            self.layer = layer
            self.ident = ident
            self.t_ident = t_ident
            self.epst = epst
            self.hr = Ring(ph, "pm_h", [128, D], F32, 4)
            self.jr = Ring(ph, "pm_junk", [128, D], BF16, 1)
            self.tr = Ring(ph, "pm_t", [128, D], F32, 2)
            self.ur = Ring(ph, "pm_u", [128, D], BF16, 4)
            self.sr = Ring(ph, "pm_ss", [128, 1], F32, 4)
            self.sr4 = Ring(ph, "pm_ss4", [128, 4], F32, 3)
            self.pr = Ring(ph, "pm_ps", [128, 8, 128], BF16, 2, psum=True)
            self.A = [sb(ph, nm("pm_A"), [128, D], F32) for _ in range(2)]
            self.Bm = [sb(ph, nm("pm_B"), [128, D], F32) for _ in range(2)]
            self.t_ab = [T(), T()]
            self.cur = None
            self.nload = 0

        def set_mod(self, row):
            if self.cur == row:
                return
            self.cur = row
            k = self.nload % 2
            self.nload += 1
            self.k = k
            S.dma("sp", lambda e: e.dma_start(out=self.A[k][:], in_=modbc[0, row, :].partition_broadcast(128)), writes=[self.t_ab[k]])
            S.dma("sp", lambda e: e.dma_start(out=self.Bm[k][:], in_=modbc[1, row, :].partition_broadcast(128)), writes=[self.t_ab[k]])

        def group(self, tts, dsts):
            b, is_ctx, r, gt = tok_rows(tts[0])
            self.set_mod(2 if is_ctx else b)
            k = self.k
            A, Bm, t_ab = self.A[k], self.Bm[k], self.t_ab[k]
            ss, t_ss = self.sr4.next()
            S.op("pool", lambda e: e.memset(ss[:], 0.0), writes=[t_ss])
            hts = []
            for tt in tts:
                ht, t_h = self.hr.next()
                src = h_src(self.layer, tt)
                S.dma("sp", lambda e, ht=ht, src=src: e.dma_start(out=ht[:], in_=src), writes=[t_h])
                hts.append((ht, t_h))
            for ti, (ht, t_h) in enumerate(hts):
                jk, t_j = self.jr.next()
                S.op("act", lambda e, jk=jk, ht=ht, ti=ti: e.activation(out=jk[:], in_=ht[:], func=AF.Square, accum_out=ss[:, ti:ti + 1]), reads=[t_h, t_ss], writes=[t_j, t_ss])
            rstd_chain(ss[:, 0:len(tts)], t_ss, self.epst[:], 1.0 / D)
            ubs = []
            for ti, (ht, t_h) in enumerate(hts):
                tm, t_t = self.tr.next()
                ub, t_u = self.ur.next()
                S.op("dve", lambda e, tm=tm, ht=ht, ti=ti: e.scalar_tensor_tensor(out=tm[:], in0=ht[:], scalar=ss[:, ti:ti + 1], in1=A[:], op0=ALU.mult, op1=ALU.mult), reads=[t_h, t_ss, t_ab], writes=[t_t])
                S.op("dve", lambda e, ub=ub, tm=tm: e.tensor_add(out=ub[:], in0=tm[:], in1=Bm[:]), reads=[t_t, t_ab], writes=[t_u])
                ubs.append((ub, t_u))
            for (ub, t_u), (dst, t_dst) in zip(ubs, dsts):
                pt, t_p = self.pr.next()

                def tr(e, pt=pt, ub=ub):
                    for kk in range(8):
                        r_ = e.transpose(out=pt[:, kk, :], in_=ub[:, kk * 128:(kk + 1) * 128], identity=self.ident[:])
                    return r_
                S.op("pe", tr, reads=[t_u, self.t_ident], writes=[t_p])
                S.op("act", lambda e, dst=dst, pt=pt: e.copy(out=dst, in_=pt[:]), reads=[t_p], writes=[t_dst])

        def tile(self, tt, dst, t_dst):
            b, is_ctx, r, gt = tok_rows(tt)
            self.set_mod(2 if is_ctx else b)
            k = self.k
            A, Bm, t_ab = self.A[k], self.Bm[k], self.t_ab[k]
            ht, t_h = self.hr.next()
            jk, t_j = self.jr.next()
            tm, t_t = self.tr.next()
            ub, t_u = self.ur.next()
            ss, t_ss = self.sr.next()
            pt, t_p = self.pr.next()
            src = h_src(self.layer, tt)
            S.dma("sp", lambda e: e.dma_start(out=ht[:], in_=src), writes=[t_h])
            S.op("pool", lambda e: e.memset(ss[:], 0.0), writes=[t_ss])
            S.op("act", lambda e: e.activation(out=jk[:], in_=ht[:], func=AF.Square, accum_out=ss[:]), reads=[t_h], writes=[t_j, t_ss])
            rstd_chain(ss[:], t_ss, self.epst[:], 1.0 / D)
            S.op("dve", lambda e: e.scalar_tensor_tensor(out=tm[:], in0=ht[:], scalar=ss[:, 0:1], in1=A[:], op0=ALU.mult, op1=ALU.mult), reads=[t_h, t_ss, t_ab], writes=[t_t])
            S.op("dve", lambda e: e.tensor_add(out=ub[:], in0=tm[:], in1=Bm[:]), reads=[t_t, t_ab], writes=[t_u])

            def tr(e):
                for kk in range(8):
                    r_ = e.transpose(out=pt[:, kk, :], in_=ub[:, kk * 128:(kk + 1) * 128], identity=self.ident[:])
                return r_
            S.op("pe", tr, reads=[t_u, self.t_ident], writes=[t_p])
            S.op("act", lambda e: e.copy(out=dst, in_=pt[:]), reads=[t_p], writes=[t_dst])

    def load_consts(ph):
        ident = sb(ph, nm("ident"), [128, 128], BF16)
        identf = sb(ph, nm("identf"), [128, 128], F32)
        epst = sb(ph, nm("eps"), [128, 1], F32)
        t_id = T()
        S.dma("sp", lambda e: e.dma_start(out=identf[:], in_=k_ident[:, :]), writes=[t_id])
        S.op("dve", lambda e: e.tensor_copy(out=ident[:], in_=identf[:]), reads=[t_id], writes=[t_id])
        S.op("pool", lambda e: e.memset(epst[:], EPS), writes=[t_id])
        return ident, t_id, epst

    def phase_attn(i, j, ctx_q):
        lam_init = 0.8 - 0.6 * math.exp(-0.3 * i)
        with ExitStack() as ph:
            ident, t_id, epst = load_consts(ph)
            Wqkv = sb(ph, nm("Wqkv"), [128, 8, 3 * D], BF16)
            t_w = T()
            load_w_bf16(Wqkv, t_w, attn_w_qkv[j], 8, 3 * D)
            cosT = sb(ph, nm("cosT"), [128, S_LAT], F32)
            sinT = sb(ph, nm("sinT"), [128, S_LAT], F32)
            Rm = sb(ph, nm("Rm"), [128, 128], F32)
            ones = sb(ph, nm("ones"), [128, 128], BF16)
            Sel = sb(ph, nm("Sel"), [128, 2, 128], F32)
            lamt = sb(ph, nm("lamt"), [128, 256], F32)
            lprod = sb(ph, nm("lprod"), [128, 128], F32)
            lsum = sb(ph, nm("lsum"), [128, 2], F32)
            neglam = sb(ph, nm("neglam"), [128, 1], F32)
            subg = sb(ph, nm("subg"), [128, 1], F32)
            t_k = T()
            t_lam = T()
            S.dma("sp", lambda e: e.dma_start(out=cosT[:], in_=k_cos[:, :]), writes=[t_k])
            S.dma("sp", lambda e: e.dma_start(out=sinT[:], in_=k_sin[:, :]), writes=[t_k])
            S.dma("sp", lambda e: e.dma_start(out=Rm[:], in_=k_rm[:, :]), writes=[t_k])
            S.op("pool", lambda e: e.memset(ones[:], 1.0), writes=[t_k])
            S.dma("sp", lambda e: e.dma_start(out=Sel[:], in_=k_sel[:, :, :]), writes=[t_k])
            S.dma("sp", lambda e: e.dma_start(out=lamt[:], in_=attn_lambda[j, :].partition_broadcast(128)), writes=[t_lam])
            S.dma("sp", lambda e: e.dma_start(out=subg[:], in_=attn_subln[j, :].rearrange("(p o) -> p o", o=1)), writes=[t_lam])
            S.op("dve", lambda e: e.tensor_mul(out=lprod[:, 0:64], in0=lamt[:, 0:64], in1=lamt[:, 64:128]), reads=[t_lam], writes=[t_lam])
            S.op("dve", lambda e: e.tensor_mul(out=lprod[:, 64:128], in0=lamt[:, 128:192], in1=lamt[:, 192:256]), reads=[t_lam], writes=[t_lam])
            S.op("dve", lambda e: e.reduce_sum(out=lsum[:, 0:1], in_=lprod[:, 0:64], axis=AX.X), reads=[t_lam], writes=[t_lam])
            S.op("dve", lambda e: e.reduce_sum(out=lsum[:, 1:2], in_=lprod[:, 64:128], axis=AX.X), reads=[t_lam], writes=[t_lam])
            S.op("act", lambda e: e.activation(out=lsum[:], in_=lsum[:], func=AF.Exp), reads=[t_lam], writes=[t_lam])
            S.op("dve", lambda e: e.scalar_tensor_tensor(out=neglam[:], in0=lsum[:, 1:2], scalar=-lam_init, in1=lsum[:, 0:1], op0=ALU.add, op1=ALU.subtract), reads=[t_lam], writes=[t_lam])
            S.op("dve", lambda e: e.tensor_scalar_mul(out=subg[:], in0=subg[:], scalar1=1.0 - lam_init), reads=[t_lam], writes=[t_lam])

            uT = sb(ph, nm("uT"), [128, 8, TPB], BF16)
            V = sb(ph, nm("V"), [128, 18, D], BF16)
            t_uT = [T() for _ in range(18)]
            t_V = [T() for _ in range(18)]
            for b in range(NB):
                with ExitStack() as pa:
                    pm = PreMix(pa, i, ident, t_id, epst)
                    vr = Ring(pa, "pV", [128, 2, 512], F32, 2, psum=True)
                    pgroups = [[0, 1, 2, 3], [4, 5, 6, 7], [8, 9, 10, 11], [12, 13, 14, 15], [16, 17]]

                    def pmg(gi_):
                        sts_ = pgroups[gi_]
                        pm.group([b * 18 + s_ for s_ in sts_], [(uT[:, :, s_ * 128:(s_ + 1) * 128], t_uT[s_]) for s_ in sts_])
                    pmg(0)
                    for st in range(18):
                        if st % 4 == 0 and st // 4 + 1 < len(pgroups):
                            pmg(st // 4 + 1)
                        pv, t_pv = vr.next()

                        def mmv(e, pv=pv, st=st):
                            for n2 in range(2):
                                for k in range(8):
                                    r_ = e.matmul(pv[:, n2, :], lhsT=uT[:, k, st * 128:(st + 1) * 128], rhs=Wqkv[:, k, 2 * D + n2 * 512:2 * D + (n2 + 1) * 512], start=(k == 0), stop=(k == 7))
                            return r_
                        S.op("pe", mmv, reads=[t_uT[st], t_w], writes=[t_pv])
                        S.op("dve", lambda e, pv=pv, st=st: e.tensor_copy(out=V[:, st, :], in_=pv[:].rearrange("p a b -> p (a b)")), reads=[t_pv], writes=[t_V[st]])
                    S.barrier()
                with ExitStack() as pb:
                    qk_r = Ring(pb, "QKT", [128, 2, TPB], BF16, 2)
                    sps = Ring(pb, "Sps", [128, 2, 512], F32, 2, psum=True)
                    O = [ps(pb, nm("O"), [128, 512]) for _ in range(2)]
                    smb = ps(pb, nm("smb"), [128, 512])
                    bc = ps(pb, nm("bc"), [128, 512])
                    t_O = [T(), T()]
                    t_smb, t_bc = T(), T()
                    sms_r = Ring(pb, "smS", [128, 512], F32, 1)
                    qf_r = Ring(pb, "qf", [128, 512], F32, 2)
                    t1_r = Ring(pb, "t1", [128, 512], F32, 2)
                    t2_r = Ring(pb, "t2", [128, 512], F32, 2)
                    pt_r = Ring(pb, "PT", [128, 2, 512], BF16, 4)
                    r_r = Ring(pb, "rr", [128, 2, 512], F32, 2)
                    o_r = Ring(pb, "oo", [128, 3, 512], F32, 1)
                    o2_r = Ring(pb, "o2", [128, 512], BF16, 1)
                    rs_r = Ring(pb, "rs", [128, 512], F32, 1)
                    mo_r = Ring(pb, "mo", [128, 512], BF16, 2)
                    chunks = [(0, 512, True), (512, 512, True), (1024, 512, True), (1536, 512, True), (2048, 256, False)]
                    defq = []

                    def tick_defq():
                        for it_ in list(defq):
                            it_[0] -= 1
                            if it_[0] <= 0:
                                defq.remove(it_)
                                it_[1]()

                    def flush_defq():
                        for it_ in list(defq):
                            defq.remove(it_)
                            it_[1]()
                    for h in range(8):
                        qk, t_qk = qk_r.next()
                        t_qkc = [[T(), T()] for _ in chunks]
                        deferred = [None]

                        def flush():
                            if deferred[0] is not None:
                                deferred[0]()
                                deferred[0] = None
                        for ci, (c0, n, is_lat) in enumerate(chunks):
                            for which in (0, 1):
                                if which == 0 and (not is_lat) and (not ctx_q):
                                    continue
                                pp, t_pp = sps.next()
                                off = which * D + h * 128

                                def mmq(e, pp=pp, off=off, c0=c0, n=n):
                                    for k in range(8):
                                        r_ = e.matmul(pp[:, 0, 0:n], lhsT=Wqkv[:, k, off:off + 128], rhs=uT[:, k, c0:c0 + n], start=(k == 0), stop=(k == 7))
                                    return r_
                                sts = list(range(c0 // 128, (c0 + n) // 128))
                                S.op("pe", mmq, reads=[t_w] + [t_uT[s_] for s_ in sts], writes=[t_pp])
                                dst = qk[:, which, c0:c0 + n]
                                if is_lat:
                                    qf, t_qf = qf_r.next()
                                    S.op("act", lambda e, qf=qf, pp=pp: e.copy(out=qf[:], in_=pp[:, 0, :]), reads=[t_pp], writes=[t_qf])
                                    flush()

                                    def rope(pp=pp, t_pp=t_pp, qf=qf, t_qf=t_qf, c0=c0, dst=dst, t_d=t_qkc[ci][which]):
                                        t1, t_t1 = t1_r.next()
                                        t2, t_t2 = t2_r.next()
                                        S.op("pe", lambda e: e.matmul(pp[:, 1, :], lhsT=Rm[:], rhs=qf[:], start=True, stop=True), reads=[t_qf, t_k], writes=[t_pp])
                                        S.op("dve", lambda e: e.tensor_mul(out=t1[:], in0=qf[:], in1=cosT[:, c0:c0 + 512]), reads=[t_qf, t_k], writes=[t_t1])
                                        S.op("dve", lambda e: e.tensor_mul(out=t2[:], in0=pp[:, 1, :], in1=sinT[:, c0:c0 + 512]), reads=[t_pp, t_k], writes=[t_t2])
                                        S.op("pool", lambda e: e.tensor_tensor(out=dst, in0=t1[:], in1=t2[:], op=ALU.add), reads=[t_t1, t_t2], writes=[t_d, t_qk])
                                    deferred[0] = rope
                                else:
                                    S.op("act", lambda e, dst=dst, pp=pp, n=n: e.copy(out=dst, in_=pp[:, 0, 0:n]), reads=[t_pp], writes=[t_qkc[ci][which], t_qk])
                        flush()
                        for ci, (c0, n, is_lat) in enumerate(chunks):
                            if (not is_lat) and (not ctx_q):
                                continue
                            kts = list(range(18)) if is_lat else [16, 17]
                            t_kall = [t_qkc[c_][1] for c_ in range(len(chunks))]
                            pend = None
                            sum_pend = []
                            for kt in kts + [None]:
                                if kt is not None:
                                    sp_, t_sp = sps.next()

                                    def mms(e, sp_=sp_, kt=kt, c0=c0, n=n, qk=qk):
                                        for comp in (0, 1):
                                            p0 = comp * 64
                                            r_ = e.matmul(sp_[:, comp, 0:n], lhsT=qk[p0:p0 + 64, 1, kt * 128:(kt + 1) * 128], rhs=qk[p0:p0 + 64, 0, c0:c0 + n], start=True, stop=True, tile_position=(p0, 0))
                                        return r_
                                    S.op("pe", mms, reads=t_kall + [t_qkc[ci][0], t_qk], writes=[t_sp])
                                    ptt, t_pt = pt_r.next()
                                    S.op("act", lambda e, ptt=ptt, sp_=sp_, n=n: e.activation(out=ptt[:, :, 0:n], in_=sp_[:, :, 0:n], func=AF.Exp, scale=0.125), reads=[t_sp], writes=[t_pt])
                                    cur = (kt, ptt, t_pt)
                                else:
                                    cur = None
                                if pend is not None:
                                    pkt, pptt, pt_pt = pend
                                    first = (pkt == kts[0])
                                    lastp = (pkt == kts[-1])

                                    def mmpv(e, pkt=pkt, pptt=pptt, first=first, lastp=lastp, n=n, h=h):
                                        for comp in (0, 1):
                                            r_ = e.matmul(O[comp][:, 0:n], lhsT=V[:, pkt, h * 128:(h + 1) * 128], rhs=pptt[:, comp, 0:n], start=first, stop=lastp)
                                        return r_
                                    S.op("pe", mmpv, reads=[pt_pt, t_k, t_V[pkt]], writes=[t_O[0], t_O[1]])
                                    sum_pend.append((pkt, pptt, pt_pt))
                                    if len(sum_pend) == 2:
                                        (ka_, pa_, ta_), (kb_, pb_, tb_) = sum_pend
                                        fs_ = (ka_ == kts[0])
                                        ls_ = (kb_ == kts[-1])

                                        def mmsum(e, pa_=pa_, pb_=pb_, fs_=fs_, ls_=ls_, n=n):
                                            for idx_, pp_ in enumerate((pa_, pb_)):
                                                for comp in (0, 1):
                                                    g_ = idx_ * 2 + comp
                                                    r_ = e.matmul(smb[g_ * 32:(g_ + 1) * 32, 0:n], lhsT=ones[:, 0:32], rhs=pp_[:, comp, 0:n], start=fs_, stop=ls_, tile_position=(0, g_ * 32), skip_group_check=True)
                                            return r_
                                        S.op("pe", mmsum, reads=[ta_, tb_, t_k], writes=[t_smb])
                                        del sum_pend[:]
                                pend = cur
                                tick_defq()
                            rr, t_rr = r_r.next()
                            oo, t_oo = o_r.next()
                            o2, t_o2 = o2_r.next()
                            rs, t_rs = rs_r.next()
                            mo, t_mo = mo_r.next()
                            smS, t_smS = sms_r.next()
                            g0 = b * TPB + c0
                            flush_defq()
                            for c_ in (0, 1):
                                S.op("act", lambda e, c_=c_, rr=rr, n=n: e.copy(out=rr[:, c_, 0:n], in_=O[c_][:, 0:n]), reads=[t_O[c_]], writes=[t_rr])
                            S.op("act", lambda e, smS=smS, n=n: e.copy(out=smS[:, 0:n], in_=smb[:, 0:n]), reads=[t_smb], writes=[t_smS])
                            S.op("dve", lambda e, smS=smS, n=n: e.reciprocal(out=smS[:, 0:n], in_=smS[:, 0:n]), reads=[t_smS], writes=[t_smS])

                            def tailA(rr=rr, t_rr=t_rr, oo=oo, t_oo=t_oo, o2=o2, t_o2=t_o2, smS=smS, t_smS=t_smS, n=n):
                                for c_ in (0, 1):
                                    S.op("pe", lambda e, c_=c_: e.matmul(bc[:, 0:n], lhsT=Sel[:, c_, :], rhs=smS[:, 0:n], start=True, stop=True), reads=[t_smS, t_k], writes=[t_bc])
                                    S.op("dve", lambda e, c_=c_: e.tensor_mul(out=oo[:, c_, 0:n], in0=rr[:, c_, 0:n], in1=bc[:, 0:n]), reads=[t_bc, t_rr], writes=[t_oo])
                                S.op("dve", lambda e: e.scalar_tensor_tensor(out=oo[:, 2, 0:n], in0=oo[:, 1, 0:n], scalar=neglam[:, 0:1], in1=oo[:, 0, 0:n], op0=ALU.mult, op1=ALU.add), reads=[t_oo, t_lam], writes=[t_oo])
                                S.op("pool", lambda e: e.tensor_tensor(out=o2[:, 0:n], in0=oo[:, 2, 0:n], in1=oo[:, 2, 0:n], op=ALU.mult), reads=[t_oo], writes=[t_o2])

                            def tailB(oo=oo, t_oo=t_oo, o2=o2, t_o2=t_o2, rs=rs, t_rs=t_rs, mo=mo, t_mo=t_mo, n=n, g0=g0, h=h):
                                S.op("pe", lambda e: e.matmul(bc[:, 0:n], lhsT=ones[:], rhs=o2[:, 0:n], start=True, stop=True), reads=[t_o2, t_k], writes=[t_bc])
                                S.op("act", lambda e: e.activation(out=rs[:, 0:n], in_=bc[:, 0:n], func=AF.Ln, scale=1.0 / 128, bias=epst[:]), reads=[t_bc, t_id], writes=[t_rs])
                                S.op("act", lambda e: e.activation(out=rs[:, 0:n], in_=rs[:, 0:n], func=AF.Exp, scale=-0.5), reads=[t_rs], writes=[t_rs])
                                S.op("dve", lambda e: e.scalar_tensor_tensor(out=mo[:, 0:n], in0=oo[:, 2, 0:n], scalar=subg[:, 0:1], in1=rs[:, 0:n], op0=ALU.mult, op1=ALU.mult), reads=[t_oo, t_rs, t_lam], writes=[t_mo])
                                S.dma("pool", lambda e: e.dma_start(out=mixT[h * 128:(h + 1) * 128, g0:g0 + n], in_=mo[:, 0:n]), reads=[t_mo])
                            defq.append([4, tailA])
                            defq.append([9, tailB])
                        flush_defq()
                    S.barrier()
            S.barrier()

    def phase_p3a(i, w_o, b_o, skip_ctx):
        with ExitStack() as ph:
            ident, t_id, epst = load_consts(ph)
            Wo = sb(ph, nm("Wo"), [128, 8, D], BF16)
            t_w = T()
            load_w_bf16(Wo, t_w, w_o, 8, D)
            bo = None
            if b_o is not None:
                bo = sb(ph, nm("bo"), [128, D], F32)
                S.dma("sp", lambda e: e.dma_start(out=bo[:], in_=b_o.partition_broadcast(128)), writes=[t_w])
            GAB = [[sb(ph, nm("GAB"), [128, D], F32) for _ in range(3)] for _ in range(2)]
            t_ab = [T(), T()]
            mx_r = Ring(ph, "mx", [128, 8, 512], BF16, 2)
            vt_r = Ring(ph, "vt", [128, 8, 512], BF16, 2)
            py_r = Ring(ph, "py", [128, 2, 512], F32, 3, psum=True)
            pt_r = Ring(ph, "ptr", [128, 8, 128], BF16, 2, psum=True)
            y_r = Ring(ph, "y", [128, D], F32, 8)
            h_r = Ring(ph, "hold", [128, D], F32, 8)
            hm_r = Ring(ph, "hm", [128, D], F32, 8)
            tm_r = Ring(ph, "tm", [128, D], F32, 3)
            vb_r = Ring(ph, "vb", [128, D], BF16, 8)
            jk_r = Ring(ph, "jk", [128, D], BF16, 2)
            ss_r = Ring(ph, "ss", [128, 8], F32, 6)

            glist = []
            nload = 0
            for b in range(NB):
                for seg in (0, 1):
                    if seg == 1 and skip_ctx:
                        continue
                    groups = [(g_ * 512, 512) for g_ in range(4)] if seg == 0 else [(S_LAT, 256)]
                    for gi, (c0, n) in enumerate(groups):
                        glist.append(dict(b=b, seg=seg, c0=c0, n=n, newmod=(gi == 0), k=nload % 2, row=(2 if seg == 1 else b)))
                    nload += 1

            def S0(gd):
                b, c0, n, k = gd["b"], gd["c0"], gd["n"], gd["k"]
                if gd["newmod"]:
                    for q3, q in enumerate((2, 3, 4)):
                        S.dma("sp", lambda e, q=q, dstt=GAB[k][q3], row=gd["row"]: e.dma_start(out=dstt[:], in_=modbc[q, row, :].partition_broadcast(128)), writes=[t_ab[k]])
                g0 = b * TPB + c0
                gd["g0"] = g0
                nti = n // 128
                mx, t_mx = mx_r.next()
                ss, t_ss = ss_r.next()
                gd["ss"], gd["t_ss"] = ss, t_ss
                S.dma("sp", lambda e, mx=mx, g0=g0, n=n: e.dma_start(out=mx[:, :, 0:n], in_=mixT[:, g0:g0 + n].rearrange("(k p) t -> p k t", p=128)), writes=[t_mx])
                S.op("pool", lambda e, ss=ss: e.memset(ss[:], 0.0), writes=[t_ss])
                tiles = []
                for ti in range(nti):
                    tt = b * 18 + (c0 // 128) + ti
                    py, t_py = py_r.next()
                    y, t_y = y_r.next()
                    ho, t_ho = h_r.next()
                    src = h_src(i, tt)
                    S.dma("sp", lambda e, ho=ho, src=src: e.dma_start(out=ho[:], in_=src), writes=[t_ho])

                    def mmy(e, py=py, mx=mx, ti=ti):
                        for n2 in range(2):
                            for kk in range(8):
                                r_ = e.matmul(py[:, n2, :], lhsT=mx[:, kk, ti * 128:(ti + 1) * 128], rhs=Wo[:, kk, n2 * 512:(n2 + 1) * 512], start=(kk == 0), stop=(kk == 7))
                        return r_
                    S.op("pe", mmy, reads=[t_mx, t_w], writes=[t_py])
                    pyf = py[:].rearrange("p a b -> p (a b)")
                    if bo is not None:
                        S.op("dve", lambda e, y=y, pyf=pyf: e.tensor_add(out=y[:], in0=pyf, in1=bo[:]), reads=[t_py, t_w], writes=[t_y])
                    else:
                        S.op("act", lambda e, y=y, pyf=pyf: e.copy(out=y[:], in_=pyf), reads=[t_py], writes=[t_y])
                    tiles.append(dict(tt=tt, y=y, t_y=t_y, ho=ho, t_ho=t_ho))
                gd["tiles"] = tiles

            def S12(gd):
                ss, t_ss, k, tiles = gd["ss"], gd["t_ss"], gd["k"], gd["tiles"]
                nti = len(tiles)
                for ti, tl in enumerate(tiles):
                    jk, t_jk = jk_r.next()
                    S.op("act", lambda e, jk=jk, y=tl["y"], ss=ss, ti=ti: e.activation(out=jk[:], in_=y[:], func=AF.Square, accum_out=ss[:, ti:ti + 1]), reads=[tl["t_y"], t_ss], writes=[t_jk, t_ss])
                rstd_chain(ss[:, 0:nti], t_ss, epst[:], 1.0 / D)
                for ti, tl in enumerate(tiles):
                    tm, t_tm = tm_r.next()
                    hm, t_hm = hm_r.next()
                    S.op("dve", lambda e, tm=tm, y=tl["y"], ss=ss, k=k, ti=ti: e.scalar_tensor_tensor(out=tm[:], in0=y[:], scalar=ss[:, ti:ti + 1], in1=GAB[k][0][:], op0=ALU.mult, op1=ALU.mult), reads=[tl["t_y"], t_ss, t_ab[k]], writes=[t_tm])
                    S.op("pool", lambda e, hm=hm, tm=tm, ho=tl["ho"]: e.tensor_tensor(out=hm[:], in0=tm[:], in1=ho[:], op=ALU.add), reads=[t_tm, tl["t_ho"]], writes=[t_hm])
                    gt = tok_rows(tl["tt"])[3]
                    S.dma("pool", lambda e, hm=hm, gt=gt: e.dma_start(out=hmid[gt:gt + 128, :], in_=hm[:]), reads=[t_hm])
                    tl["hm"], tl["t_hm"] = hm, t_hm

            def S34(gd):
                ss, t_ss, k, tiles = gd["ss"], gd["t_ss"], gd["k"], gd["tiles"]
                nti = len(tiles)
                for ti, tl in enumerate(tiles):
                    jk, t_jk = jk_r.next()
                    S.op("act", lambda e, jk=jk, hm=tl["hm"], ss=ss, ti=ti: e.activation(out=jk[:], in_=hm[:], func=AF.Square, accum_out=ss[:, 4 + ti:5 + ti]), reads=[tl["t_hm"], t_ss], writes=[t_jk, t_ss])
                rstd_chain(ss[:, 4:4 + nti], t_ss, epst[:], 1.0 / D)
                vbs = []
                for ti, tl in enumerate(tiles):
                    tm, t_tm = tm_r.next()
                    vb, t_vb = vb_r.next()
                    S.op("dve", lambda e, tm=tm, hm=tl["hm"], ss=ss, k=k, ti=ti: e.scalar_tensor_tensor(out=tm[:], in0=hm[:], scalar=ss[:, 4 + ti:5 + ti], in1=GAB[k][1][:], op0=ALU.mult, op1=ALU.mult), reads=[tl["t_hm"], t_ss, t_ab[k]], writes=[t_tm])
                    S.op("dve", lambda e, vb=vb, tm=tm, k=k: e.tensor_add(out=vb[:], in0=tm[:], in1=GAB[k][2][:]), reads=[t_tm, t_ab[k]], writes=[t_vb])
                    vbs.append((vb, t_vb))
                gd["vbs"] = vbs

            def S5(gd):
                vt, t_vt = vt_r.next()
                g0, n = gd["g0"], gd["n"]
                for ti, (vb, t_vb) in enumerate(gd["vbs"]):
                    ptr, t_ptr = pt_r.next()

                    def tr(e, ptr=ptr, vb=vb):
                        for kk in range(8):
                            r_ = e.transpose(out=ptr[:, kk, :], in_=vb[:, kk * 128:(kk + 1) * 128], identity=ident[:])
                        return r_
                    S.op("pe", tr, reads=[t_vb, t_id], writes=[t_ptr])
                    S.op("act", lambda e, vt=vt, ptr=ptr, ti=ti: e.copy(out=vt[:, :, ti * 128:(ti + 1) * 128], in_=ptr[:]), reads=[t_ptr], writes=[t_vt])
                S.dma("pool", lambda e, vt=vt, g0=g0, n=n: e.dma_start(out=vT[:, g0:g0 + n].rearrange("(k p) t -> p k t", p=128), in_=vt[:, :, 0:n]), reads=[t_vt])

            stages = (S0, S12, S34, S5)
            segs = []
            for gd in glist:
                if gd["newmod"]:
                    segs.append([])
                segs[-1].append(gd)
            for sg in segs:
                ng = len(sg)
                for it in range(ng + len(stages) - 1):
                    for si, fn in enumerate(stages):
                        gi = it - si
                        if 0 <= gi < ng:
                            fn(sg[gi])
            S.barrier()

    def phase_p3b(i, last):
        with ExitStack() as ph:
            ident, t_id, epst = load_consts(ph)
            Win = sb(ph, nm("Win"), [128, 8, DFF], BF16)
            Wout = sb(ph, nm("Wout"), [128, 32, D], BF16)
            t_winc = [T() for _ in range(4)]
            t_woutc = [T() for _ in range(4)]
            for cc in range(4):
                for k_ in range(8):
                    S.dma("pool", lambda e, k_=k_, cc=cc: e.dma_start(out=Win[:, k_, cc * 1024:(cc + 1) * 1024], in_=w_mlp_in[i][k_ * 128:(k_ + 1) * 128, cc * 1024:(cc + 1) * 1024]), writes=[t_winc[cc]])
            for cc in range(4):
                for k_ in range(8):
                    kk_ = cc * 8 + k_
                    S.dma("pool", lambda e, kk_=kk_: e.dma_start(out=Wout[:, kk_, :], in_=w_mlp_out[i][kk_ * 128:(kk_ + 1) * 128, :]), writes=[t_woutc[cc]])
            G2 = [sb(ph, nm("G2"), [128, D], F32) for _ in range(2)]
            t_ab = [T(), T()]
            vt_r = Ring(ph, "vt", [128, 8, 512], BF16, 2)
            hid = sb(ph, nm("hid"), [128, 32, 512], BF16)
            t_hid = [T() for _ in range(32)]
            p1_r = Ring(ph, "p1", [128, 512], F32, 2, psum=True)
            p2_r = Ring(ph, "p2", [128, 2, 512], F32, 2, psum=True)
            rl_r = Ring(ph, "rl", [128, 512], F32, 2)
            m_r = Ring(ph, "m", [128, D], F32, 1)
            hm_r = Ring(ph, "hm", [128, D], F32, 1)
            tm_r = Ring(ph, "tm", [128, D], F32, 1)
            ho_r = Ring(ph, "ho", [128, D], F32, 1)
            jk_r = Ring(ph, "jk", [128, D], BF16, 1)
            ss_r = Ring(ph, "ss", [128, 1], F32, 4)
            nload = [0]
            for b in range(NB):
                for seg in (0, 1):
                    if seg == 1 and last:
                        continue
                    row = 2 if seg == 1 else b
                    k = nload[0] % 2
                    nload[0] += 1
                    S.dma("sp", lambda e, k=k, row=row: e.dma_start(out=G2[k][:], in_=modbc[5, row, :].partition_broadcast(128)), writes=[t_ab[k]])
                    groups = [(g_ * 512, 512) for g_ in range(4)] if seg == 0 else [(S_LAT, 256)]
                    for (c0, n) in groups:
                        g0 = b * TPB + c0
                        vt, t_vt = vt_r.next()
                        S.dma("sp", lambda e, vt=vt, g0=g0, n=n: e.dma_start(out=vt[:, :, 0:n], in_=vT[:, g0:g0 + n].rearrange("(k p) t -> p k t", p=128)), writes=[t_vt])
                        for f in range(32):
                            p1, t_p1 = p1_r.next()
                            rl, t_rl = rl_r.next()

                            def mm1(e, p1=p1, vt=vt, f=f, n=n):
                                for kk in range(8):
                                    r_ = e.matmul(p1[:, 0:n], lhsT=Win[:, kk, f * 128:(f + 1) * 128], rhs=vt[:, kk, 0:n], start=(kk == 0), stop=(kk == 7))
                                return r_
                            S.op("pe", mm1, reads=[t_vt, t_winc[f // 8]], writes=[t_p1])
                            S.op("act", lambda e, rl=rl, p1=p1, n=n: e.activation(out=rl[:, 0:n], in_=p1[:, 0:n], func=AF.Relu), reads=[t_p1], writes=[t_rl])
                            S.op("dve", lambda e, rl=rl, f=f, n=n: e.tensor_mul(out=hid[:, f, 0:n], in0=rl[:, 0:n], in1=rl[:, 0:n]), reads=[t_rl], writes=[t_hid[f]])
                        for ti in range(n // 128):
                            tt = b * 18 + (c0 // 128) + ti
                            gt = tok_rows(tt)[3]
                            p2, t_p2 = p2_r.next()
                            m, t_m = m_r.next()
                            hm, t_hm = hm_r.next()
                            tm, t_tm = tm_r.next()
                            ho, t_ho = ho_r.next()
                            jk, t_jk = jk_r.next()
                            ss, t_ss = ss_r.next()
                            S.dma("sp", lambda e, hm=hm, gt=gt: e.dma_start(out=hm[:], in_=hmid[gt:gt + 128, :]), writes=[t_hm])

                            def mm2(e, p2=p2, ti=ti):
                                for n2 in range(2):
                                    for f in range(32):
                                        r_ = e.matmul(p2[:, n2, :], lhsT=hid[:, f, ti * 128:(ti + 1) * 128], rhs=Wout[:, f, n2 * 512:(n2 + 1) * 512], start=(f == 0), stop=(f == 31))
                                return r_
                            S.op("pe", mm2, reads=t_hid + t_woutc, writes=[t_p2])
                            S.op("act", lambda e, m=m, p2=p2: e.copy(out=m[:], in_=p2[:].rearrange("p a b -> p (a b)")), reads=[t_p2], writes=[t_m])
                            S.op("pool", lambda e, ss=ss: e.memset(ss[:], 0.0), writes=[t_ss])
                            S.op("act", lambda e, jk=jk, m=m, ss=ss: e.activation(out=jk[:], in_=m[:], func=AF.Square, accum_out=ss[:, 0:1]), reads=[t_m], writes=[t_jk, t_ss])
                            rstd_chain(ss[:, 0:1], t_ss, epst[:], 1.0 / D)
                            S.op("dve", lambda e, tm=tm, m=m, ss=ss, k=k: e.scalar_tensor_tensor(out=tm[:], in0=m[:], scalar=ss[:, 0:1], in1=G2[k][:], op0=ALU.mult, op1=ALU.mult), reads=[t_m, t_ss, t_ab[k]], writes=[t_tm])
                            S.op("pool", lambda e, ho=ho, tm=tm, hm=hm: e.tensor_tensor(out=ho[:], in0=tm[:], in1=hm[:], op=ALU.add), reads=[t_tm, t_hm], writes=[t_ho])
                            if last:
                                r0 = b * S_LAT + c0 + ti * 128
                                S.dma("pool", lambda e, ho=ho, r0=r0: e.dma_start(out=out[r0:r0 + 128, :], in_=ho[:]), reads=[t_ho])
                            else:
                                S.dma("pool", lambda e, ho=ho, gt=gt: e.dma_start(out=hcur[gt:gt + 128, :], in_=ho[:]), reads=[t_ho])
            S.barrier()


    def token_groups(skip_ctx=False):
        for b in range(NB):
            for seg in (0, 1):
                if seg == 1 and skip_ctx:
                    continue
                groups = [(g_ * 512, 512) for g_ in range(4)] if seg == 0 else [(S_LAT, 256)]
                for (c0, n) in groups:
                    yield b, seg, c0, n

    def phase_proj_fm(i, w_d, nf, biasT_d, post):
        with ExitStack() as ph:
            ident, t_id, epst = load_consts(ph)
            W = sb(ph, nm("Wp"), [128, 8, nf * 128], BF16)
            t_w = T()
            load_w_bf16(W, t_w, w_d, 8, nf * 128)
            bT = sb(ph, nm("bT"), [128, nf], F32)
            S.dma("sp", lambda e: e.dma_start(out=bT[:], in_=biasT_d[:, :]), writes=[t_w])
            pm = PreMix(ph, i, ident, t_id, epst)
            ug_r = Ring(ph, "ug", [128, 8, 512], BF16, 3)
            pp_r = Ring(ph, "pp", [128, 512], F32, 4, psum=True)
            st = post("init", ph)
            gl = list(token_groups())

            def premix_group(gi):
                (b, seg, c0, n) = gl[gi]
                ug, t_ug = ug_r.next()
                tts = [b * 18 + c0 // 128 + ti for ti in range(n // 128)]
                pm.group(tts, [(ug[:, :, ti * 128:(ti + 1) * 128], t_ug) for ti in range(n // 128)])
                return ug, t_ug
            nxt = premix_group(0)
            for gi, (b, seg, c0, n) in enumerate(gl):
                g0 = b * TPB + c0
                ug, t_ug = nxt
                if gi + 1 < len(gl):
                    nxt = premix_group(gi + 1)
                for f in range(nf):
                    pp, t_pp = pp_r.next()

                    def mm(e, pp=pp, ug=ug, f=f, n=n):
                        for k in range(8):
                            r_ = e.matmul(pp[:, 0:n], lhsT=W[:, k, f * 128:(f + 1) * 128], rhs=ug[:, k, 0:n], start=(k == 0), stop=(k == 7))
                        return r_
                    S.op("pe", mm, reads=[t_ug, t_w], writes=[t_pp])
                    post("tile", st, f, pp, t_pp, bT, t_w, n, g0)
            S.barrier()

    gluT = None
    convT = None

    def phase_conformer(i):
        gluT_ = dscr(nm("gluT"), [D, NTOK], BF16)
        convT_ = dscr(nm("convT"), [D, NTOK])

        def post(kind, st, *a):
            if kind == "init":
                ph = st
                return dict(a_r=Ring(ph, "ca", [128, 8, 512], F32, 2), sg_r=Ring(ph, "csg", [128, 512], F32, 2),
                            gl_r=Ring(ph, "cgl", [128, 512], BF16, 2), cur=[None, None])
            f, pp, t_pp, bT, t_w, n, g0 = a
            if f < 8:
                if f == 0:
                    st["cur"] = st["a_r"].next()
                at, t_a = st["cur"]
                S.op("act", lambda e, at=at, pp=pp, f=f, n=n: e.activation(out=at[:, f, 0:n], in_=pp[:, 0:n], func=AF.Identity, bias=bT[:, f:f + 1]), reads=[t_pp, t_w], writes=[t_a])
            else:
                at, t_a = st["cur"]
                sg, t_sg = st["sg_r"].next()
                gl, t_gl = st["gl_r"].next()
                fa = f - 8
                S.op("act", lambda e, sg=sg, pp=pp, f=f, n=n: e.activation(out=sg[:, 0:n], in_=pp[:, 0:n], func=AF.Sigmoid, bias=bT[:, f:f + 1]), reads=[t_pp, t_w], writes=[t_sg])
                S.op("dve", lambda e, gl=gl, at=at, sg=sg, fa=fa, n=n: e.tensor_mul(out=gl[:, 0:n], in0=at[:, fa, 0:n], in1=sg[:, 0:n]), reads=[t_a, t_sg], writes=[t_gl])
                S.dma("pool", lambda e, gl=gl, fa=fa, g0=g0, n=n: e.dma_start(out=gluT_[fa * 128:(fa + 1) * 128, g0:g0 + n], in_=gl[:, 0:n]), reads=[t_gl])
        phase_proj_fm(i, cv_w_pw1[0], 16, cv_b_pw1T, post)

        with ExitStack() as ph:
            wdw = sb(ph, nm("wdw"), [128, 8, 31], F32)
            bdw = sb(ph, nm("bdw"), [128, 8], F32)
            identf = sb(ph, nm("identf"), [128, 128], F32)
            t_w = T()
            S.dma("sp", lambda e: e.dma_start(out=wdw[:], in_=cv_w_dwT[:, :, :]), writes=[t_w])
            S.dma("sp", lambda e: e.dma_start(out=bdw[:], in_=cv_b_dwT[:, :]), writes=[t_w])
            S.dma("sp", lambda e: e.dma_start(out=identf[:], in_=k_ident[:, :]), writes=[t_w])
            dg_r = Ring(ph, "dg", [128, 31, 128], BF16, 2)
            rings = {}
            for n in (S_LAT, S_CTX):
                bufs = Ring(ph, "cbuf%d" % n, [128, n + 32], BF16, 3)
                for (bt, t_b) in bufs.bufs:
                    S.op("pool", lambda e, bt=bt: e.memset(bt[:, 0:15], 0.0), writes=[t_b])
                    S.op("pool", lambda e, bt=bt, n=n: e.memset(bt[:, n + 15:n + 32], 0.0), writes=[t_b])
                rings[n] = bufs
            pc_r = Ring(ph, "cps", [128, 512], F32, 4, psum=True)
            ac_r = Ring(ph, "cacc", [128, 512], F32, 4)
            for f in range(8):
                dg, t_dg = dg_r.next()
                for jt in range(31):
                    S.op("dve", lambda e, dg=dg, f=f, jt=jt: e.tensor_scalar_mul(out=dg[:, jt, :], in0=identf[:], scalar1=wdw[:, f, jt:jt + 1]), reads=[t_w], writes=[t_dg])
                for b in range(NB):
                    for (c0, n) in ((0, S_LAT), (S_LAT, S_CTX)):
                        g0 = b * TPB + c0
                        bt, t_b = rings[n].next()
                        S.dma("sp", lambda e, bt=bt, f=f, g0=g0, n=n: e.dma_start(out=bt[:, 15:15 + n], in_=gluT_[f * 128:(f + 1) * 128, g0:g0 + n]), writes=[t_b])
                        cw = min(512, n)
                        for tc in range(n // cw):
                            pc, t_pc = pc_r.next()
                            acc, t_acc = ac_r.next()

                            def mmc(e, pc=pc, bt=bt, dg=dg, tc=tc, cw=cw):
                                for jt in range(31):
                                    r_ = e.matmul(pc[:, 0:cw], lhsT=dg[:, jt, :], rhs=bt[:, tc * cw + jt:tc * cw + jt + cw], start=(jt == 0), stop=(jt == 30))
                                return r_
                            S.op("pe", mmc, reads=[t_b, t_dg], writes=[t_pc])
                            S.op("act", lambda e, acc=acc, pc=pc, f=f, cw=cw: e.activation(out=acc[:, 0:cw], in_=pc[:, 0:cw], func=AF.Identity, bias=bdw[:, f:f + 1]), reads=[t_pc, t_w], writes=[t_acc])
                            S.dma("pool", lambda e, acc=acc, f=f, g0=g0, tc=tc, cw=cw: e.dma_start(out=convT_[f * 128:(f + 1) * 128, g0 + tc * cw:g0 + (tc + 1) * cw], in_=acc[:, 0:cw]), reads=[t_acc])
            S.barrier()

        with ExitStack() as ph:
            lg = sb(ph, nm("lg"), [128, 8], F32)
            lb = sb(ph, nm("lb"), [128, 8], F32)
            onesb = sb(ph, nm("onesb"), [128, 128], BF16)
            epsl = sb(ph, nm("epsl"), [128, 1], F32)
            t_w = T()
            S.dma("sp", lambda e: e.dma_start(out=lg[:], in_=cv_ln_gT[:, :]), writes=[t_w])
            S.dma("sp", lambda e: e.dma_start(out=lb[:], in_=cv_ln_bT[:, :]), writes=[t_w])
            S.op("pool", lambda e: e.memset(onesb[:], 1.0), writes=[t_w])
            S.op("pool", lambda e: e.memset(epsl[:], LN_EPS), writes=[t_w])
            y_r = Ring(ph, "ly", [128, 8, 512], F32, 2)
            yb_r = Ring(ph, "lyb", [128, 8, 512], BF16, 2)
            sq_r = Ring(ph, "lsq", [128, 8, 512], BF16, 2)
            p_r = Ring(ph, "lps", [128, 2, 512], F32, 2, psum=True)
            mean_r = Ring(ph, "lmean", [128, 512], F32, 2)
            rstd_r = Ring(ph, "lrstd", [128, 512], F32, 2)
            msq_r = Ring(ph, "lmsq", [128, 512], F32, 2)
            zc_r = Ring(ph, "lzc", [128, 512], F32, 3)
            mo_r = Ring(ph, "lmo", [128, 512], BF16, 3)
            gl = list(token_groups())

            def stA(gi):
                (b, seg, c0, n) = gl[gi]
                g0 = b * TPB + c0
                y, t_y = y_r.next()
                yb, t_yb = yb_r.next()
                sq, t_sq = sq_r.next()
                pst, t_ps = p_r.next()
                mean, t_mean = mean_r.next()
                rstd, t_rstd = rstd_r.next()
                msq, t_msq = msq_r.next()
                S.dma("sp", lambda e: e.dma_start(out=y[:, :, 0:n], in_=convT_[:, g0:g0 + n].rearrange("(k p) t -> p k t", p=128)), writes=[t_y])
                S.op("act", lambda e: e.copy(out=yb[:, :, 0:n], in_=y[:, :, 0:n]), reads=[t_y], writes=[t_yb])
                S.op("act", lambda e: e.activation(out=sq[:, :, 0:n], in_=y[:, :, 0:n], func=AF.Square), reads=[t_y], writes=[t_sq])

                def mms(e):
                    for k in range(8):
                        e.matmul(pst[:, 0, 0:n], lhsT=onesb[:], rhs=yb[:, k, 0:n], start=(k == 0), stop=(k == 7))
                    for k in range(8):
                        r_ = e.matmul(pst[:, 1, 0:n], lhsT=onesb[:], rhs=sq[:, k, 0:n], start=(k == 0), stop=(k == 7))
                    return r_
                S.op("pe", mms, reads=[t_yb, t_sq, t_w], writes=[t_ps])
                S.op("act", lambda e: e.mul(out=mean[:, 0:n], in_=pst[:, 0, 0:n], mul=1.0 / D), reads=[t_ps], writes=[t_mean])
                S.op("dve", lambda e: e.tensor_mul(out=msq[:, 0:n], in0=mean[:, 0:n], in1=mean[:, 0:n]), reads=[t_mean], writes=[t_msq])
                S.op("dve", lambda e: e.scalar_tensor_tensor(out=rstd[:, 0:n], in0=pst[:, 1, 0:n], scalar=1.0 / D, in1=msq[:, 0:n], op0=ALU.mult, op1=ALU.subtract), reads=[t_ps, t_msq], writes=[t_rstd])
                S.op("act", lambda e: e.activation(out=rstd[:, 0:n], in_=rstd[:, 0:n], func=AF.Ln, bias=epsl[:]), reads=[t_rstd, t_w], writes=[t_rstd])
                S.op("act", lambda e: e.activation(out=rstd[:, 0:n], in_=rstd[:, 0:n], func=AF.Exp, scale=-0.5), reads=[t_rstd], writes=[t_rstd])
                return dict(y=y, t_y=t_y, mean=mean, t_mean=t_mean, rstd=rstd, t_rstd=t_rstd, n=n, g0=g0)

            def stB(st):
                y, t_y, mean, t_mean, rstd, t_rstd, n, g0 = (st[k_] for k_ in ("y", "t_y", "mean", "t_mean", "rstd", "t_rstd", "n", "g0"))
                for k in range(8):
                    zc, t_zc = zc_r.next()
                    mo, t_mo = mo_r.next()
                    S.op("dve", lambda e, zc=zc, k=k: e.tensor_sub(out=zc[:, 0:n], in0=y[:, k, 0:n], in1=mean[:, 0:n]), reads=[t_y, t_mean], writes=[t_zc])
                    S.op("dve", lambda e, zc=zc: e.tensor_mul(out=zc[:, 0:n], in0=zc[:, 0:n], in1=rstd[:, 0:n]), reads=[t_zc, t_rstd], writes=[t_zc])
                    S.op("act", lambda e, mo=mo, zc=zc, k=k: e.activation(out=mo[:, 0:n], in_=zc[:, 0:n], func=AF.Silu, scale=lg[:, k:k + 1], bias=lb[:, k:k + 1]), reads=[t_zc, t_w], writes=[t_mo])
                    S.dma("pool", lambda e, mo=mo, k=k: e.dma_start(out=mixT[k * 128:(k + 1) * 128, g0:g0 + n], in_=mo[:, 0:n]), reads=[t_mo])
            prev = None
            for gi in range(len(gl)):
                cur = stA(gi)
                if prev is not None:
                    stB(prev)
                prev = cur
            stB(prev)
            S.barrier()

    TWO_PI = 2.0 * math.pi
    MAGIC = 12582912.0

    def phase_hy_filters(n, hsd):
        nt = n // 128
        cw = min(512, n)
        ncw = n // cw
        with ExitStack() as ph:
            feat = sb(ph, nm("feat"), [33, n], F32)
            w1 = sb(ph, nm("fw1"), [33, 64], F32)
            w2 = sb(ph, nm("fw2"), [64, 2, 64], F32)
            wo = sb(ph, nm("fwo"), [64, 2 * D], F32)
            fb1 = sb(ph, nm("fb1"), [64, 1], F32)
            fb2 = sb(ph, nm("fb2"), [64, 2], F32)
            ffr = sb(ph, nm("ffr"), [64, 1], F32)
            hdn = [sb(ph, nm("hdn"), [64, n], F32) for _ in range(2)]
            t_hdn = [T(), T()]
            t_c = T()
            S.dma("sp", lambda e: e.dma_start(out=feat[:], in_=k_feat[n][:, :]), writes=[t_c])
            S.dma("sp", lambda e: e.dma_start(out=w1[:], in_=hy_filt_w1[0, :, :]), writes=[t_c])
            for l_ in range(2):
                S.dma("sp", lambda e, l_=l_: e.dma_start(out=w2[:, l_, :], in_=hy_filt_w2[0, l_, :, :]), writes=[t_c])
            S.dma("sp", lambda e: e.dma_start(out=wo[:], in_=hy_filt_w_out[0, :, :]), writes=[t_c])
            S.dma("sp", lambda e: e.dma_start(out=fb1[:], in_=hy_fb1T[:, :]), writes=[t_c])
            S.dma("sp", lambda e: e.dma_start(out=fb2[:], in_=hy_fb2T[:, :]), writes=[t_c])
            S.dma("sp", lambda e: e.dma_start(out=ffr[:], in_=hy_ffreqT[:, :]), writes=[t_c])
            pp_r = Ring(ph, "fpp", [128, 512], F32, 2, psum=True)
            arg_r = Ring(ph, "farg", [64, 512], F32, 2)
            tq_r = Ring(ph, "ftq", [64, 512], F32, 2)
            for l_ in range(3):
                dst, t_dst = hdn[l_ % 2], t_hdn[l_ % 2]
                for c in range(ncw):
                    pp, t_pp = pp_r.next()
                    arg, t_arg = arg_r.next()
                    tq, t_tq = tq_r.next()
                    if l_ == 0:
                        S.op("pe", lambda e, pp=pp, c=c: e.matmul(pp[0:64, 0:cw], lhsT=w1[0:33, :], rhs=feat[0:33, c * cw:(c + 1) * cw], start=True, stop=True), reads=[t_c], writes=[t_pp])
                        bcol = fb1[:, 0:1]
                    else:
                        src, t_src = hdn[(l_ - 1) % 2], t_hdn[(l_ - 1) % 2]
                        S.op("pe", lambda e, pp=pp, c=c, src=src, l_=l_: e.matmul(pp[0:64, 0:cw], lhsT=w2[:, l_ - 1, :], rhs=src[:, c * cw:(c + 1) * cw], start=True, stop=True), reads=[t_c, t_src], writes=[t_pp])
                        bcol = fb2[:, l_ - 1:l_]
                    S.op("dve", lambda e, arg=arg, pp=pp, bcol=bcol: e.tensor_scalar(out=arg[:, 0:cw], in0=pp[0:64, 0:cw], scalar1=bcol, scalar2=ffr[:, 0:1], op0=ALU.add, op1=ALU.mult), reads=[t_pp, t_c], writes=[t_arg])
                    S.op("dve", lambda e, tq=tq, arg=arg: e.tensor_scalar(out=tq[:, 0:cw], in0=arg[:, 0:cw], scalar1=1.0 / TWO_PI, scalar2=MAGIC, op0=ALU.mult, op1=ALU.add), reads=[t_arg], writes=[t_tq])
                    S.op("dve", lambda e, tq=tq: e.tensor_scalar_add(out=tq[:, 0:cw], in0=tq[:, 0:cw], scalar1=-MAGIC), reads=[t_tq], writes=[t_tq])
                    S.op("dve", lambda e, tq=tq, arg=arg: e.scalar_tensor_tensor(out=arg[:, 0:cw], in0=tq[:, 0:cw], scalar=-TWO_PI, in1=arg[:, 0:cw], op0=ALU.mult, op1=ALU.add), reads=[t_tq, t_arg], writes=[t_arg])
                    S.op("dve", lambda e, arg=arg: e.tensor_scalar(out=arg[:, 0:cw], in0=arg[:, 0:cw], scalar1=-3.14159, scalar2=3.14159, op0=ALU.max, op1=ALU.min), reads=[t_arg], writes=[t_arg])
                    S.op("act", lambda e, dst=dst, arg=arg, c=c: e.activation(out=dst[:, c * cw:(c + 1) * cw], in_=arg[:, 0:cw], func=AF.Sin), reads=[t_arg], writes=[t_dst])
            fin, t_fin = hdn[0], t_hdn[0]
            dk_r = Ring(ph, "fdk", [128, D], F32, 2)
            hf_r = Ring(ph, "fhf", [128, 2, D], F32, 2)
            hs_r = Ring(ph, "fhs", [128, 2, D], BF16, 2)
            for tt in range(nt):
                dk, t_dk = dk_r.next()
                hf, t_hf = hf_r.next()
                hs, t_hs = hs_r.next()
                S.dma("sp", lambda e, dk=dk, tt=tt: e.dma_start(out=dk[:], in_=k_decay[n][tt * 128:(tt + 1) * 128, :]), writes=[t_dk])
                for cc in range(4):
                    pp, t_pp = pp_r.next()
                    S.op("pe", lambda e, pp=pp, tt=tt, cc=cc: e.matmul(pp[:, :], lhsT=fin[:, tt * 128:(tt + 1) * 128], rhs=wo[:, cc * 512:(cc + 1) * 512], start=True, stop=True), reads=[t_fin, t_c], writes=[t_pp])
                    S.op("dve", lambda e, pp=pp, hf=hf, dk=dk, cc=cc: e.tensor_mul(out=hf[:, cc // 2, (cc % 2) * 512:(cc % 2 + 1) * 512], in0=pp[:, :], in1=dk[:, (cc % 2) * 512:(cc % 2 + 1) * 512]), reads=[t_pp, t_dk], writes=[t_hf])
                if tt == 0:
                    S.op("dve", lambda e, hf=hf: e.memset(hf[0:1, 1, :], 0.0), reads=[t_hf], writes=[t_hf])
                S.op("dve", lambda e, hf=hf, hs=hs: e.tensor_add(out=hs[:, 0, :], in0=hf[:, 0, :], in1=hf[:, 1, :]), reads=[t_hf], writes=[t_hs])
                S.op("dve", lambda e, hf=hf, hs=hs: e.tensor_sub(out=hs[:, 1, :], in0=hf[:, 0, :], in1=hf[:, 1, :]), reads=[t_hf], writes=[t_hs])
                for q in range(2):
                    S.dma("pool", lambda e, hs=hs, q=q, tt=tt: e.dma_start(out=hsd[q, tt * 128:(tt + 1) * 128, :], in_=hs[:, q, :]), reads=[t_hs])
            S.barrier()

    def load_tab(tab, t_tab, src, nt):
        for a in range(nt):
            S.dma("sp" if a % 2 == 0 else "pool", lambda e, a=a: e.dma_start(out=tab[:, a, :], in_=src[a * 128:(a + 1) * 128, :]), writes=[t_tab])

    def phase_hy_khat(n, hsd, KAB):
        nt = n // 128
        with ExitStack() as ph:
            tab = sb(ph, nm("ktab"), [128, nt, n], BF16)
            dat = sb(ph, nm("kdat"), [128, nt, D], BF16)
            t_tab, t_dat = T(), T()
            pp_r = Ring(ph, "kpp", [128, 512], F32, 2, psum=True)
            o_r = Ring(ph, "ko", [128, 512], F32, 3)
            for q in range(2):
                load_tab(tab, t_tab, k_dft[n][q], nt)
                S.dma("sp", lambda e, q=q: e.dma_start(out=dat[:], in_=hsd[q, :, :].rearrange("(a p) c -> p a c", p=128)), writes=[t_dat])
                for ft in range(nt):
                    for cc in range(2):
                        pp, t_pp = pp_r.next()
                        ot, t_o = o_r.next()

                        def mm(e, pp=pp, ft=ft, cc=cc):
                            for a in range(nt):
                                r_ = e.matmul(pp[:, :], lhsT=tab[:, a, ft * 128:(ft + 1) * 128], rhs=dat[:, a, cc * 512:(cc + 1) * 512], start=(a == 0), stop=(a == nt - 1))
                            return r_
                        S.op("pe", mm, reads=[t_tab, t_dat], writes=[t_pp])
                        S.op("act", lambda e, ot=ot, pp=pp: e.copy(out=ot[:], in_=pp[:, :]), reads=[t_pp], writes=[t_o])
                        S.dma("pool", lambda e, ot=ot, q=q, ft=ft, cc=cc: e.dma_start(out=KAB[q, ft * 128:(ft + 1) * 128, cc * 512:(cc + 1) * 512], in_=ot[:]), reads=[t_o])
            S.barrier()

    def phase_hy_conv3(zT_, x0T_, v1T_, v1tok):
        with ExitStack() as ph:
            ident, t_id, epst = load_consts(ph)
            identf = sb(ph, nm("identf2"), [128, 128], F32)
            ws = sb(ph, nm("hws"), [128, 24, 3], F32)
            bs = sb(ph, nm("hbs"), [128, 24], F32)
            dg = sb(ph, nm("hdg"), [128, 24, 3, 128], BF16)
            t_w = T()
            S.dma("sp", lambda e: e.dma_start(out=ws[:], in_=hy_w_shortT[:, :, :]), writes=[t_w])
            S.dma("sp", lambda e: e.dma_start(out=bs[:], in_=hy_b_shortT[:, :]), writes=[t_w])
            S.dma("sp", lambda e: e.dma_start(out=identf[:], in_=k_ident[:, :]), writes=[t_w])
            for fi in range(24):
                for jt in range(3):
                    S.op("dve", lambda e, fi=fi, jt=jt: e.tensor_scalar_mul(out=dg[:, fi, jt, :], in0=identf[:], scalar1=ws[:, fi, jt:jt + 1]), reads=[t_w], writes=[t_w])
            rings = {}
            for n in (S_LAT, S_CTX):
                bufs = Ring(ph, "hbuf%d" % n, [128, n + 2], BF16, 6)
                for (bt, t_b) in bufs.bufs:
                    S.op("pool", lambda e, bt=bt: e.memset(bt[:, 0:1], 0.0), writes=[t_b])
                    S.op("pool", lambda e, bt=bt, n=n: e.memset(bt[:, n + 1:n + 2], 0.0), writes=[t_b])
                rings[n] = (bufs, Ring(ph, "hacc%d" % n, [128, n], F32, 4), Ring(ph, "hv1_%d" % n, [128, n], F32, 2), Ring(ph, "hv1b_%d" % n, [128, n], BF16, 2))
            pc_r = Ring(ph, "hcps", [128, 512], F32, 4, psum=True)
            ptr_r = Ring(ph, "hptr", [128, 8, 128], BF16, 2, psum=True)
            vt_r = Ring(ph, "hvt", [128, 8, 128], BF16, 2)
            for f in range(8):
                for b in range(NB):
                    for (c0, n) in ((0, S_LAT), (S_LAT, S_CTX)):
                        g0 = b * TPB + c0
                        cw = min(512, n)
                        accs = []
                        for part in range(3):
                            fi = part * 8 + f
                            bt, t_b = rings[n][0].next()
                            acc, t_acc = rings[n][1].next()
                            S.dma("sp", lambda e, bt=bt, fi=fi, g0=g0, n=n: e.dma_start(out=bt[:, 1:1 + n], in_=zT_[fi * 128:(fi + 1) * 128, g0:g0 + n]), writes=[t_b])
                            for tc in range(n // cw):
                                pc, t_pc = pc_r.next()

                                def mmc(e, pc=pc, bt=bt, fi=fi, tc=tc, cw=cw):
                                    for jt in range(3):
                                        r_ = e.matmul(pc[:, 0:cw], lhsT=dg[:, fi, jt, :], rhs=bt[:, tc * cw + jt:tc * cw + jt + cw], start=(jt == 0), stop=(jt == 2))
                                    return r_
                                S.op("pe", mmc, reads=[t_b, t_w], writes=[t_pc])
                                S.op("act", lambda e, acc=acc, pc=pc, fi=fi, tc=tc, cw=cw: e.activation(out=acc[:, tc * cw:(tc + 1) * cw], in_=pc[:, 0:cw], func=AF.Identity, bias=bs[:, fi:fi + 1]), reads=[t_pc, t_w], writes=[t_acc])
                            accs.append((acc, t_acc))
                        (x0c, t_x0), (x1c, t_x1), (vc, t_vc) = accs
                        v1, t_v1 = rings[n][2].next()
                        v1b, t_v1b = rings[n][3].next()
                        S.dma("pool", lambda e, x0c=x0c, f=f, g0=g0, n=n: e.dma_start(out=x0T_[f * 128:(f + 1) * 128, g0:g0 + n], in_=x0c[:]), reads=[t_x0])
                        S.op("dve", lambda e, v1=v1, vc=vc, x1c=x1c: e.tensor_mul(out=v1[:], in0=vc[:], in1=x1c[:]), reads=[t_vc, t_x1], writes=[t_v1])
                        S.op("dve", lambda e, v1b=v1b, v1=v1: e.tensor_copy(out=v1b[:], in_=v1[:]), reads=[t_v1], writes=[t_v1b])
                        S.dma("pool", lambda e, v1=v1, f=f, g0=g0, n=n: e.dma_start(out=v1T_[f * 128:(f + 1) * 128, g0:g0 + n], in_=v1[:]), reads=[t_v1])
                        for t0 in range(0, n // 128, 8):
                            na = min(8, n // 128 - t0)
                            ptr, t_ptr = ptr_r.next()
                            vt, t_vt = vt_r.next()

                            def tr(e, ptr=ptr, v1b=v1b, t0=t0, na=na):
                                for a_ in range(na):
                                    r_ = e.transpose(out=ptr[:, a_, :], in_=v1b[:, (t0 + a_) * 128:(t0 + a_ + 1) * 128], identity=ident[:])
                                return r_
                            S.op("pe", tr, reads=[t_v1b, t_id], writes=[t_ptr])
                            S.op("dve", lambda e, vt=vt, ptr=ptr, na=na: e.tensor_copy(out=vt[:, 0:na, :], in_=ptr[:, 0:na, :]), reads=[t_ptr], writes=[t_vt])
                            r0 = g0 + t0 * 128
                            S.dma("pool", lambda e, vt=vt, r0=r0, na=na, f=f: e.dma_start(out=v1tok[r0:r0 + na * 128, f * 128:(f + 1) * 128].rearrange("(a p) c -> p a c", p=128), in_=vt[:, 0:na, :]), reads=[t_vt])
            S.barrier()

    def phase_hy_fwd(n, c0, KAB, Yh, v1tok):
        nt = n // 128
        with ExitStack() as ph:
            Ct = sb(ph, nm("Ct"), [128, nt, n], BF16)
            St = sb(ph, nm("St"), [128, nt, n], BF16)
            t_tab = T()
            load_tab(Ct, t_tab, k_dft[n][0], nt)
            load_tab(St, t_tab, k_dft[n][1], nt)
            v_r = Ring(ph, "fv", [128, nt, 512], BF16, 2)
            pp_r = Ring(ph, "fpp", [128, 2, 512], F32, 2, psum=True)
            k_r = Ring(ph, "fk", [128, 2, 512], F32, 2)
            m_r = Ring(ph, "fm", [128, 4, 512], F32, 2)
            y_r = Ring(ph, "fy", [128, 2, 512], BF16, 2)
            for b in range(NB):
                for cc in range(2):
                    v, t_v = v_r.next()
                    r0 = b * TPB + c0
                    S.dma("sp", lambda e, v=v, r0=r0, cc=cc: e.dma_start(out=v[:], in_=v1tok[r0:r0 + n, cc * 512:(cc + 1) * 512].rearrange("(a p) c -> p a c", p=128)), writes=[t_v])
                    for ft in range(nt):
                        pp, t_pp = pp_r.next()
                        kk, t_kk = k_r.next()
                        mm_, t_mm = m_r.next()
                        yy, t_yy = y_r.next()
                        for q in range(2):
                            S.dma("sp", lambda e, kk=kk, q=q, ft=ft, cc=cc: e.dma_start(out=kk[:, q, :], in_=KAB[q, ft * 128:(ft + 1) * 128, cc * 512:(cc + 1) * 512]), writes=[t_kk])

                        def mm(e, pp=pp, v=v, ft=ft):
                            for q, tab in enumerate((Ct, St)):
                                for a in range(nt):
                                    r_ = e.matmul(pp[:, q, :], lhsT=tab[:, a, ft * 128:(ft + 1) * 128], rhs=v[:, a, :], start=(a == 0), stop=(a == nt - 1))
                            return r_
                        S.op("pe", mm, reads=[t_tab, t_v], writes=[t_pp])
                        for mi, (pq, kq) in enumerate(((0, 0), (1, 1), (0, 1), (1, 0))):
                            S.op("dve", lambda e, mm_=mm_, pp=pp, kk=kk, mi=mi, pq=pq, kq=kq: e.tensor_mul(out=mm_[:, mi, :], in0=pp[:, pq, :], in1=kk[:, kq, :]), reads=[t_pp, t_kk], writes=[t_mm])
                        S.op("pool", lambda e, yy=yy, mm_=mm_: e.tensor_tensor(out=yy[:, 0, :], in0=mm_[:, 0, :], in1=mm_[:, 1, :], op=ALU.subtract), reads=[t_mm], writes=[t_yy])
                        S.op("pool", lambda e, yy=yy, mm_=mm_: e.tensor_tensor(out=yy[:, 1, :], in0=mm_[:, 2, :], in1=mm_[:, 3, :], op=ALU.add), reads=[t_mm], writes=[t_yy])
                        for q in range(2):
                            S.dma("pool", lambda e, yy=yy, q=q, b=b, ft=ft, cc=cc: e.dma_start(out=Yh[q, b, ft * 128:(ft + 1) * 128, cc * 512:(cc + 1) * 512], in_=yy[:, q, :]), reads=[t_yy])
            S.barrier()

    def phase_hy_inv(n, c0, Yh, x0T_, v1T_):
        nt = n // 128
        cw = min(512, n)
        with ExitStack() as ph:
            Ci = sb(ph, nm("Ci"), [128, nt, n], BF16)
            Si = sb(ph, nm("Si"), [128, nt, n], BF16)
            t_tab = T()
            load_tab(Ci, t_tab, k_dft[n][2], nt)
            load_tab(Si, t_tab, k_dft[n][3], nt)
            dbias = sb(ph, nm("dbias"), [128, 8], F32)
            S.dma("sp", lambda e: e.dma_start(out=dbias[:], in_=hy_biasT[:, :]), writes=[t_tab])
            y_r = Ring(ph, "iy", [128, 2, nt, 128], BF16, 2)
            pp_r = Ring(ph, "ipp", [128, 512], F32, 2, psum=True)
            vx_r = Ring(ph, "ivx", [128, 2, 512], F32, 2)
            t_r = Ring(ph, "it", [128, 512], F32, 2)
            mo_r = Ring(ph, "imo", [128, 512], BF16, 2)
            for b in range(NB):
                for ct in range(8):
                    yy, t_yy = y_r.next()
                    for q in range(2):
                        S.dma("sp", lambda e, yy=yy, q=q, b=b, ct=ct: e.dma_start(out=yy[:, q, :, :], in_=Yh[q, b, :, ct * 128:(ct + 1) * 128].rearrange("(a p) c -> p a c", p=128)), writes=[t_yy])
                    for tc in range(n // cw):
                        g0 = b * TPB + c0 + tc * cw
                        pp, t_pp = pp_r.next()
                        vx, t_vx = vx_r.next()
                        tq, t_tq = t_r.next()
                        mo, t_mo = mo_r.next()
                        S.dma("sp", lambda e, vx=vx, ct=ct, g0=g0: e.dma_start(out=vx[:, 0, 0:cw], in_=v1T_[ct * 128:(ct + 1) * 128, g0:g0 + cw]), writes=[t_vx])
                        S.dma("sp", lambda e, vx=vx, ct=ct, g0=g0: e.dma_start(out=vx[:, 1, 0:cw], in_=x0T_[ct * 128:(ct + 1) * 128, g0:g0 + cw]), writes=[t_vx])

                        def mm(e, pp=pp, yy=yy, tc=tc):
                            k_ = 0
                            for q, tab in enumerate((Ci, Si)):
                                for a in range(nt):
                                    r_ = e.matmul(pp[:, 0:cw], lhsT=yy[:, q, a, :], rhs=tab[:, a, tc * cw:(tc + 1) * cw], start=(k_ == 0), stop=(k_ == 2 * nt - 1))
                                    k_ += 1
                            return r_
                        S.op("pe", mm, reads=[t_tab, t_yy], writes=[t_pp])
                        S.op("dve", lambda e, tq=tq, vx=vx, pp=pp, ct=ct: e.scalar_tensor_tensor(out=tq[:, 0:cw], in0=vx[:, 0, 0:cw], scalar=dbias[:, ct:ct + 1], in1=pp[:, 0:cw], op0=ALU.mult, op1=ALU.add), reads=[t_vx, t_pp, t_tab], writes=[t_tq])
                        S.op("dve", lambda e, mo=mo, tq=tq, vx=vx: e.tensor_mul(out=mo[:, 0:cw], in0=tq[:, 0:cw], in1=vx[:, 1, 0:cw]), reads=[t_tq, t_vx], writes=[t_mo])
                        S.dma("pool", lambda e, mo=mo, ct=ct, g0=g0: e.dma_start(out=mixT[ct * 128:(ct + 1) * 128, g0:g0 + cw], in_=mo[:, 0:cw]), reads=[t_mo])
            S.barrier()

    def phase_hyena(i):
        zT_ = dscr(nm("zT"), [3 * D, NTOK], BF16)
        x0T_ = dscr(nm("x0T"), [D, NTOK])
        v1T_ = dscr(nm("v1T"), [D, NTOK])
        v1tok = dscr(nm("v1tok"), [NTOK, D], BF16)
        hsd = {n: dscr(nm("hsd"), [2, n, D], BF16) for n in (S_LAT, S_CTX)}
        KAB = {n: dscr(nm("KAB"), [2, n, D]) for n in (S_LAT, S_CTX)}
        Yh = {n: dscr(nm("Yh"), [2, NB, n, D], BF16) for n in (S_LAT, S_CTX)}
        for n in (S_LAT, S_CTX):
            phase_hy_filters(n, hsd[n])
            phase_hy_khat(n, hsd[n], KAB[n])

        def post(kind, st, *a):
            if kind == "init":
                return dict(z_r=Ring(st, "hz", [128, 512], BF16, 3))
            f, pp, t_pp, bT, t_w, n, g0 = a
            zt, t_z = st["z_r"].next()
            S.op("act", lambda e, zt=zt, pp=pp, f=f, n=n: e.activation(out=zt[:, 0:n], in_=pp[:, 0:n], func=AF.Identity, bias=bT[:, f:f + 1]), reads=[t_pp, t_w], writes=[t_z])
            S.dma("pool", lambda e, zt=zt, f=f, g0=g0, n=n: e.dma_start(out=zT_[f * 128:(f + 1) * 128, g0:g0 + n], in_=zt[:, 0:n]), reads=[t_z])
        phase_proj_fm(i, hy_w_in[0], 24, hy_b_inT, post)
        phase_hy_conv3(zT_, x0T_, v1T_, v1tok)
        for (n, c0) in ((S_LAT, 0), (S_CTX, S_LAT)):
            phase_hy_fwd(n, c0, KAB[n], Yh[n], v1tok)
            phase_hy_inv(n, c0, Yh[n], x0T_, v1T_)

    g.hooks = dict(phase_mod=phase_mod, phase_attn=phase_attn, phase_p3a=phase_p3a, phase_p3b=phase_p3b)
    lay = list(layers) if layers is not None else list(range(n_layers))
    g.first = lay[0]
    for i in lay:
        last = (i == lay[-1])
        kind, j = i % 3, i // 3
        phase_mod(i)
        if kind == 0:
            phase_attn(i, j, ctx_q=not last)
            phase_p3a(i, attn_w_out[j], None, skip_ctx=last)
        elif kind == 1:
            phase_hyena(i)
            phase_p3a(i, hy_w_out[0], hy_b_out[0, :], skip_ctx=last)
        else:
            phase_conformer(i)
            phase_p3a(i, cv_w_pw2[0], cv_b_pw2[0, :], skip_ctx=last)
        phase_p3b(i, last)
    S.emit()
    return nc


_CACHE = {}


def _consts():
    if "c" in _CACHE:
        return _CACHE["c"]
    cos, sin, rmT = _rope_tables()
    sel = np.zeros((128, 2, 128), np.float32)
    for g_ in range(4):
        sel[g_ * 32:(g_ + 1) * 32, g_ % 2, :] = 1.0 / 32
    c = {"k_cos": cos, "k_sin": sin, "k_rm": rmT, "k_ident": np.eye(128, dtype=np.float32), "k_sel": sel}
    for n in (S_LAT, S_CTX):
        tabs = _dft_tables(n)
        for q in range(4):
            c["k_dft%d_%d" % (n, q)] = tabs[q].astype(ml_dtypes.bfloat16)
        zt, decay = _hyena_feats(n)
        c["k_feat%d" % n] = zt
        c["k_decay%d" % n] = decay
    _CACHE["c"] = c
    return c


def kernel(n_layers=DEPTH, cores=NCORES, trace=False, layers=None, h_init=None, **inputs):
    f = lambda a: np.ascontiguousarray(np.asarray(a, dtype=np.float32))
    x = f(inputs["x"])
    c = f(inputs["c"])
    ctx = f(inputs["ctx"])
    c_ctx = f(inputs["c_ctx"])
    nc = build(n_layers, layers)
    consts = _consts()
    shared = dict(consts)
    for k_ in ("w_mod", "b_mod", "norm_mix_pre", "norm_mix_post", "norm_mlp_pre", "norm_mlp_post", "w_mlp_in", "w_mlp_out",
               "attn_w_qkv", "attn_w_out", "attn_subln", "hy_w_in", "hy_b_in", "hy_w_short", "hy_b_short", "hy_filt_w1",
               "hy_filt_b1", "hy_filt_w2", "hy_filt_b2", "hy_filt_freq", "hy_filt_w_out", "hy_bias", "hy_w_out", "hy_b_out",
               "cv_w_pw1", "cv_b_pw1", "cv_w_dw", "cv_b_dw", "cv_ln_g", "cv_ln_b", "cv_w_pw2", "cv_b_pw2"):
        shared[k_] = f(inputs[k_])
    shared["attn_lambda"] = f(inputs["attn_lambda"]).reshape(2, 256)
    colT = lambda v, nt: np.ascontiguousarray(f(v).reshape(nt, 128).T)
    shared["cv_b_pw1T"] = colT(inputs["cv_b_pw1"][0], 16)
    shared["cv_w_dwT"] = np.ascontiguousarray(f(inputs["cv_w_dw"][0]).reshape(31, 8, 128).transpose(2, 1, 0))
    shared["cv_b_dwT"] = colT(inputs["cv_b_dw"][0], 8)
    shared["cv_ln_gT"] = colT(inputs["cv_ln_g"][0], 8)
    shared["cv_ln_bT"] = colT(inputs["cv_ln_b"][0], 8)
    shared["hy_b_inT"] = colT(inputs["hy_b_in"][0], 24)
    shared["hy_w_shortT"] = np.ascontiguousarray(f(inputs["hy_w_short"][0]).reshape(3, 24, 128).transpose(2, 1, 0))
    shared["hy_b_shortT"] = colT(inputs["hy_b_short"][0], 24)
    shared["hy_biasT"] = colT(inputs["hy_bias"][0], 8)
    shared["hy_fb1T"] = np.ascontiguousarray(f(inputs["hy_filt_b1"][0]).reshape(1, 64).T)
    shared["hy_fb2T"] = np.ascontiguousarray(f(inputs["hy_filt_b2"][0]).reshape(2, 64).T)
    shared["hy_ffreqT"] = np.ascontiguousarray(f(inputs["hy_filt_freq"][0]).reshape(1, 64).T)
    in_maps = []
    for core in range(cores):
        b0 = core * NB
        m = dict(shared)
        m["x2"] = x[b0:b0 + NB].reshape(NB * S_LAT, D)
        m["ctx2"] = ctx[b0:b0 + NB].reshape(NB * S_CTX, D)
        cm = np.stack([c[b0], c[b0 + 1], c_ctx], axis=0)
        m["cT"] = np.ascontiguousarray(cm.reshape(3, 8, 128).transpose(2, 1, 0))
        in_maps.append(m)
    res = run_bass_kernel_spmd(nc, in_maps, core_ids=list(range(cores)), **({'trace': True} if trace else {}))
    _CACHE['res'] = res
    outs = [np.asarray(r["out"]).reshape(NB, S_LAT, D) for r in res.results]
    return np.concatenate(outs, axis=0).astype(np.float32)
```
